# Optimizing a Trainium2 kernel written in Bass

```python
import math
import jax, jax.numpy as jnp
from jax import lax
import numpy as np

D_MODEL = 1024
BATCH = 4
SEQ = 8192
DEPTH = 4

CHUNK = 64
N_META = 16
Q_BLOCK = 128
ROPE_THETA = 500000.0
NORM_EPS = 1e-6

POOL_WINDOWS = (2, 4, 8, 16)
POOL_GROUP = D_MODEL // 8
POOL_WIDTH = POOL_GROUP * len(POOL_WINDOWS)

DA_HEADS = 4
DA_DIM = 64
DA_VDIM = 2 * DA_DIM
DA_QK = DA_HEADS * 2 * DA_DIM
DA_WIDTH = DA_HEADS * DA_VDIM
DA_ROPE = DA_DIM // 4
DA_SUBLN_EPS = 1e-5

RW_HEAD = 64
RW_WIDTH = D_MODEL // 2
RW_HEADS = RW_WIDTH // RW_HEAD
RW_DECAY_LORA = 64
RW_A_LORA = 64
RW_GATE_LORA = 128
RW_SHIFT_WIDTH = 3 * RW_WIDTH + RW_DECAY_LORA + RW_A_LORA + RW_GATE_LORA
RW_SPLITS = (RW_WIDTH, 2 * RW_WIDTH, 3 * RW_WIDTH, 3 * RW_WIDTH + RW_DECAY_LORA,
             3 * RW_WIDTH + RW_DECAY_LORA + RW_A_LORA)
RW_GN_EPS = 64e-5

N_BRANCH = 3
BRANCH_WIDTH = 512
IN_PARTS = (POOL_WIDTH, DA_QK, DA_QK, DA_WIDTH, RW_SHIFT_WIDTH, N_BRANCH * D_MODEL)
IN_WIDTH = sum(IN_PARTS)
IN_SPLITS = tuple(int(s) for s in np.cumsum(IN_PARTS)[:-1])

D_FF = 2816
CONV_WIDTH = 3

kernel_name = "hybrid_pool_diffattn_rwkv7_stream_block"


def rms_norm(x, g, eps=NORM_EPS):
    xf = x.astype(jnp.float32)
    y = xf * lax.rsqrt(jnp.mean(xf * xf, axis=-1, keepdims=True) + eps)
    return (y * g.astype(jnp.float32)).astype(x.dtype)


def rope_tables(n):
    pos = jnp.arange(n, dtype=jnp.float32)
    inv = ROPE_THETA ** (-jnp.arange(0, DA_ROPE, 2, dtype=jnp.float32) / DA_ROPE)
    ang = pos[:, None] * inv[None, :]
    return jnp.cos(ang), jnp.sin(ang)


def partial_rope(t, cos, sin):
    half = DA_ROPE // 2
    c = cos[None, :, None, None, :].astype(t.dtype)
    s = sin[None, :, None, None, :].astype(t.dtype)
    t1 = t[..., :half]
    t2 = t[..., half:DA_ROPE]
    return jnp.concatenate([t1 * c - t2 * s, t2 * c + t1 * s, t[..., DA_ROPE:]], axis=-1)


def pool_mixer(u, w_group, scale):
    B, L, _ = u.shape
    uf = u.astype(jnp.float32).reshape(B, L, len(POOL_WINDOWS), POOL_GROUP)
    cs = jnp.cumsum(uf, axis=1)
    t1 = jnp.arange(1, L + 1, dtype=jnp.float32)
    outs = []
    for g, w in enumerate(POOL_WINDOWS):
        c = cs[:, :, g]
        lag = jnp.pad(c, ((0, 0), (w, 0), (0, 0)))[:, :L]
        cnt = jnp.minimum(t1, float(w))[None, :, None]
        outs.append((c - lag) / cnt - uf[:, :, g])
    pooled = jnp.stack(outs, axis=2).astype(u.dtype)
    mixed = jnp.einsum('blgc,gcd->blgd', pooled, w_group)
    return mixed.reshape(B, L, POOL_WIDTH) * scale


def diff_attention(q, k, v, lam, lambda_init, subln_w):
    B, L = q.shape[:2]
    n_blk = -(-L // Q_BLOCK)
    Lp = n_blk * Q_BLOCK
    pad5 = ((0, 0), (0, Lp - L), (0, 0), (0, 0), (0, 0))
    q = jnp.pad(q, pad5)
    k = jnp.pad(k, pad5)
    v = jnp.pad(v, pad5[:4])
    chunk = (jnp.arange(Lp) - N_META) // CHUNK
    qb = q.reshape(B, n_blk, Q_BLOCK, DA_HEADS, 2, DA_DIM).transpose(1, 0, 2, 3, 4, 5)
    cb = chunk.reshape(n_blk, Q_BLOCK)
    scale = DA_DIM ** -0.5

    def block(args):
        qi, ci = args
        s = jnp.einsum('bqhcd,bkhcd->bhcqk', qi, k).astype(jnp.float32) * scale
        mask = chunk[None, :] <= ci[:, None]
        s = jnp.where(mask, s, -jnp.inf)
        p = jax.nn.softmax(s, axis=-1)
        a = p[:, :, 0] - lam * p[:, :, 1]
        return jnp.einsum('bhqk,bkhe->bqhe', a.astype(v.dtype), v)

    o = lax.map(block, (qb, cb))
    o = o.transpose(1, 0, 2, 3, 4).reshape(B, Lp, DA_HEADS, DA_VDIM)[:, :L]
    o = rms_norm(o, subln_w, DA_SUBLN_EPS) * (1.0 - lambda_init)
    return o.reshape(B, L, DA_WIDTH)


def token_shift(t):
    return jnp.pad(t, ((0, 0), (1, 0), (0, 0)))[:, :-1]


def rwkv7_mixer(p, mu, w0, w2, a0, a2, g2, k_k, k_a, r_k, lnx_w, lnx_b):
    B, L, _ = p.shape
    f32 = jnp.float32
    p = p + mu * (token_shift(p) - p)
    r, k, v, wl, al, gl = jnp.split(p, RW_SPLITS, axis=-1)
    w_log = -jax.nn.softplus(-(w0 + jnp.tanh(wl) @ w2)) - 0.5
    decay = jnp.exp(-jnp.exp(w_log.astype(f32)))
    a = jax.nn.sigmoid(a0 + al @ a2)
    g = jax.nn.sigmoid(gl) @ g2
    heads = lambda t: t.reshape(B, L, RW_HEADS, RW_HEAD).astype(f32)
    kk = heads(k * k_k)
    kk = kk / jnp.maximum(jnp.sqrt(jnp.sum(kk * kk, axis=-1, keepdims=True)), 1e-12)
    k = k * (1.0 + (a - 1.0) * k_a)
    rh, kh, vh, ah, dh = heads(r), heads(k), heads(v), heads(a), heads(decay)
    rem = -kk
    rep = kk * ah

    def step(S, inp):
        r_t, w_t, k_t, v_t, a_t, b_t = inp
        S = (S * w_t[:, :, None, :]
             + jnp.einsum('bhvk,bhk->bhv', S, a_t)[..., None] * b_t[:, :, None, :]
             + v_t[..., None] * k_t[:, :, None, :])
        return S, jnp.einsum('bhvk,bhk->bhv', S, r_t)

    xs = tuple(t.transpose(1, 0, 2, 3) for t in (rh, dh, kh, vh, rem, rep))
    S0 = jnp.zeros((B, RW_HEADS, RW_HEAD, RW_HEAD), f32)
    _, y = lax.scan(step, S0, xs)
    y = y.transpose(1, 0, 2, 3)
    mean = jnp.mean(y, axis=-1, keepdims=True)
    var = jnp.mean(jnp.square(y - mean), axis=-1, keepdims=True)
    y = ((y - mean) * lax.rsqrt(var + RW_GN_EPS)).reshape(B, L, RW_WIDTH)
    y = y * lnx_w.astype(f32) + lnx_b.astype(f32)
    bonus = jnp.sum(rh * kh * r_k.astype(f32), axis=-1, keepdims=True) * vh
    y = (y + bonus.reshape(B, L, RW_WIDTH)) * g.astype(f32)
    return y.astype(p.dtype)


def conv_glu_ffn(h, w_up, conv_w, w_down):
    u = h @ w_up
    C = u.shape[-1]
    u = lax.conv_general_dilated(u, conv_w[:, None, :].astype(u.dtype), window_strides=(1,),
                                 padding=[(CONV_WIDTH - 1, 0)],
                                 dimension_numbers=('NWC', 'WIO', 'NWC'),
                                 feature_group_count=C)
    gate, up = jnp.split(u, 2, axis=-1)
    return (jax.nn.silu(gate) * up) @ w_down


def setup_inputs(seed: int = 0) -> dict:
    key = jax.random.key(seed)
    ks = iter(jax.random.split(key, 32))
    f32 = jnp.float32
    nrm = lambda shape, s: jax.random.normal(next(ks), shape, f32) * s
    uni = lambda shape, lo, hi: jax.random.uniform(next(ks), shape, f32, lo, hi)
    Ld = DEPTH
    return {
        "x": nrm((BATCH, SEQ, D_MODEL), 1.0),
        "meta_tokens": nrm((N_META, D_MODEL), 1.0),
        "norm_mix": 1.0 + nrm((Ld, D_MODEL), 0.02),
        "norm_ffn": 1.0 + nrm((Ld, D_MODEL), 0.02),
        "norm_final": 1.0 + nrm((D_MODEL,), 0.02),
        "w_in": nrm((Ld, D_MODEL, IN_WIDTH), D_MODEL ** -0.5),
        "pool_w": nrm((Ld, len(POOL_WINDOWS), POOL_GROUP, POOL_GROUP), POOL_GROUP ** -0.5),
        "pool_scale": 1.0 + nrm((Ld, POOL_WIDTH), 0.02),
        "da_lambda": nrm((Ld, 4, DA_DIM), 0.1),
        "da_subln": 1.0 + nrm((Ld, DA_VDIM), 0.02),
        "rw_mu": uni((Ld, RW_SHIFT_WIDTH), 0.0, 1.0),
        "rw_w0": uni((Ld, RW_WIDTH), -5.0, 1.0),
        "rw_w2": nrm((Ld, RW_DECAY_LORA, RW_WIDTH), 0.05),
        "rw_a0": nrm((Ld, RW_WIDTH), 0.5),
        "rw_a2": nrm((Ld, RW_A_LORA, RW_WIDTH), 0.1),
        "rw_g2": nrm((Ld, RW_GATE_LORA, RW_WIDTH), RW_GATE_LORA ** -0.5),
        "rw_k_k": 0.85 + nrm((Ld, RW_WIDTH), 0.05),
        "rw_k_a": 1.0 + nrm((Ld, RW_WIDTH), 0.05),
        "rw_r_k": nrm((Ld, RW_HEADS, RW_HEAD), 0.1),
        "rw_lnx_w": 1.0 + nrm((Ld, RW_WIDTH), 0.02),
        "rw_lnx_b": nrm((Ld, RW_WIDTH), 0.02),
        "w_branch": nrm((Ld, N_BRANCH, BRANCH_WIDTH, D_MODEL), BRANCH_WIDTH ** -0.5),
        "w_out": nrm((Ld, D_MODEL, D_MODEL), D_MODEL ** -0.5),
        "ffn_up": nrm((Ld, D_MODEL, 2 * D_FF), D_MODEL ** -0.5),
        "ffn_conv": nrm((Ld, CONV_WIDTH, 2 * D_FF), 0.5),
        "ffn_down": nrm((Ld, D_FF, D_MODEL), D_FF ** -0.5),
    }


def reference(x, meta_tokens, norm_mix, norm_ffn, norm_final, w_in, pool_w, pool_scale,
              da_lambda, da_subln, rw_mu, rw_w0, rw_w2, rw_a0, rw_a2, rw_g2, rw_k_k, rw_k_a,
              rw_r_k, rw_lnx_w, rw_lnx_b, w_branch, w_out, ffn_up, ffn_conv, ffn_down):
    B = x.shape[0]
    meta = jnp.broadcast_to(meta_tokens[None].astype(x.dtype), (B, N_META, D_MODEL))
    x = jnp.concatenate([meta, x], axis=1)
    L = x.shape[1]
    cos, sin = rope_tables(L)
    for l in range(DEPTH):
        lambda_init = 0.8 - 0.6 * math.exp(-0.3 * l)
        h = rms_norm(x, norm_mix[l])
        proj = h @ w_in[l]
        u_pool, q, k, v, p_rw, gate_logits = jnp.split(proj, IN_SPLITS, axis=-1)
        b_pool = pool_mixer(u_pool, pool_w[l], pool_scale[l])
        q = partial_rope(q.reshape(B, L, DA_HEADS, 2, DA_DIM), cos, sin)
        k = partial_rope(k.reshape(B, L, DA_HEADS, 2, DA_DIM), cos, sin)
        v = v.reshape(B, L, DA_HEADS, DA_VDIM)
        lq1, lk1, lq2, lk2 = [da_lambda[l, i].astype(jnp.float32) for i in range(4)]
        lam = jnp.exp(jnp.sum(lq1 * lk1)) - jnp.exp(jnp.sum(lq2 * lk2)) + lambda_init
        b_da = diff_attention(q, k, v, lam, lambda_init, da_subln[l])
        b_rw = rwkv7_mixer(p_rw, rw_mu[l], rw_w0[l], rw_w2[l], rw_a0[l], rw_a2[l], rw_g2[l],
                           rw_k_k[l], rw_k_a[l], rw_r_k[l], rw_lnx_w[l], rw_lnx_b[l])
        branches = jnp.stack([b_pool, b_da, b_rw], axis=2)
        up = jnp.einsum('blnc,ncd->blnd', branches, w_branch[l])
        gates = jax.nn.sigmoid(gate_logits.reshape(B, L, N_BRANCH, D_MODEL).astype(jnp.float32))
        merged = jnp.sum(gates.astype(up.dtype) * up, axis=2)
        x = x + merged @ w_out[l]
        x = x + conv_glu_ffn(rms_norm(x, norm_ffn[l]), ffn_up[l], ffn_conv[l], ffn_down[l])
    return rms_norm(x, norm_final)[:, N_META:]
```

```python
import math
import os
import numpy as np
import concourse.bass as bass
import concourse.mybir as mybir
from concourse.bass_utils import run_bass_kernel_spmd

F32 = mybir.dt.float32
BF16 = mybir.dt.bfloat16
AF = mybir.ActivationFunctionType
ALU = mybir.AluOpType
AX = mybir.AxisListType

D = 1024
INW = 6912
DFF = 2816
NPAD = 112
NMETA = 16
SEQ = 8192
C1 = -math.exp(-0.5)
SAME_ENGINE_SYNC = True
FUSED = False
SKIP = set(os.environ.get("KSKIP", "").split(","))

PP_GMIX, PP_GFFN, PP_PSC, PP_MU, PP_W0, PP_A0, PP_KK, PP_KA, PP_RK = 0, 8, 16, 20, 34, 38, 42, 46, 50
PP_CONV = 54
PP_SUBLN = 186
PP_LNW = 314
PP_LNB = 826
PP_LAM = 1338
PP_GFIN = 1594
PP_OML = 1602
PP_NLI = 1603
NPP = 1604
CS_ID, CS_MSU, CS_MIU, CS_MSL, CS_BO, CS_PERM, CS_DM, CS_IC = 0, 128, 256, 384, 512, 640, 768, 896
NCS = 896 + 512


class Res:
    __slots__ = ("name", "w", "r")

    def __init__(self, name):
        self.name = name
        self.w = None
        self.r = {}


class Buf:
    def __init__(self, t, name):
        self.t = t
        self.res = Res(name)

    def __getitem__(self, k):
        return self.t[k]


def _res(x):
    return x.res if isinstance(x, Buf) else x


class _Rec:
    def __init__(self):
        self.call = None

    def __getattr__(self, name):
        def f(*a, **kw):
            self.call = (name, a, kw)
            return None
        return f


class Prog:
    ENG = ["pe", "act", "dve", "pool", "sp"]

    def __init__(self, nc):
        self.nc = nc
        self.q = {e: [] for e in self.ENG}
        self.cnt = {e: 0 for e in self.ENG}
        self.sems = {}
        self.dcnt = {}
        self.waited = {e: {} for e in self.ENG}
        self.mute = False
        for e in self.ENG:
            self.sems[e] = nc.alloc_semaphore("sem_" + e)

    def _deps(self, eng, reads, writes):
        deps = {}

        def add(ev):
            if ev is None:
                return
            k, v = ev
            if deps.get(k, 0) < v:
                deps[k] = v
        for r in reads:
            add(r.w)
        for w in writes:
            add(w.w)
            for k, v in w.r.items():
                add((k, v))
        out = []
        for k, v in deps.items():
            if k == eng and (eng == "pe" or not SAME_ENGINE_SYNC):
                continue
            if self.waited[eng].get(k, 0) >= v:
                continue
            self.waited[eng][k] = v
            out.append((k, v))
        return out

    def _mark(self, ev, reads, writes):
        k, v = ev
        for r in reads:
            if r.r.get(k, 0) < v:
                r.r[k] = v
        for w in writes:
            w.w = ev
            w.r = {}

    def op(self, eng, fn, reads=(), writes=()):
        if self.mute:
            return
        rec = _Rec()
        fn(rec)
        fn = rec.call
        reads = [_res(x) for x in reads]
        writes = [_res(x) for x in writes]
        waits = self._deps(eng, reads, writes)
        self.cnt[eng] += 1
        self.q[eng].append((waits, fn, (eng, 1)))
        self._mark((eng, self.cnt[eng]), reads, writes)

    def dma(self, qeng, key, fn, reads=(), writes=()):
        if self.mute:
            return
        rec = _Rec()
        fn(rec)
        fn = rec.call
        reads = [_res(x) for x in reads]
        writes = [_res(x) for x in writes]
        waits = self._deps(qeng, reads, writes)
        if key not in self.sems:
            self.sems[key] = self.nc.alloc_semaphore("dsem_" + key)
            self.dcnt[key] = 0
        self.dcnt[key] += 16
        self.q[qeng].append((waits, fn, (key, 16)))
        self._mark((key, self.dcnt[key]), reads, writes)

    def emit(self, final_res=()):
        nc = self.nc
        fin = {}
        for r in final_res:
            r = _res(r)
            if r.w is not None:
                k, v = r.w
                fin[k] = max(fin.get(k, 0), v)
        with nc.Block() as block:
            def mk(e):
                def body(eng):
                    for waits, fn, (k, amt) in self.q[e]:
                        for (wk, wv) in waits:
                            eng.wait_ge(self.sems[wk], wv)
                        name, a, kw = fn
                        getattr(eng, name)(*a, **kw).then_inc(self.sems[k], amt)
                    if e == "sp":
                        for wk, wv in fin.items():
                            eng.wait_ge(self.sems[wk], wv)
                return body
            block.tensor(mk("pe"))
            block.scalar(mk("act"))
            block.vector(mk("dve"))
            block.gpsimd(mk("pool"))
            block.sync(mk("sp"))


def build(NT, NL, NTB=2, lam_inits=None, dbg=False):
    nc = bass.Bass("TRN2", target_bir_lowering=False)
    Lp = NT * 128
    P = Prog(nc)

    def dram_in(name, shape, dt=F32):
        return nc.dram_tensor(name, list(shape), dt, kind="ExternalInput").ap()

    def dram_tmp(name, shape, dt):
        return nc.dram_tensor(name, list(shape), dt, kind="Internal").ap()

    xT_in = dram_in("xT", [D, Lp])
    w_in = dram_in("w_in", [NL, D, INW])
    w_branch = dram_in("w_branch", [NL, 1536, D])
    w_out = dram_in("w_out", [NL, D, D])
    ffn_up = dram_in("ffn_up", [NL, D, 2 * DFF])
    ffn_down = dram_in("ffn_down", [NL, DFF, D])
    pool_w = dram_in("pool_w", [NL, 4, 128, 128])
    rw_w2 = dram_in("rw_w2", [NL, 64, 512])
    rw_a2 = dram_in("rw_a2", [NL, 64, 512])
    rw_g2 = dram_in("rw_g2", [NL, 128, 512])
    pp_in = dram_in("pp", [NL, 128, NPP])
    cst_in = dram_in("cst", [128, NCS])
    ropeC = dram_in("ropeC", [128, Lp])
    ropeS = dram_in("ropeS", [128, Lp])
    outT = nc.dram_tensor("outT", [D, Lp], F32, kind="ExternalOutput").ap()
    xTo = nc.dram_tensor("xTo", [D, Lp], F32, kind="ExternalOutput").ap()

    win_bf = dram_tmp("win_bf", [NL, D, INW], BF16)
    wbr_bf = dram_tmp("wbr_bf", [NL, 1536, D], BF16)
    wout_bf = dram_tmp("wout_bf", [NL, D, D], BF16)
    up_bf = dram_tmp("up_bf", [NL, D, 2 * DFF], BF16)
    dn_bf = dram_tmp("dn_bf", [NL, DFF, D], BF16)
    xbufs = [dram_tmp("xT_a", [D, Lp], F32), dram_tmp("xT_b", [D, Lp], F32)]
    kT_hist = dram_tmp("kT_hist", [512, Lp], BF16)
    v_hist = dram_tmp("v_hist", [Lp, 516], BF16)
    dres = {}

    def DR(key):
        if key not in dres:
            dres[key] = Res(key)
        return dres[key]

    def S(name, shape, dt):
        return Buf(nc.alloc_sbuf_tensor("s_" + name, list(shape), dt), name)

    n = NTB * 128
    pb = [Buf(nc.alloc_psum_tensor("pb%d" % i, [128, 512], F32), "pb%d" % i) for i in range(7)]
    pbT = Buf(nc.alloc_psum_tensor("pbT", [128, 1024], BF16), "pbT")
    rr = [0]

    MAINB = [0, 1, 2]
    AUXB = [0, 1, 2]

    def bank(lst=None):
        lst = lst or MAINB
        rr[0] += 1
        return pb[lst[rr[0] % len(lst)]]

    cst = S("cst", [128, NCS], F32)
    cstb = S("cstb", [128, 896], BF16)
    ident2 = S("ident2", [128, 2, 128], BF16)
    ones_bf = S("ones_bf", [128, 128], BF16)
    rmask = S("rmask", [128, n], F32)
    eps6 = S("eps6", [128, 1], F32)
    eps5 = S("eps5", [128, 1], F32)
    epsg = S("epsg", [128, 1], F32)
    eps18 = S("eps18", [128, 1], F32)
    P.dma("sp", "cst", lambda e: e.dma_start(out=cst[:, :], in_=cst_in[:, :]), writes=[cst])
    P.op("dve", lambda e: e.tensor_copy(out=cstb[:, :], in_=cst[:, 0:896]), reads=[cst], writes=[cstb])
    P.op("pool", lambda e: e.tensor_copy(out=ident2[:, 0, :], in_=cst[:, CS_ID:CS_ID + 128]), reads=[cst], writes=[ident2])
    P.op("pool", lambda e: e.tensor_copy(out=ident2[:, 1, :], in_=cst[:, CS_ID:CS_ID + 128]), reads=[cst], writes=[ident2])
    P.op("pool", lambda e: e.memset(ones_bf[:, :], 1.0), writes=[ones_bf])
    P.op("pool", lambda e: e.memset(rmask[:, :], 1.0), writes=[rmask])
    for tt in range(NTB):
        P.op("pool", lambda e, tt=tt: e.memset(rmask[:, tt * 128:tt * 128 + 1], 0.0), writes=[rmask])
    P.op("pool", lambda e: e.memset(eps6[:, :], 1e-6), writes=[eps6])
    P.op("pool", lambda e: e.memset(eps5[:, :], 1e-5), writes=[eps5])
    P.op("pool", lambda e: e.memset(epsg[:, :], 64e-5), writes=[epsg])
    P.op("pool", lambda e: e.memset(eps18[:, :], 1e-18), writes=[eps18])
    ident_bf = lambda: cstb[:, CS_ID:CS_ID + 128]
    msu = lambda: cst[:, CS_MSU:CS_MSU + 128]
    miu = lambda: cst[:, CS_MIU:CS_MIU + 128]
    msl = lambda: cst[:, CS_MSL:CS_MSL + 128]
    bones_bf = lambda: cstb[:, CS_BO:CS_BO + 128]
    perm_bf = lambda: cstb[:, CS_PERM:CS_PERM + 128]
    dmask_bf = lambda: cstb[:, CS_DM:CS_DM + 128]

    def convert(src, dst, rows, key):
        for r0 in range(0, rows, 128):
            r1 = min(rows, r0 + 128)
            P.dma("pool", key, lambda e, r0=r0, r1=r1: e.dma_start(out=dst[r0:r1, :], in_=src[r0:r1, :]),
                  writes=[DR(key)])

    P.mute = ("conv" in SKIP)
    for l in range(NL):
        convert(w_in[l], win_bf[l], D, "cv%d" % l)
        convert(w_branch[l], wbr_bf[l], 1536, "cv%d" % l)
        convert(w_out[l], wout_bf[l], D, "cv%d" % l)
        convert(ffn_up[l], up_bf[l], D, "cv%d" % l)
        convert(ffn_down[l], dn_bf[l], DFF, "cv%d" % l)

    P.mute = False
    pp = S("pp", [128, NPP], F32)
    poolw_bf = S("poolw_bf", [128, 4, 128], BF16)
    w2_bf = S("w2_bf", [128, 512], BF16)
    g2_bf = S("g2_bf", [128, 512], BF16)
    rkones = S("rkones", [128, 4, 128], BF16)
    omka = S("omka", [128, 4], F32)
    neglam = S("neglam", [128, 1], F32)
    lamt = S("lamt", [128, 8], F32)
    sublnw = S("sublnw", [128, 128], F32)

    xt = S("xt", [128, 8, n], F32)
    sq = S("sq", [128, 8, n], BF16)
    hT = S("hT", [128, 8, n], BF16)
    lnv = S("lnv", [128, n], F32)
    rstd = S("rstd", [128, n], F32)
    wg = [S("wg%d" % i, [128, 8, 512], BF16) for i in range(2)]
    gates = S("gates", [128, 24, n], BF16)
    brP = S("brP", [128, 4, n], BF16)
    brA = S("brA", [128, 4, n], BF16)
    brR = S("brR", [128, 4, n], BF16)
    qT = S("qT", [128, 4, n], BF16)
    kTb = S("kTb", [128, 4, n], BF16)
    vaug = S("vaug", [128, NTB, 4, 129], BF16)
    rC = S("rC", [128, n], F32)
    rS = S("rS", [128, n], F32)
    qraw = [S("qraw%d" % i, [128, n], BF16) for i in range(2)]
    scr = [S("scr%d" % i, [128, n], F32) for i in range(8)]
    pu = S("pu", [128, 4, 16 + n], F32)
    pta = S("pta", [128, 16 + n], F32)
    ptb = S("ptb", [128, 16 + n], F32)
    pooled = S("pooled", [128, 4, n], BF16)
    rp = S("rp", [128, 12, n], BF16)
    ltmp = [S("ltmp%d" % i, [128, 1 + n], F32) for i in range(2)]
    ldt = [S("ldt%d" % i, [128, n], F32) for i in range(2)]
    lora = S("lora", [128, 2, n], F32)
    rhalo = S("rhalo", [128, 14], F32)
    tw_bf = S("tw_bf", [128, n], BF16)
    sg_bf = S("sg_bf", [128, n], BF16)
    kk2_bf = S("kk2_bf", [128, n], BF16)
    rk_bf = S("rk_bf", [128, n], BF16)
    AR = S("AR", [128, 4, NTB, 2, 128], BF16)
    kt_bf = S("kt_bf", [128, 4, n], BF16)
    bt_bf = S("bt_bf", [128, 4, n], BF16)
    tokm = S("tokm", [128, NTB, 4, 3, 128], BF16)
    gT = S("gT", [128, 4, n], BF16)
    bonT = S("bonT", [128, 4, n], BF16)
    emt = S("emt", [128, 4, NTB], F32)
    eet = S("eet", [128, 4, NTB], F32)
    emet = S("emet", [128, 4, NTB], F32)
    Qp = [S("Qp%d" % i, [128, 8, 128], BF16) for i in range(2)]
    Np = [S("Np%d" % i, [128, 8, 128], BF16) for i in range(2)]
    Sb = [S("Sb%d" % i, [128, 8, 128], BF16) for i in range(2)]
    ABm = S("ABm", [128, 8, 128], BF16)
    AKm = S("AKm", [128, 8, 128], BF16)
    RKm = S("RKm", [128, 8, 128], BF16)
    Xs = S("Xs", [128, 512], BF16)
    Us = S("Us", [128, 512], BF16)
    Ysb = S("Ysb", [128, 512], F32)
    ysq = S("ysq", [128, 512], F32)
    yc = S("yc", [128, 512], F32)
    ynb = S("ynb", [128, 512], BF16)
    yst = S("yst", [128, 40], F32)
    Hs32 = [S("Hs32_%d" % j, [128, 64], F32) for j in range(4)]
    Hb = [S("Hb_%d" % j, [128, 64], BF16) for j in range(4)]
    Hpe = [S("Hpe_%d" % j, [128, 64], F32) for j in range(4)]
    kTk = [S("kTk%d" % i, [128, n], BF16) for i in range(2)]
    vk = [S("vk%d" % i, [128, NTB, 129], BF16) for i in range(2)]
    ET = [S("ET%d" % i, [128, n], BF16) for i in range(3)]
    ao = S("ao", [128, 128], F32)
    at = S("at", [128, 128], F32)
    aj = S("aj", [128, 128], F32)
    aon = S("aon", [128, 128], BF16)
    ast = S("ast", [128, 8], F32)
    merged = S("merged", [128, 8, n], BF16)
    mtmp = [S("mtmp%d" % i, [128, n], BF16) for i in range(2)]
    wbr = S("wbr", [128, 4, 1024], BF16)
    fact = S("fact", [128, 22, n], BF16)
    dnw = [S("dnw%d" % i, [128, 22, 128], BF16) for i in range(2)]
    chalo = S("chalo", [128, 44, 2], F32)
    cgs = gates
    invc = lambda g: cst[:, CS_IC + g * 128:CS_IC + (g + 1) * 128]

    slot_res = {}

    def oslot(c, qi):
        idx = c * NTB + qi
        bk = 3 + idx
        return pb[bk], 0, pb[bk].res
    assert 2 * NTB <= 4

    nblocks = (NT + NTB - 1) // NTB

    def norm_stage(gcol, nb_):
        P.op("act", lambda e: e.activation(out=sq[:, :, :nb_], in_=xt[:, :, :nb_], func=AF.Square), reads=[xt], writes=[sq])
        ps = bank()
        for c in range(8):
            P.op("pe", lambda e, c=c: e.matmul(ps[:, :nb_], lhsT=ones_bf[:, :], rhs=sq[:, c, :nb_], start=(c == 0), stop=(c == 7)),
                 reads=[ones_bf, sq], writes=[ps])
        P.op("act", lambda e: e.activation(out=lnv[:, :nb_], in_=ps[:, :nb_], func=AF.Ln, bias=eps6[:, 0:1], scale=1.0 / D),
             reads=[ps, eps6], writes=[lnv])
        P.op("act", lambda e: e.activation(out=rstd[:, :nb_], in_=lnv[:, :nb_], func=AF.Exp, scale=-0.5), reads=[lnv], writes=[rstd])
        for c in range(8):
            P.op("dve", lambda e, c=c: e.scalar_tensor_tensor(out=hT[:, c, :nb_], in0=xt[:, c, :nb_], scalar=pp[:, gcol + c:gcol + c + 1],
                                                              in1=rstd[:, :nb_], op0=ALU.mult, op1=ALU.mult),
                 reads=[xt, pp, rstd], writes=[hT])

    wgi = [0]

    def load_wg(src, c0, gc, key):
        wgi[0] += 1
        w = wg[wgi[0] % 2]
        P.dma("sp", w.res.name, lambda e: e.dma_start(out=w[:, :, :gc], in_=src[:, c0:c0 + gc].rearrange("(c p) m -> p c m", p=128)),
              reads=[DR(key)], writes=[w])
        return w

    def proj_fm(w, ml, nb_, ps):
        for c in range(8):
            P.op("pe", lambda e, c=c: e.matmul(ps[:, :nb_], lhsT=w[:, c, ml * 128:(ml + 1) * 128], rhs=hT[:, c, :nb_], start=(c == 0), stop=(c == 7)),
                 reads=[w, hT], writes=[ps])

    def layer(l, xsrc, xdst, last):
        P.dma("sp", "pp", lambda e: e.dma_start(out=pp[:, :], in_=pp_in[l]), writes=[pp])
        P.dma("pool", "poolw", lambda e: e.dma_start(out=poolw_bf[:, :, :], in_=pool_w[l].rearrange("g c d -> c g d")), writes=[poolw_bf])
        P.dma("pool", "w2", lambda e: e.dma_start(out=w2_bf[0:64, :], in_=rw_w2[l]), writes=[w2_bf])
        P.dma("pool", "w2", lambda e: e.dma_start(out=w2_bf[64:128, :], in_=rw_a2[l]), writes=[w2_bf])
        P.dma("pool", "g2", lambda e: e.dma_start(out=g2_bf[:, :], in_=rw_g2[l]), writes=[g2_bf])
        for j in range(4):
            P.op("dve", lambda e, j=j: e.tensor_scalar(out=rkones[:, j, :], in0=cst[:, CS_BO:CS_BO + 128], scalar1=pp[:, PP_RK + j:PP_RK + j + 1],
                                                      scalar2=None, op0=ALU.mult), reads=[cst, pp], writes=[rkones])
        P.op("dve", lambda e: e.tensor_scalar(out=omka[:, :], in0=pp[:, PP_KA:PP_KA + 4], scalar1=-1.0, scalar2=1.0, op0=ALU.mult, op1=ALU.add),
             reads=[pp], writes=[omka])
        P.op("dve", lambda e: e.tensor_scalar(out=sublnw[:, :], in0=pp[:, PP_SUBLN:PP_SUBLN + 128], scalar1=pp[:, PP_OML:PP_OML + 1], scalar2=None, op0=ALU.mult),
             reads=[pp], writes=[sublnw])
        P.op("dve", lambda e: e.tensor_tensor(out=aj[:, 0:64], in0=pp[:, PP_LAM:PP_LAM + 64], in1=pp[:, PP_LAM + 64:PP_LAM + 128], op=ALU.mult), reads=[pp], writes=[aj])
        P.op("dve", lambda e: e.tensor_tensor(out=aj[:, 64:128], in0=pp[:, PP_LAM + 128:PP_LAM + 192], in1=pp[:, PP_LAM + 192:PP_LAM + 256], op=ALU.mult), reads=[pp], writes=[aj])
        P.op("dve", lambda e: e.tensor_reduce(out=lamt[:, 0:2], in_=aj[:, :].rearrange("p (a b) -> p a b", b=64), axis=AX.X, op=ALU.add), reads=[aj], writes=[lamt])
        P.op("act", lambda e: e.activation(out=lamt[:, 2:4], in_=lamt[:, 0:2], func=AF.Exp), reads=[lamt], writes=[lamt])
        P.op("dve", lambda e: e.tensor_tensor(out=lamt[:, 4:5], in0=lamt[:, 3:4], in1=lamt[:, 2:3], op=ALU.subtract), reads=[lamt], writes=[lamt])
        P.op("dve", lambda e: e.tensor_scalar(out=neglam[:, :], in0=lamt[:, 4:5], scalar1=pp[:, PP_NLI:PP_NLI + 1], scalar2=None, op0=ALU.add), reads=[lamt, pp], writes=[neglam])
        P.op("pool", lambda e: e.memset(rhalo[:, :], 0.0), writes=[rhalo])
        P.op("pool", lambda e: e.memset(pu[:, :, 0:16], 0.0), writes=[pu])
        P.op("pool", lambda e: e.memset(chalo[:, :, :], 0.0), writes=[chalo])
        for j in range(4):
            P.op("pool", lambda e, j=j: e.memset(Hs32[j][:, :], 0.0), writes=[Hs32[j]])

        WIN = win_bf[l]
        for b in range(nblocks):
            t0 = b * n
            nt = min(NTB, NT - b * NTB)
            nb_ = nt * 128
            tsl = slice(t0, t0 + nb_)
            xin_key = "x%d_%d_%d" % (l, 0, b)
            P.dma("sp", "xt", lambda e: e.dma_start(out=xt[:, :, :nb_], in_=xsrc[:, tsl].rearrange("(c p) t -> p c t", p=128)),
                  reads=[DR("xs%d_%d" % (l, b))], writes=[xt])
            P.dma("sp", "rC", lambda e: e.dma_start(out=rC[:, :nb_], in_=ropeC[:, tsl]), writes=[rC])
            P.dma("sp", "rS", lambda e: e.dma_start(out=rS[:, :nb_], in_=ropeS[:, tsl]), writes=[rS])
            norm_stage(PP_GMIX, nb_)

            P.mute = ("pool" in SKIP)
            w = load_wg(WIN, 0, 512, "cv%d" % l)
            for g in range(4):
                ps = bank()
                proj_fm(w, g, nb_, ps)
                P.op("act", lambda e, g=g, ps=ps: e.activation(out=pu[:, g, 16:16 + nb_], in_=ps[:, :nb_], func=AF.Copy), reads=[ps], writes=[pu])
                src = pu
                bufs = [pta, ptb]
                cur = None
                for lev in range(g + 1):
                    sh = 1 << lev
                    lo = 2 * sh - 1
                    dst = bufs[lev % 2]
                    if lev == 0:
                        P.op("dve", lambda e, dst=dst, g=g, lo=lo, sh=sh: e.tensor_tensor(out=dst[:, lo:16 + nb_], in0=pu[:, g, lo:16 + nb_], in1=pu[:, g, lo - sh:16 + nb_ - sh], op=ALU.add),
                             reads=[pu], writes=[dst])
                    else:
                        P.op("dve", lambda e, dst=dst, cur=cur, lo=lo, sh=sh: e.tensor_tensor(out=dst[:, lo:16 + nb_], in0=cur[:, lo:16 + nb_], in1=cur[:, lo - sh:16 + nb_ - sh], op=ALU.add),
                             reads=[cur], writes=[dst])
                    cur = dst
                wv = float(2 << g)
                P.op("dve", lambda e, g=g, cur=cur, wv=wv: e.scalar_tensor_tensor(out=pooled[:, g, :nb_], in0=cur[:, 16:16 + nb_], scalar=1.0 / wv, in1=pu[:, g, 16:16 + nb_],
                                                                                  op0=ALU.mult, op1=ALU.subtract), reads=[cur, pu], writes=[pooled])
                if b == 0:
                    P.op("dve", lambda e, g=g, cur=cur: e.tensor_tensor(out=cur[:, 16:144], in0=cur[:, 16:144], in1=invc(g), op=ALU.mult), reads=[cur, cst], writes=[cur])
                    P.op("dve", lambda e, g=g, cur=cur: e.tensor_tensor(out=pooled[:, g, 0:128], in0=cur[:, 16:144], in1=pu[:, g, 16:144], op=ALU.subtract),
                         reads=[cur, pu], writes=[pooled])
                P.op("pool", lambda e, g=g: e.tensor_copy(out=pu[:, g, 0:16], in_=pu[:, g, nb_:nb_ + 16]), reads=[pu], writes=[pu])
                ps2 = bank(AUXB)
                P.op("pe", lambda e, g=g, ps2=ps2: e.matmul(ps2[:, :nb_], lhsT=poolw_bf[:, g, :], rhs=pooled[:, g, :nb_], start=True, stop=True),
                     reads=[poolw_bf, pooled], writes=[ps2])
                P.op("act", lambda e, g=g, ps2=ps2: e.activation(out=brP[:, g, :nb_], in_=ps2[:, :nb_], func=AF.Identity, scale=pp[:, PP_PSC + g:PP_PSC + g + 1]),
                     reads=[ps2, pp], writes=[brP])

            P.mute = ("qk" in SKIP)
            for which in range(2):
                w = load_wg(WIN, 512 + which * 512, 512, "cv%d" % l)
                dstb = qT if which == 0 else kTb
                for m in range(4):
                    ps = bank()
                    proj_fm(w, m, nb_, ps)
                    qr = qraw[m % 2]
                    P.op("act", lambda e, ps=ps, qr=qr: e.activation(out=qr[:, :nb_], in_=ps[:, :nb_], func=AF.Copy), reads=[ps], writes=[qr])
                    ps2 = bank(AUXB)
                    P.op("pe", lambda e, ps2=ps2, qr=qr: e.matmul(ps2[:, :nb_], lhsT=perm_bf(), rhs=qr[:, :nb_], start=True, stop=True), reads=[cstb, qr], writes=[ps2])
                    s1 = scr[(2 * m) % 8]
                    s2 = scr[(2 * m + 1) % 8]
                    P.op("dve", lambda e, qr=qr, s1=s1: e.tensor_tensor(out=s1[:, :nb_], in0=qr[:, :nb_], in1=rC[:, :nb_], op=ALU.mult), reads=[qr, rC], writes=[s1])
                    P.op("dve", lambda e, ps2=ps2, s2=s2: e.tensor_tensor(out=s2[:, :nb_], in0=ps2[:, :nb_], in1=rS[:, :nb_], op=ALU.mult), reads=[ps2, rS], writes=[s2])
                    P.op("pool", lambda e, s1=s1, s2=s2, m=m, dstb=dstb: e.tensor_tensor(out=dstb[:, m, :nb_], in0=s1[:, :nb_], in1=s2[:, :nb_], op=ALU.add),
                         reads=[s1, s2], writes=[dstb])
            P.dma("pool", "kst", lambda e: e.dma_start(out=kT_hist[:, tsl].rearrange("(h p) t -> p h t", p=128), in_=kTb[:, :, :nb_]),
                  reads=[kTb], writes=[DR("kh%d" % b)])

            P.mute = ("v" in SKIP)
            w = load_wg(WIN, 1536, 512, "cv%d" % l)
            P.op("pool", lambda e: e.memset(vaug[:, :, :, 128:129], 1.0), writes=[vaug])
            if b == 0:
                P.op("pool", lambda e: e.memset(vaug[0:NPAD, 0, :, 128:129], 0.0), writes=[vaug])
            for tt in range(nt):
                ps = bank()
                for c in range(8):
                    P.op("pe", lambda e, c=c, ps=ps, tt=tt, w=w: e.matmul(ps[:, :], lhsT=hT[:, c, tt * 128:(tt + 1) * 128], rhs=w[:, c, :], start=(c == 0), stop=(c == 7)),
                         reads=[hT, w], writes=[ps])
                P.op("act", lambda e, ps=ps, tt=tt: e.activation(out=vaug[:, tt, :, 0:128], in_=ps[:, :].rearrange("p (h e) -> p h e", h=4), func=AF.Copy),
                     reads=[ps], writes=[vaug])
            P.dma("pool", "vst", lambda e: e.dma_start(out=v_hist[tsl, :].rearrange("(t p) f -> p t f", p=128), in_=vaug[:, :nt, :, :].rearrange("p t h e -> p t (h e)")),
                  reads=[vaug], writes=[DR("vh%d" % b)])

            P.mute = ("rwproj" in SKIP)
            def lerp_tile(ps, mi, out_ap_fn, outbuf):
                lt = ltmp[mi % 2]
                ld = ldt[mi % 2]
                P.op("pool", lambda e: e.tensor_copy(out=lt[:, 0:1], in_=rhalo[:, mi:mi + 1]), reads=[rhalo], writes=[lt])
                P.op("act", lambda e: e.activation(out=lt[:, 1:1 + nb_], in_=ps[:, :nb_], func=AF.Copy), reads=[ps], writes=[lt])
                P.op("pool", lambda e: e.tensor_copy(out=rhalo[:, mi:mi + 1], in_=lt[:, nb_:nb_ + 1]), reads=[lt], writes=[rhalo])
                P.op("dve", lambda e: e.tensor_tensor(out=ld[:, :nb_], in0=lt[:, 0:nb_], in1=lt[:, 1:1 + nb_], op=ALU.subtract), reads=[lt], writes=[ld])
                P.op("dve", lambda e: e.scalar_tensor_tensor(out=out_ap_fn(), in0=ld[:, :nb_], scalar=pp[:, PP_MU + mi:PP_MU + mi + 1], in1=lt[:, 1:1 + nb_],
                                                             op0=ALU.mult, op1=ALU.add), reads=[ld, lt, pp], writes=[outbuf])
            w = load_wg(WIN, 3584, 256, "cv%d" % l)
            for m in range(2):
                ps = bank()
                proj_fm(w, m, nb_, ps)
                lerp_tile(ps, 12 + m, lambda m=m: lora[:, m, :nb_], lora)
            P.op("act", lambda e: e.activation(out=tw_bf[0:64, :nb_], in_=lora[0:64, 0, :nb_], func=AF.Tanh), reads=[lora], writes=[tw_bf])
            P.op("act", lambda e: e.activation(out=tw_bf[64:128, :nb_], in_=lora[64:128, 0, :nb_], func=AF.Copy), reads=[lora], writes=[tw_bf])
            P.op("act", lambda e: e.activation(out=sg_bf[:, :nb_], in_=lora[:, 1, :nb_], func=AF.Sigmoid), reads=[lora], writes=[sg_bf])
            for which in range(3):
                w = load_wg(WIN, 2048 + which * 512, 512, "cv%d" % l)
                for m in range(4):
                    ps = bank()
                    proj_fm(w, m, nb_, ps)
                    mi = which * 4 + m
                    lerp_tile(ps, mi, lambda mi=mi: rp[:, mi, :nb_], rp)

            P.mute = ("gates" in SKIP)
            for gg in range(6):
                w = load_wg(WIN, 3840 + gg * 512, 512, "cv%d" % l)
                for m in range(4):
                    ps = bank()
                    proj_fm(w, m, nb_, ps)
                    gi = gg * 4 + m
                    P.op("act", lambda e, ps=ps, gi=gi: e.activation(out=gates[:, gi, :nb_], in_=ps[:, :nb_], func=AF.Sigmoid), reads=[ps], writes=[gates])

            P.mute = ("attn" in SKIP)
            ei = 0
            for h in range(4):
                for kb in range(b + 1):
                    nkt = min(NTB, NT - kb * NTB)
                    nk = nkt * 128
                    kk_ = kTk[(h * 64 + kb) % 2]
                    vv = vk[(h * 64 + kb) % 2]
                    ksl = slice(kb * n, kb * n + nk)
                    P.dma("sp", kk_.res.name, lambda e, kk_=kk_, ksl=ksl, nk=nk: e.dma_start(out=kk_[:, :nk], in_=kT_hist[h * 128:(h + 1) * 128, ksl]),
                          reads=[DR("kh%d" % kb)], writes=[kk_])
                    P.dma("sp", vv.res.name, lambda e, vv=vv, ksl=ksl, nkt=nkt: e.dma_start(out=vv[:, :nkt, :], in_=v_hist[ksl, h * 129:(h + 1) * 129].rearrange("(t p) f -> p t f", p=128)),
                          reads=[DR("vh%d" % kb)], writes=[vv])
                    for jj in range(nkt):
                        jg = kb * NTB + jj
                        q0 = 0 if kb < b else jj
                        ncol = (nt - q0) * 128
                        if ncol <= 0:
                            continue
                        for c in range(2):
                            st = bank()
                            et = ET[ei % 3]
                            ei += 1
                            P.op("pe", lambda e, st=st, kk_=kk_, c=c, jj=jj, q0=q0, ncol=ncol: e.matmul(st[:, :ncol], lhsT=kk_[c * 64:(c + 1) * 64, jj * 128:(jj + 1) * 128],
                                                                                                    rhs=qT[c * 64:(c + 1) * 64, h, q0 * 128:q0 * 128 + ncol], start=True, stop=True),
                                 reads=[kk_, qT], writes=[st])
                            P.op("act", lambda e, st=st, et=et, ncol=ncol: e.activation(out=et[:, :ncol], in_=st[:, :ncol], func=AF.Exp, scale=0.125), reads=[st], writes=[et])
                            if kb == b and jg > 0:
                                P.op("pool", lambda e, et=et: e.tensor_tensor(out=et[:, 0:128], in0=et[:, 0:128], in1=dmask_bf(), op=ALU.mult), reads=[et, cstb], writes=[et])
                            for qi in range(q0, nt):
                                ob, off, ores = oslot(c, qi)
                                ig = b * NTB + qi
                                P.op("pe", lambda e, ob=ob, off=off, et=et, qi=qi, q0=q0, vv=vv, jj=jj, jg=jg, ig=ig: e.matmul(
                                    ob[:, off:off + 129], lhsT=et[:, (qi - q0) * 128:(qi - q0 + 1) * 128], rhs=vv[:, jj, :], start=(jg == 0), stop=(jg == ig)),
                                    reads=[et, vv], writes=[ores])
                        if kb == b:
                            qi = jj
                            o0b, o0off, o0r = oslot(0, qi)
                            o1b, o1off, o1r = oslot(1, qi)
                            P.op("dve", lambda e, o0b=o0b, o0off=o0off: e.reciprocal(out=ast[:, 0:1], in_=o0b[:, o0off + 128:o0off + 129]), reads=[o0r], writes=[ast])
                            P.op("dve", lambda e, o1b=o1b, o1off=o1off: e.reciprocal(out=ast[:, 1:2], in_=o1b[:, o1off + 128:o1off + 129]), reads=[o1r], writes=[ast])
                            P.op("dve", lambda e: e.tensor_tensor(out=ast[:, 2:3], in0=ast[:, 1:2], in1=neglam[:, 0:1], op=ALU.mult), reads=[ast, neglam], writes=[ast])
                            P.op("dve", lambda e, o1b=o1b, o1off=o1off: e.tensor_scalar(out=at[:, :], in0=o1b[:, o1off:o1off + 128], scalar1=ast[:, 2:3], scalar2=None, op0=ALU.mult),
                                 reads=[o1r, ast], writes=[at])
                            P.op("dve", lambda e, o0b=o0b, o0off=o0off: e.scalar_tensor_tensor(out=ao[:, :], in0=o0b[:, o0off:o0off + 128], scalar=ast[:, 0:1], in1=at[:, :],
                                                                                              op0=ALU.mult, op1=ALU.add), reads=[o0r, ast, at], writes=[ao])
                            P.op("act", lambda e: e.activation(out=aj[:, :], in_=ao[:, :], func=AF.Square, accum_out=ast[:, 3:4]), reads=[ao], writes=[aj, ast])
                            P.op("act", lambda e: e.activation(out=ast[:, 4:5], in_=ast[:, 3:4], func=AF.Ln, bias=eps5[:, 0:1], scale=1.0 / 128), reads=[ast, eps5], writes=[ast])
                            P.op("act", lambda e: e.activation(out=ast[:, 5:6], in_=ast[:, 4:5], func=AF.Exp, scale=-0.5), reads=[ast], writes=[ast])
                            P.op("dve", lambda e: e.scalar_tensor_tensor(out=aon[:, :], in0=ao[:, :], scalar=ast[:, 5:6], in1=sublnw[:, :], op0=ALU.mult, op1=ALU.mult),
                                 reads=[ao, ast, sublnw], writes=[aon])
                            P.op("pe", lambda e: e.transpose(pbT[:, 0:128], aon[:, :], ident_bf()), reads=[aon, cstb], writes=[pbT])
                            P.op("act", lambda e, qi=qi: e.activation(out=brA[:, h, qi * 128:(qi + 1) * 128], in_=pbT[:, 0:128], func=AF.Copy), reads=[pbT], writes=[brA])

            P.mute = ("rwprep" in SKIP)
            A_, B_, C_, D_, E_, F_, G_ = scr[0], scr[1], scr[2], scr[3], scr[4], scr[5], scr[6]
            v3 = lambda buf: buf[:, :nb_].rearrange("p (t s) -> p t s", s=128)
            for j in range(4):
                jc = slice(j * 128, (j + 1) * 128)
                r_ap = lambda: rp[:, j, :nb_]
                k_ap = lambda: rp[:, 4 + j, :nb_]
                v_ap = lambda: rp[:, 8 + j, :nb_]
                ps = bank()
                P.op("pe", lambda e, ps=ps, jc=jc: e.matmul(ps[:, :nb_], lhsT=w2_bf[0:64, jc], rhs=tw_bf[0:64, :nb_], start=True, stop=True), reads=[w2_bf, tw_bf], writes=[ps])
                P.op("act", lambda e, ps=ps, j=j: e.activation(out=A_[:, :nb_], in_=ps[:, :nb_], func=AF.Sigmoid, bias=pp[:, PP_W0 + j:PP_W0 + j + 1]), reads=[ps, pp], writes=[A_])
                ps = bank()
                P.op("pe", lambda e, ps=ps, jc=jc: e.matmul(ps[:, :nb_], lhsT=w2_bf[64:128, jc], rhs=tw_bf[64:128, :nb_], start=True, stop=True), reads=[w2_bf, tw_bf], writes=[ps])
                P.op("act", lambda e, ps=ps, j=j: e.activation(out=B_[:, :nb_], in_=ps[:, :nb_], func=AF.Sigmoid, bias=pp[:, PP_A0 + j:PP_A0 + j + 1]), reads=[ps, pp], writes=[B_])
                ps = bank()
                P.op("pe", lambda e, ps=ps, jc=jc: e.matmul(ps[:, :nb_], lhsT=g2_bf[:, jc], rhs=sg_bf[:, :nb_], start=True, stop=True), reads=[g2_bf, sg_bf], writes=[ps])
                P.op("act", lambda e, ps=ps, j=j: e.activation(out=gT[:, j, :nb_], in_=ps[:, :nb_], func=AF.Copy), reads=[ps], writes=[gT])
                P.op("dve", lambda e: e.tensor_tensor_scan(out=C_[:, :nb_], data0=rmask[:, :nb_], data1=A_[:, :nb_], initial=0.0, op0=ALU.mult, op1=ALU.add),
                     reads=[rmask, A_], writes=[C_])
                P.op("dve", lambda e: e.tensor_tensor(out=D_[:, :nb_], in0=C_[:, :nb_], in1=A_[:, :nb_], op=ALU.subtract), reads=[C_, A_], writes=[D_])
                P.op("dve", lambda e: e.tensor_tensor(out=v3(A_), in0=v3(C_), in1=v3(C_)[:, :, 63:64].broadcast_to([128, nt, 128]), op=ALU.subtract), reads=[C_], writes=[A_])
                P.op("dve", lambda e: e.tensor_tensor(out=v3(E_), in0=v3(D_), in1=v3(C_)[:, :, 63:64].broadcast_to([128, nt, 128]), op=ALU.subtract), reads=[C_, D_], writes=[E_])
                P.op("act", lambda e, j=j: e.activation(out=emt[:, j, :nt], in_=v3(C_)[:, :, 63], func=AF.Exp, scale=C1), reads=[C_], writes=[emt])
                P.op("act", lambda e, j=j: e.activation(out=eet[:, j, :nt], in_=v3(A_)[:, :, 127], func=AF.Exp, scale=C1), reads=[A_], writes=[eet])
                P.op("act", lambda e, j=j: e.activation(out=emet[:, j, :nt], in_=v3(C_)[:, :, 127], func=AF.Exp, scale=C1), reads=[C_], writes=[emet])
                P.op("act", lambda e: e.activation(out=D_[:, :nb_], in_=A_[:, :nb_], func=AF.Exp, scale=C1), reads=[A_], writes=[D_])
                P.op("act", lambda e: e.activation(out=F_[:, :nb_], in_=A_[:, :nb_], func=AF.Exp, scale=-C1), reads=[A_], writes=[F_])
                P.op("act", lambda e: e.activation(out=G_[:, :nb_], in_=E_[:, :nb_], func=AF.Exp, scale=C1), reads=[E_], writes=[G_])
                P.op("act", lambda e, j=j: e.activation(out=kk2_bf[:, :nb_], in_=k_ap(), func=AF.Square, scale=pp[:, PP_KK + j:PP_KK + j + 1]), reads=[rp, pp], writes=[kk2_bf])
                ps = bank()
                P.op("pe", lambda e, ps=ps: e.matmul(ps[:, :nb_], lhsT=bones_bf(), rhs=kk2_bf[:, :nb_], start=True, stop=True), reads=[cstb, kk2_bf], writes=[ps])
                P.op("act", lambda e, ps=ps: e.activation(out=C_[:, :nb_], in_=ps[:, :nb_], func=AF.Ln, bias=eps18[:, 0:1]), reads=[ps, eps18], writes=[C_])
                P.op("act", lambda e: e.activation(out=C_[:, :nb_], in_=C_[:, :nb_], func=AF.Exp, scale=-0.5), reads=[C_], writes=[C_])
                P.op("dve", lambda e, j=j: e.scalar_tensor_tensor(out=A_[:, :nb_], in0=k_ap(), scalar=pp[:, PP_KK + j:PP_KK + j + 1], in1=C_[:, :nb_], op0=ALU.mult, op1=ALU.mult),
                     reads=[rp, pp, C_], writes=[A_])
                P.op("dve", lambda e, j=j: e.tensor_scalar(out=E_[:, :nb_], in0=B_[:, :nb_], scalar1=pp[:, PP_KA + j:PP_KA + j + 1], scalar2=omka[:, j:j + 1], op0=ALU.mult, op1=ALU.add),
                     reads=[B_, pp, omka], writes=[E_])
                P.op("dve", lambda e: e.tensor_tensor(out=E_[:, :nb_], in0=E_[:, :nb_], in1=k_ap(), op=ALU.mult), reads=[E_, rp], writes=[E_])
                P.op("dve", lambda e: e.tensor_tensor(out=C_[:, :nb_], in0=A_[:, :nb_], in1=B_[:, :nb_], op=ALU.mult), reads=[A_, B_], writes=[C_])
                P.op("dve", lambda e, j=j: e.scalar_tensor_tensor(out=AR[:, j, :nt, 0, :], in0=v3(A_), scalar=-1.0, in1=v3(G_), op0=ALU.mult, op1=ALU.mult), reads=[A_, G_], writes=[AR])
                P.op("dve", lambda e, j=j: e.tensor_tensor(out=AR[:, j, :nt, 1, :], in0=rp[:, j, :nb_].rearrange("p (t s) -> p t s", s=128), in1=v3(D_), op=ALU.mult), reads=[rp, D_], writes=[AR])
                P.op("dve", lambda e, j=j: e.tensor_tensor(out=kt_bf[:, j, :nb_], in0=E_[:, :nb_], in1=F_[:, :nb_], op=ALU.mult), reads=[E_, F_], writes=[kt_bf])
                P.op("dve", lambda e, j=j: e.tensor_tensor(out=bt_bf[:, j, :nb_], in0=C_[:, :nb_], in1=F_[:, :nb_], op=ALU.mult), reads=[C_, F_], writes=[bt_bf])
                P.op("dve", lambda e: e.tensor_tensor(out=rk_bf[:, :nb_], in0=r_ap(), in1=E_[:, :nb_], op=ALU.mult), reads=[rp, E_], writes=[rk_bf])
                ps = bank()
                P.op("pe", lambda e, ps=ps, j=j: e.matmul(ps[:, :nb_], lhsT=rkones[:, j, :], rhs=rk_bf[:, :nb_], start=True, stop=True), reads=[rkones, rk_bf], writes=[ps])
                P.op("dve", lambda e, ps=ps, j=j: e.tensor_tensor(out=bonT[:, j, :nb_], in0=ps[:, :nb_], in1=v_ap(), op=ALU.mult), reads=[ps, rp], writes=[bonT])
                for tt in range(nt):
                    tsl2 = slice(tt * 128, (tt + 1) * 128)
                    P.op("pe", lambda e, j=j, tsl2=tsl2: e.transpose(pbT[:, 0:128], bt_bf[:, j, tsl2], ident_bf()), reads=[bt_bf, cstb], writes=[pbT])
                    P.op("pe", lambda e, j=j, tsl2=tsl2: e.transpose(pbT[:, 128:256], kt_bf[:, j, tsl2], ident_bf()), reads=[kt_bf, cstb], writes=[pbT])
                    P.op("pe", lambda e, j=j, tsl2=tsl2: e.transpose(pbT[:, 256:384], rp[:, 8 + j, tsl2], ident_bf()), reads=[rp, cstb], writes=[pbT])
                    P.op("act", lambda e, j=j, tt=tt: e.activation(out=tokm[:, tt, j, :, :], in_=pbT[:, 0:384].rearrange("p (a b) -> p a b", a=3), func=AF.Copy), reads=[pbT], writes=[tokm])

            P.mute = ("rwchain" in SKIP)
            for tt in range(nt):
                tsl2 = slice(tt * 128, (tt + 1) * 128)
                for j in range(4):
                    P.op("dve", lambda e, j=j, tt=tt: e.tensor_scalar(out=Hb[j][:, :], in0=Hs32[j][:, :], scalar1=emt[:, j, tt:tt + 1], scalar2=None, op0=ALU.mult), reads=[Hs32[j], emt], writes=[Hb[j]])
                    P.op("pool", lambda e, j=j, tt=tt: e.tensor_scalar(out=Hpe[j][:, :], in0=Hs32[j][:, :], scalar1=emet[:, j, tt:tt + 1], scalar2=None, op0=ALU.mult), reads=[Hs32[j], emet], writes=[Hpe[j]])
                P.mute = ("rwchain" in SKIP) or ("rc_gram" in SKIP)
                def gram(lhs_buf, lhs_fn, rhs_fn, rhs_bufs, mask_fn, dst, eng):
                    for half in range(2):
                        ps = bank()
                        for hh in range(4):
                            j, hp = hh, half
                            bp = slice(hp * 64, hp * 64 + 64)
                            P.op("pe", lambda e, ps=ps, hh=hh, j=j, bp=bp: e.matmul(ps[:, hh * 128:(hh + 1) * 128], lhsT=lhs_fn(j, bp), rhs=rhs_fn(j, bp), start=True, stop=True),
                                 reads=lhs_buf + rhs_bufs, writes=[ps])
                        P.op(eng, lambda e, ps=ps, half=half: e.tensor_tensor(out=dst[:, half * 4:half * 4 + 4, :], in0=ps[:, :].rearrange("p (h s) -> p h s", h=4),
                                                                            in1=mask_fn().unsqueeze(1).broadcast_to([128, 4, 128]), op=ALU.mult), reads=[ps, cst], writes=[dst])
                bt_l = lambda j, bp: bt_bf[bp, j, tsl2]
                kt_l = lambda j, bp: kt_bf[bp, j, tsl2]
                a_r = lambda j, bp: AR[bp, j, tt, 0, :]
                r_r = lambda j, bp: AR[bp, j, tt, 1, :]
                gram([bt_bf], bt_l, a_r, [AR], msu, Qp[0], "dve")
                gram([AR], a_r, bt_l, [bt_bf], msl, Np[0], "dve")
                gram([kt_bf], kt_l, a_r, [AR], msu, AKm, "dve")
                gram([bt_bf], bt_l, r_r, [AR], miu, ABm, "dve")
                gram([kt_bf], kt_l, r_r, [AR], miu, RKm, "dve")
                P.mute = ("rwchain" in SKIP) or ("rc_inv" in SKIP)
                for jp in range(4):
                    P.op("pool", lambda e, jp=jp: e.tensor_tensor(out=Sb[0][:, 2 * jp:2 * jp + 2, :], in0=Qp[0][:, 2 * jp:2 * jp + 2, :], in1=ident2[:, :, :], op=ALU.add),
                         reads=[Qp[0], ident2], writes=[Sb[0]])
                qc, ncur, sc = 0, 0, 0
                for lev in range(6):
                    qn, nn, sn = 1 - qc, 1 - ncur, 1 - sc
                    lastlev = (lev == 5)
                    if not lastlev:
                        for half in range(2):
                            ps = bank()
                            for hh in range(4):
                                h8 = half * 4 + hh
                                P.op("pe", lambda e, ps=ps, hh=hh, h8=h8, qc=qc, ncur=ncur: e.matmul(ps[:, hh * 128:(hh + 1) * 128], lhsT=Np[ncur][:, h8, :], rhs=Qp[qc][:, h8, :], start=True, stop=True),
                                     reads=[Np[ncur], Qp[qc]], writes=[ps])
                            P.op("act", lambda e, ps=ps, half=half, qn=qn: e.activation(out=Qp[qn][:, half * 4:half * 4 + 4, :], in_=ps[:, :].rearrange("p (h s) -> p h s", h=4), func=AF.Copy),
                                 reads=[ps], writes=[Qp[qn]])
                    for half in range(2):
                        ps = bank()
                        for hh in range(4):
                            h8 = half * 4 + hh
                            P.op("pe", lambda e, ps=ps, hh=hh, h8=h8, qc=qc, ncur=ncur: e.matmul(ps[:, hh * 128:(hh + 1) * 128], lhsT=Qp[qc][:, h8, :], rhs=Np[ncur][:, h8, :], start=True, stop=True),
                                 reads=[Np[ncur], Qp[qc]], writes=[ps])
                        P.op("act", lambda e, ps=ps, half=half, nn=nn: e.activation(out=Np[nn][:, half * 4:half * 4 + 4, :], in_=ps[:, :].rearrange("p (h s) -> p h s", h=4), func=AF.Copy),
                             reads=[ps], writes=[Np[nn]])
                    for half in range(2):
                        ps = bank()
                        for hh in range(4):
                            h8 = half * 4 + hh
                            P.op("pe", lambda e, ps=ps, hh=hh, h8=h8, nn=nn, sc=sc: e.matmul(ps[:, hh * 128:(hh + 1) * 128], lhsT=Np[nn][:, h8, :], rhs=Sb[sc][:, h8, :], start=True, stop=True),
                                 reads=[Np[nn], Sb[sc]], writes=[ps])
                        P.op("dve", lambda e, ps=ps, half=half, sc=sc, sn=sn: e.tensor_tensor(out=Sb[sn][:, half * 4:half * 4 + 4, :], in0=ps[:, :].rearrange("p (h s) -> p h s", h=4),
                                                                                           in1=Sb[sc][:, half * 4:half * 4 + 4, :], op=ALU.add), reads=[ps, Sb[sc]], writes=[Sb[sn]])
                    qc, ncur, sc = qn, nn, sn
                P.mute = ("rwchain" in SKIP) or ("rc_xuy" in SKIP)
                Sf = Sb[sc]
                vtok = lambda j, hp: tokm[:, tt, j, 2, hp * 64:hp * 64 + 64]
                nat4 = lambda buf, hp: buf[:, :].rearrange("p (j h v) -> p j h v", j=4, h=2)[:, :, hp, :]
                psx = [bank(), bank()]
                for h8 in range(8):
                    j, hp = h8 // 2, h8 % 2
                    hidx = hp * 4 + j
                    bp = slice(hp * 64, hp * 64 + 64)
                    ps = psx[hp]
                    P.op("pe", lambda e, ps=ps, j=j, bp=bp: e.matmul(ps[:, j * 64:(j + 1) * 64], lhsT=AR[bp, j, tt, 0, :], rhs=Hb[j][bp, :], start=True, stop=False), reads=[AR, Hb[j]], writes=[ps])
                    P.op("pe", lambda e, ps=ps, j=j, hp=hp, hidx=hidx: e.matmul(ps[:, j * 64:(j + 1) * 64], lhsT=AKm[:, hidx, :], rhs=vtok(j, hp), start=False, stop=True), reads=[AKm, tokm], writes=[ps])
                for hp in range(2):
                    P.op("act", lambda e, hp=hp: e.activation(out=nat4(Xs, hp), in_=psx[hp][:, 0:256].rearrange("p (j v) -> p j v", j=4), func=AF.Copy), reads=[psx[hp]], writes=[Xs])
                ps = bank()
                for h8 in range(8):
                    j, hp = h8 // 2, h8 % 2
                    hidx = hp * 4 + j
                    P.op("pe", lambda e, ps=ps, h8=h8, hidx=hidx: e.matmul(ps[:, h8 * 64:(h8 + 1) * 64], lhsT=Sf[:, hidx, :], rhs=Xs[:, h8 * 64:(h8 + 1) * 64], start=True, stop=True), reads=[Sf, Xs], writes=[ps])
                P.op("act", lambda e, ps=ps: e.activation(out=Us[:, :], in_=ps[:, :], func=AF.Copy), reads=[ps], writes=[Us])
                psy = [bank(), bank()]
                for h8 in range(8):
                    j, hp = h8 // 2, h8 % 2
                    hidx = hp * 4 + j
                    bp = slice(hp * 64, hp * 64 + 64)
                    ps = psy[hp]
                    P.op("pe", lambda e, ps=ps, j=j, bp=bp: e.matmul(ps[:, j * 64:(j + 1) * 64], lhsT=AR[bp, j, tt, 1, :], rhs=Hb[j][bp, :], start=True, stop=False), reads=[AR, Hb[j]], writes=[ps])
                    P.op("pe", lambda e, ps=ps, j=j, h8=h8, hidx=hidx: e.matmul(ps[:, j * 64:(j + 1) * 64], lhsT=ABm[:, hidx, :], rhs=Us[:, h8 * 64:(h8 + 1) * 64], start=False, stop=False), reads=[ABm, Us], writes=[ps])
                    P.op("pe", lambda e, ps=ps, j=j, hp=hp, hidx=hidx: e.matmul(ps[:, j * 64:(j + 1) * 64], lhsT=RKm[:, hidx, :], rhs=vtok(j, hp), start=False, stop=True), reads=[RKm, tokm], writes=[ps])
                for hp in range(2):
                    P.op("act", lambda e, hp=hp: e.activation(out=nat4(Ysb, hp), in_=psy[hp][:, 0:256].rearrange("p (j v) -> p j v", j=4), func=AF.Copy), reads=[psy[hp]], writes=[Ysb])
                P.mute = ("rwchain" in SKIP) or ("rc_state" in SKIP)
                ps = bank()
                for j in range(4):
                    P.op("pe", lambda e, ps=ps, j=j: e.matmul(ps[:, j * 128:(j + 1) * 128], lhsT=tokm[:, tt, j, 0, :], rhs=Us[:, j * 128:(j + 1) * 128], start=True, stop=False), reads=[tokm, Us], writes=[ps])
                    P.op("pe", lambda e, ps=ps, j=j: e.matmul(ps[:, j * 128:(j + 1) * 128], lhsT=tokm[:, tt, j, 1, :], rhs=tokm[:, tt, j, 2, :], start=False, stop=True), reads=[tokm], writes=[ps])
                for j in range(4):
                    for hp in range(2):
                        bp = slice(hp * 64, hp * 64 + 64)
                        P.op("dve", lambda e, ps=ps, j=j, hp=hp, bp=bp: e.scalar_tensor_tensor(out=Hs32[j][bp, :], in0=ps[bp, j * 128 + hp * 64:j * 128 + hp * 64 + 64], scalar=eet[bp, j, tt:tt + 1],
                                                                                           in1=Hpe[j][bp, :], op0=ALU.mult, op1=ALU.add), reads=[ps, eet, Hpe[j]], writes=[Hs32[j]])
                P.mute = ("rwchain" in SKIP) or ("rc_gn" in SKIP)
                y3 = lambda buf: buf[:, :].rearrange("p (h v) -> p h v", h=8)
                bc8 = lambda ap: ap.unsqueeze(2).broadcast_to([128, 8, 64])
                P.op("dve", lambda e: e.tensor_reduce(out=yst[:, 0:8], in_=y3(Ysb), axis=AX.X, op=ALU.add), reads=[Ysb], writes=[yst])
                P.op("act", lambda e: e.activation(out=ysq[:, :], in_=Ysb[:, :], func=AF.Square), reads=[Ysb], writes=[ysq])
                P.op("dve", lambda e: e.tensor_reduce(out=yst[:, 8:16], in_=y3(ysq), axis=AX.X, op=ALU.add), reads=[ysq], writes=[yst])
                P.op("dve", lambda e: e.tensor_scalar(out=yst[:, 16:24], in0=yst[:, 0:8], scalar1=1.0 / 64, scalar2=None, op0=ALU.mult), reads=[yst], writes=[yst])
                P.op("dve", lambda e: e.tensor_tensor(out=yst[:, 24:32], in0=yst[:, 16:24], in1=yst[:, 16:24], op=ALU.mult), reads=[yst], writes=[yst])
                P.op("dve", lambda e: e.scalar_tensor_tensor(out=yst[:, 32:40], in0=yst[:, 8:16], scalar=1.0 / 64, in1=yst[:, 24:32], op0=ALU.mult, op1=ALU.subtract), reads=[yst], writes=[yst])
                P.op("act", lambda e: e.activation(out=yst[:, 24:32], in_=yst[:, 32:40], func=AF.Ln, bias=epsg[:, 0:1]), reads=[yst, epsg], writes=[yst])
                P.op("act", lambda e: e.activation(out=yst[:, 32:40], in_=yst[:, 24:32], func=AF.Exp, scale=-0.5), reads=[yst], writes=[yst])
                P.op("dve", lambda e: e.tensor_tensor(out=y3(yc), in0=y3(Ysb), in1=bc8(yst[:, 16:24]), op=ALU.subtract), reads=[Ysb, yst], writes=[yc])
                P.op("dve", lambda e: e.tensor_tensor(out=y3(yc), in0=y3(yc), in1=bc8(yst[:, 32:40]), op=ALU.mult), reads=[yc, yst], writes=[yc])
                P.op("pool", lambda e: e.tensor_tensor(out=yc[:, :], in0=yc[:, :], in1=pp[:, PP_LNW:PP_LNW + 512], op=ALU.mult), reads=[yc, pp], writes=[yc])
                P.op("pool", lambda e: e.tensor_tensor(out=ynb[:, :], in0=yc[:, :], in1=pp[:, PP_LNB:PP_LNB + 512], op=ALU.add), reads=[yc, pp], writes=[ynb])
                for j in range(4):
                    P.op("pe", lambda e, j=j: e.transpose(pbT[:, j * 128:(j + 1) * 128], ynb[:, j * 128:(j + 1) * 128], ident_bf()), reads=[ynb, cstb], writes=[pbT])
                P.op("dve", lambda e: e.tensor_tensor(out=yc[:, :].rearrange("p (j s) -> p j s", j=4), in0=pbT[:, 0:512].rearrange("p (j s) -> p j s", j=4), in1=bonT[:, :, tsl2], op=ALU.add),
                     reads=[pbT, bonT], writes=[yc])
                P.op("dve", lambda e: e.tensor_tensor(out=brR[:, :, tsl2], in0=yc[:, :].rearrange("p (j s) -> p j s", j=4), in1=gT[:, :, tsl2], op=ALU.mult), reads=[yc, gT], writes=[brR])

            P.mute = ("merge" in SKIP)
            brs = [brP, brA, brR]
            for nbi in range(3):
                P.dma("sp", "wbr", lambda e, nbi=nbi: e.dma_start(out=wbr[:, :, :], in_=wbr_bf[l][nbi * 512:(nbi + 1) * 512, :].rearrange("(c p) d -> p c d", p=128)),
                      reads=[DR("cv%d" % l)], writes=[wbr])
                for dm in range(8):
                    ps = bank()
                    for c in range(4):
                        P.op("pe", lambda e, ps=ps, c=c, dm=dm, nbi=nbi: e.matmul(ps[:, :nb_], lhsT=wbr[:, c, dm * 128:(dm + 1) * 128], rhs=brs[nbi][:, c, :nb_], start=(c == 0), stop=(c == 3)),
                             reads=[wbr, brs[nbi]], writes=[ps])
                    if nbi == 0:
                        P.op("dve", lambda e, ps=ps, dm=dm, nbi=nbi: e.tensor_tensor(out=merged[:, dm, :nb_], in0=ps[:, :nb_], in1=gates[:, nbi * 8 + dm, :nb_], op=ALU.mult),
                             reads=[ps, gates], writes=[merged])
                    else:
                        mt = mtmp[dm % 2]
                        P.op("dve", lambda e, ps=ps, dm=dm, nbi=nbi, mt=mt: e.tensor_tensor(out=mt[:, :nb_], in0=ps[:, :nb_], in1=gates[:, nbi * 8 + dm, :nb_], op=ALU.mult),
                             reads=[ps, gates], writes=[mt])
                        P.op("pool", lambda e, dm=dm, mt=mt: e.tensor_tensor(out=merged[:, dm, :nb_], in0=merged[:, dm, :nb_], in1=mt[:, :nb_], op=ALU.add),
                             reads=[merged, mt], writes=[merged])
            for gg in range(2):
                w = load_wg(wout_bf[l], gg * 512, 512, "cv%d" % l)
                for m in range(4):
                    dm = gg * 4 + m
                    ps = bank()
                    for c in range(8):
                        P.op("pe", lambda e, ps=ps, c=c, m=m, w=w: e.matmul(ps[:, :nb_], lhsT=w[:, c, m * 128:(m + 1) * 128], rhs=merged[:, c, :nb_], start=(c == 0), stop=(c == 7)),
                             reads=[w, merged], writes=[ps])
                    P.op("dve", lambda e, ps=ps, dm=dm: e.tensor_tensor(out=xt[:, dm, :nb_], in0=xt[:, dm, :nb_], in1=ps[:, :nb_], op=ALU.add), reads=[xt, ps], writes=[xt])
            if b == 0:
                P.op("pool", lambda e: e.memset(xt[:, :, 0:NPAD], 0.0), writes=[xt])

            P.mute = ("ffn" in SKIP)
            norm_stage(PP_GFFN, nb_)
            for gg in range(11):
                w = load_wg(up_bf[l], gg * 512, 512, "cv%d" % l)
                for m in range(4):
                    mi = gg * 4 + m
                    ps = bank()
                    proj_fm(w, m, nb_, ps)
                    cw = lambda i, mi=mi: pp[:, PP_CONV + i * 44 + mi:PP_CONV + i * 44 + mi + 1]
                    cv = scr[mi % 4]
                    P.op("act", lambda e, ps=ps, cv=cv, cw=cw: e.activation(out=cv[:, :nb_], in_=ps[:, :nb_], func=AF.Identity, scale=cw(2)), reads=[ps, pp], writes=[cv])
                    P.op("dve", lambda e, ps=ps, cv=cv, cw=cw: e.scalar_tensor_tensor(out=cv[:, 1:nb_], in0=ps[:, 0:nb_ - 1], scalar=cw(1), in1=cv[:, 1:nb_], op0=ALU.mult, op1=ALU.add),
                         reads=[ps, pp, cv], writes=[cv])
                    P.op("dve", lambda e, ps=ps, cv=cv, cw=cw: e.scalar_tensor_tensor(out=cv[:, 2:nb_], in0=ps[:, 0:nb_ - 2], scalar=cw(0), in1=cv[:, 2:nb_], op0=ALU.mult, op1=ALU.add),
                         reads=[ps, pp, cv], writes=[cv])
                    P.op("dve", lambda e, cv=cv, cw=cw, mi=mi: e.scalar_tensor_tensor(out=cv[:, 0:1], in0=chalo[:, mi, 1:2], scalar=cw(1), in1=cv[:, 0:1], op0=ALU.mult, op1=ALU.add),
                         reads=[chalo, pp, cv], writes=[cv])
                    P.op("dve", lambda e, cv=cv, cw=cw, mi=mi: e.scalar_tensor_tensor(out=cv[:, 0:2], in0=chalo[:, mi, 0:2], scalar=cw(0), in1=cv[:, 0:2], op0=ALU.mult, op1=ALU.add),
                         reads=[chalo, pp, cv], writes=[cv])
                    P.op("dve", lambda e, ps=ps, mi=mi: e.tensor_copy(out=chalo[:, mi, :], in_=ps[:, nb_ - 2:nb_]), reads=[ps], writes=[chalo])
                    if mi < 22:
                        P.op("act", lambda e, cv=cv, mi=mi: e.activation(out=cgs[:, mi, :nb_], in_=cv[:, :nb_], func=AF.Silu), reads=[cv], writes=[cgs])
                    else:
                        P.op("pool", lambda e, cv=cv, mi=mi: e.tensor_tensor(out=fact[:, mi - 22, :nb_], in0=cgs[:, mi - 22, :nb_], in1=cv[:, :nb_], op=ALU.mult), reads=[cgs, cv], writes=[fact])
            for dm in range(8):
                dw = dnw[dm % 2]
                P.dma("sp", dw.res.name, lambda e, dw=dw, dm=dm: e.dma_start(out=dw[:, :, :], in_=dn_bf[l][:, dm * 128:(dm + 1) * 128].rearrange("(c p) d -> p c d", p=128)),
                      reads=[DR("cv%d" % l)], writes=[dw])
                ps = bank()
                for c in range(22):
                    P.op("pe", lambda e, ps=ps, c=c, dw=dw: e.matmul(ps[:, :nb_], lhsT=dw[:, c, :], rhs=fact[:, c, :nb_], start=(c == 0), stop=(c == 21)),
                         reads=[dw, fact], writes=[ps])
                P.op("dve", lambda e, ps=ps, dm=dm: e.tensor_tensor(out=xt[:, dm, :nb_], in0=xt[:, dm, :nb_], in1=ps[:, :nb_], op=ALU.add), reads=[xt, ps], writes=[xt])
            if b == 0:
                P.op("pool", lambda e: e.memset(xt[:, :, 0:NPAD], 0.0), writes=[xt])
            P.mute = False
            if not last:
                P.dma("pool", "xst", lambda e: e.dma_start(out=xdst[:, tsl].rearrange("(c p) t -> p c t", p=128), in_=xt[:, :, :nb_]),
                      reads=[xt], writes=[DR("xs%d_%d" % (l + 1, b))])
            else:
                P.dma("pool", "xst", lambda e: e.dma_start(out=xTo[:, tsl].rearrange("(c p) t -> p c t", p=128), in_=xt[:, :, :nb_]),
                      reads=[xt], writes=[DR("xo")])
                norm_stage(PP_GFIN, nb_)
                for c in range(8):
                    P.op("dve", lambda e, c=c: e.scalar_tensor_tensor(out=xt[:, c, :nb_], in0=xt[:, c, :nb_], scalar=pp[:, PP_GFIN + c:PP_GFIN + c + 1], in1=rstd[:, :nb_],
                                                                      op0=ALU.mult, op1=ALU.mult), reads=[xt, pp, rstd], writes=[xt])
                P.dma("pool", "xst", lambda e: e.dma_start(out=outT[:, tsl].rearrange("(c p) t -> p c t", p=128), in_=xt[:, :, :nb_]),
                      reads=[xt], writes=[DR("out")])

    for l in range(NL):
        src = xT_in if l == 0 else xbufs[l % 2]
        layer(l, src, xbufs[(l + 1) % 2], last=(l == NL - 1))
    if os.environ.get("KDBG"):
        print("op counts", {e: len(P.q[e]) for e in P.ENG}, "sems", len(P.sems))
    P.emit(final_res=[DR("out"), DR("xo")])
    return nc


def _host_consts(Lp):
    cst = np.zeros((128, NCS), np.float32)
    idx = np.arange(128)
    cst[:, CS_ID:CS_ID + 128] = np.eye(128, dtype=np.float32)
    cst[:, CS_MSU:CS_MSU + 128] = (idx[:, None] < idx[None, :])
    cst[:, CS_MIU:CS_MIU + 128] = (idx[:, None] <= idx[None, :])
    cst[:, CS_MSL:CS_MSL + 128] = (idx[:, None] > idx[None, :])
    cst[:, CS_BO:CS_BO + 128] = ((idx[:, None] // 64) == (idx[None, :] // 64))
    perm = np.zeros((128, 128), np.float32)
    for m in range(128):
        d = m % 64
        if d < 8:
            perm[m + 8, m] = 1.0
        elif d < 16:
            perm[m - 8, m] = 1.0
    cst[:, CS_PERM:CS_PERM + 128] = perm
    cst[:, CS_DM:CS_DM + 128] = ((idx[:, None] // 64) <= (idx[None, :] // 64))
    for g, w in enumerate((2, 4, 8, 16)):
        p = idx - NPAD
        cnt = np.where(p >= 0, np.minimum(p + 1, w), w).astype(np.float32)
        cst[:, CS_IC + g * 128:CS_IC + (g + 1) * 128] = (1.0 / cnt)[None, :]
    pos = (np.arange(Lp) - NPAD).astype(np.float32)
    inv = (np.float32(500000.0) ** (-np.arange(0, 16, 2, dtype=np.float32) / np.float32(16))).astype(np.float32)
    ang = (pos[:, None] * inv[None, :]).astype(np.float32)
    cos = np.cos(ang).astype(np.float32).T
    sin = np.sin(ang).astype(np.float32).T
    rc = np.ones((128, Lp), np.float32)
    rs = np.zeros((128, Lp), np.float32)
    for p in range(128):
        d = p % 64
        if d < 8:
            rc[p] = cos[d]
            rs[p] = -sin[d]
        elif d < 16:
            rc[p] = cos[d - 8]
            rs[p] = sin[d - 8]
    return cst, rc, rs


def _pack_pp(l, norm_mix, norm_ffn, norm_final, pool_scale, rw_mu, rw_w0, rw_a0, rw_k_k, rw_k_a, rw_r_k,
             ffn_conv, da_subln, rw_lnx_w, rw_lnx_b, da_lambda):
    pp = np.zeros((128, NPP), np.float32)
    col = lambda v: np.asarray(v, np.float32).reshape(-1, 128).T
    pp[:, PP_GMIX:PP_GMIX + 8] = col(norm_mix[l])
    pp[:, PP_GFFN:PP_GFFN + 8] = col(norm_ffn[l])
    pp[:, PP_PSC:PP_PSC + 4] = col(pool_scale[l])
    pp[:, PP_MU:PP_MU + 14] = col(rw_mu[l])
    pp[:, PP_W0:PP_W0 + 4] = col(rw_w0[l])
    pp[:, PP_A0:PP_A0 + 4] = col(rw_a0[l])
    pp[:, PP_KK:PP_KK + 4] = col(rw_k_k[l])
    pp[:, PP_KA:PP_KA + 4] = col(rw_k_a[l])
    pp[:, PP_RK:PP_RK + 4] = col(rw_r_k[l].reshape(-1))
    for i in range(3):
        pp[:, PP_CONV + i * 44:PP_CONV + (i + 1) * 44] = col(ffn_conv[l, i])
    pp[:, PP_SUBLN:PP_SUBLN + 128] = np.asarray(da_subln[l], np.float32)[None, :]
    pp[:, PP_LNW:PP_LNW + 512] = np.asarray(rw_lnx_w[l], np.float32)[None, :]
    pp[:, PP_LNB:PP_LNB + 512] = np.asarray(rw_lnx_b[l], np.float32)[None, :]
    pp[:, PP_LAM:PP_LAM + 256] = np.asarray(da_lambda[l], np.float32).reshape(1, 256)
    pp[:, PP_GFIN:PP_GFIN + 8] = col(norm_final)
    lam_init = 0.8 - 0.6 * math.exp(-0.3 * l)
    pp[:, PP_OML] = 1.0 - lam_init
    pp[:, PP_NLI] = -lam_init
    return pp


_NC_CACHE = {}


def _run(xT_list, NT, layers, lam_ids, weights, cst, rc, rs):
    key = (NT, len(layers))
    if key not in _NC_CACHE:
        _NC_CACHE[key] = build(NT, len(layers))
    nc = _NC_CACHE[key]
    W = weights
    ls = list(layers)
    f = lambda a: np.ascontiguousarray(np.asarray(a, np.float32))
    shared = {
        "w_in": f(W["w_in"][ls]),
        "w_branch": f(W["w_branch"][ls].reshape(len(ls), 1536, D)),
        "w_out": f(W["w_out"][ls]),
        "ffn_up": f(W["ffn_up"][ls]),
        "ffn_down": f(W["ffn_down"][ls]),
        "pool_w": f(W["pool_w"][ls]),
        "rw_w2": f(W["rw_w2"][ls]),
        "rw_a2": f(W["rw_a2"][ls]),
        "rw_g2": f(W["rw_g2"][ls]),
        "pp": f(np.stack([_pack_pp(l, W["norm_mix"], W["norm_ffn"], W["norm_final"], W["pool_scale"], W["rw_mu"], W["rw_w0"], W["rw_a0"],
                                   W["rw_k_k"], W["rw_k_a"], W["rw_r_k"], W["ffn_conv"], W["da_subln"], W["rw_lnx_w"], W["rw_lnx_b"],
                                   W["da_lambda"]) for l in ls])),
        "cst": cst, "ropeC": rc, "ropeS": rs,
    }
    in_maps = []
    for xT in xT_list:
        m = dict(shared)
        m["xT"] = xT
        in_maps.append(m)
    res = run_bass_kernel_spmd(nc, in_maps, core_ids=list(range(len(in_maps))))
    return [(r["outT"], r["xTo"]) for r in res.results]


def kernel(x, meta_tokens, **W):
    x = np.asarray(x, np.float32)
    B, Lq, _ = x.shape
    L = NPAD + NMETA + Lq
    NT = (L + 127) // 128
    Lp = NT * 128
    cst, rc, rs = _host_consts(Lp)
    W = {k: np.asarray(v) for k, v in W.items()}
    NL = W["w_in"].shape[0]
    xTs = []
    for c in range(8):
        b = c % B
        xp = np.zeros((Lp, D), np.float32)
        xp[NPAD:NPAD + NMETA] = np.asarray(meta_tokens, np.float32)
        xp[NPAD + NMETA:NPAD + NMETA + Lq] = x[b]
        xTs.append(np.ascontiguousarray(xp.T))
    if FUSED:
        outs = _run(xTs, NT, list(range(NL)), list(range(NL)), W, cst, rc, rs)
    else:
        for l in range(NL):
            outs = _run(xTs, NT, [l], [l], W, cst, rc, rs)
            xTs = [np.ascontiguousarray(o[1]) for o in outs]
    out = np.stack([np.ascontiguousarray(outs[b][0].T[NPAD + NMETA:NPAD + NMETA + Lq]) for b in range(B)])
    return out.astype(np.float32)
```

```python
import math
import os
import numpy as np
import concourse.bass as bass
import concourse.mybir as mybir
from concourse.bass_utils import run_bass_kernel_spmd

F32 = mybir.dt.float32
BF16 = mybir.dt.bfloat16
AF = mybir.ActivationFunctionType
ALU = mybir.AluOpType
AX = mybir.AxisListType

D = 1024
INW = 6912
DFF = 2816
NPAD = 112
NMETA = 16
SEQ = 8192
C1 = -math.exp(-0.5)
SAME_ENGINE_SYNC = os.environ.get("KSES", "1") == "1"
FUSED = True
SKIP = set(os.environ.get("KSKIP", "").split(","))

PP_GMIX, PP_GFFN, PP_PSC, PP_MU, PP_W0, PP_A0, PP_KK, PP_KA, PP_RK = 0, 8, 16, 20, 34, 38, 42, 46, 50
PP_CONV = 54
PP_SUBLN = 186
PP_LNW = 314
PP_LNB = 826
PP_LAM = 1338
PP_GFIN = 1594
PP_OML = 1602
PP_NLI = 1603
NPP = 1604
CS_ID, CS_MSU, CS_MIU, CS_MSL, CS_BO, CS_PERM, CS_DM, CS_IC = 0, 128, 256, 384, 512, 640, 768, 896
NCS = 896 + 512


class Res:
    __slots__ = ("name", "w", "r")

    def __init__(self, name):
        self.name = name
        self.w = None
        self.r = {}


class Buf:
    def __init__(self, t, name):
        self.t = t
        self.res = Res(name)

    def __getitem__(self, k):
        return self.t[k]


def _res(x):
    return x.res if isinstance(x, Buf) else x


class _Rec:
    def __init__(self):
        self.call = None

    def __getattr__(self, name):
        def f(*a, **kw):
            self.call = (name, a, kw)
            return None
        return f


class Prog:
    ENG = ["pe", "act", "dve", "pool", "sp"]

    def __init__(self, nc):
        self.nc = nc
        self.q = {e: [] for e in self.ENG}
        self.cnt = {e: 0 for e in self.ENG}
        self.sems = {}
        self.dcnt = {}
        self.waited = {e: {} for e in self.ENG}
        self.mute = False
        for e in self.ENG:
            self.sems[e] = nc.alloc_semaphore("sem_" + e)

    def _deps(self, eng, reads, writes):
        deps = {}

        def add(ev):
            if ev is None:
                return
            k, v = ev
            if deps.get(k, 0) < v:
                deps[k] = v
        for r in reads:
            add(r.w)
        for w in writes:
            add(w.w)
            for k, v in w.r.items():
                add((k, v))
        out = []
        for k, v in deps.items():
            if k == eng and (eng == "pe" or not SAME_ENGINE_SYNC):
                continue
            if self.waited[eng].get(k, 0) >= v:
                continue
            self.waited[eng][k] = v
            out.append((k, v))
        return out

    def _mark(self, ev, reads, writes):
        k, v = ev
        for r in reads:
            if r.r.get(k, 0) < v:
                r.r[k] = v
        for w in writes:
            w.w = ev
            w.r = {}

    def op(self, eng, fn, reads=(), writes=()):
        if self.mute:
            return
        rec = _Rec()
        fn(rec)
        fn = rec.call
        reads = [_res(x) for x in reads]
        writes = [_res(x) for x in writes]
        waits = self._deps(eng, reads, writes)
        self.cnt[eng] += 1
        self.q[eng].append((waits, fn, (eng, 1)))
        self._mark((eng, self.cnt[eng]), reads, writes)

    def dma(self, qeng, key, fn, reads=(), writes=()):
        if self.mute:
            return
        rec = _Rec()
        fn(rec)
        fn = rec.call
        reads = [_res(x) for x in reads]
        writes = [_res(x) for x in writes]
        waits = self._deps(qeng, reads, writes)
        if key not in self.sems:
            self.sems[key] = self.nc.alloc_semaphore("dsem_" + key)
            self.dcnt[key] = 0
        self.dcnt[key] += 16
        self.q[qeng].append((waits, fn, (key, 16)))
        self._mark((key, self.dcnt[key]), reads, writes)

    def emit(self, final_res=()):
        nc = self.nc
        fin = {}
        for r in final_res:
            r = _res(r)
            if r.w is not None:
                k, v = r.w
                fin[k] = max(fin.get(k, 0), v)
        with nc.Block() as block:
            def mk(e):
                def body(eng):
                    for waits, fn, (k, amt) in self.q[e]:
                        for (wk, wv) in waits:
                            eng.wait_ge(self.sems[wk], wv)
                        name, a, kw = fn
                        getattr(eng, name)(*a, **kw).then_inc(self.sems[k], amt)
                    if e == "sp":
                        for wk, wv in fin.items():
                            eng.wait_ge(self.sems[wk], wv)
                return body
            block.tensor(mk("pe"))
            block.scalar(mk("act"))
            block.vector(mk("dve"))
            block.gpsimd(mk("pool"))
            block.sync(mk("sp"))


def build(NT, NL, NTB=2, lam_inits=None, dbg=False):
    nc = bass.Bass("TRN2", target_bir_lowering=False)
    Lp = NT * 128
    P = Prog(nc)

    def dram_in(name, shape, dt=F32):
        return nc.dram_tensor(name, list(shape), dt, kind="ExternalInput").ap()

    def dram_tmp(name, shape, dt):
        return nc.dram_tensor(name, list(shape), dt, kind="Internal").ap()

    xT_in = dram_in("xT", [D, Lp])
    w_in = dram_in("w_in", [NL, D, INW])
    w_branch = dram_in("w_branch", [NL, 1536, D])
    w_out = dram_in("w_out", [NL, D, D])
    ffn_up = dram_in("ffn_up", [NL, D, 2 * DFF])
    ffn_down = dram_in("ffn_down", [NL, DFF, D])
    pool_w = dram_in("pool_w", [NL, 4, 128, 128])
    rw_w2 = dram_in("rw_w2", [NL, 64, 512])
    rw_a2 = dram_in("rw_a2", [NL, 64, 512])
    rw_g2 = dram_in("rw_g2", [NL, 128, 512])
    pp_in = dram_in("pp", [NL, 128, NPP])
    cst_in = dram_in("cst", [128, NCS])
    ropeC = dram_in("ropeC", [128, Lp])
    ropeS = dram_in("ropeS", [128, Lp])
    outT = nc.dram_tensor("outT", [D, Lp], F32, kind="ExternalOutput").ap()
    xTo = nc.dram_tensor("xTo", [D, Lp], F32, kind="ExternalOutput").ap()

    win_bf = dram_tmp("win_bf", [NL, D, INW], BF16)
    wbr_bf = dram_tmp("wbr_bf", [NL, 1536, D], BF16)
    wout_bf = dram_tmp("wout_bf", [NL, D, D], BF16)
    up_bf = dram_tmp("up_bf", [NL, D, 2 * DFF], BF16)
    dn_bf = dram_tmp("dn_bf", [NL, DFF, D], BF16)
    xbufs = [dram_tmp("xT_a", [D, Lp], F32), dram_tmp("xT_b", [D, Lp], F32)]
    kT_hist = dram_tmp("kT_hist", [512, Lp], BF16)
    v_hist = dram_tmp("v_hist", [Lp, 516], BF16)
    dres = {}

    def DR(key):
        if key not in dres:
            dres[key] = Res(key)
        return dres[key]

    def S(name, shape, dt):
        return Buf(nc.alloc_sbuf_tensor("s_" + name, list(shape), dt), name)

    n = NTB * 128
    pb = [Buf(nc.alloc_psum_tensor("pb%d" % i, [128, 512], F32), "pb%d" % i) for i in range(7)]
    pbT = Buf(nc.alloc_psum_tensor("pbT", [128, 1024], BF16), "pbT")
    rr = [0]

    ALLB = [0, 1, 2, 3, 4, 5, 6]
    ATTB = [0, 1, 2]
    AUXB = ALLB
    cur_banks = [ALLB]

    def bank(lst=None):
        lst = lst or cur_banks[0]
        rr[0] += 1
        return pb[lst[rr[0] % len(lst)]]

    cst = S("cst", [128, NCS], F32)
    cstb = S("cstb", [128, 896], BF16)
    ident2 = S("ident2", [128, 2, 128], BF16)
    ones_bf = S("ones_bf", [128, 128], BF16)
    rmask = S("rmask", [128, n], F32)
    eps6 = S("eps6", [128, 1], F32)
    eps5 = S("eps5", [128, 1], F32)
    epsg = S("epsg", [128, 1], F32)
    eps18 = S("eps18", [128, 1], F32)
    P.dma("sp", "cst", lambda e: e.dma_start(out=cst[:, :], in_=cst_in[:, :]), writes=[cst])
    P.op("dve", lambda e: e.tensor_copy(out=cstb[:, :], in_=cst[:, 0:896]), reads=[cst], writes=[cstb])
    P.op("pool", lambda e: e.tensor_copy(out=ident2[:, 0, :], in_=cst[:, CS_ID:CS_ID + 128]), reads=[cst], writes=[ident2])
    P.op("pool", lambda e: e.tensor_copy(out=ident2[:, 1, :], in_=cst[:, CS_ID:CS_ID + 128]), reads=[cst], writes=[ident2])
    P.op("pool", lambda e: e.memset(ones_bf[:, :], 1.0), writes=[ones_bf])
    P.op("pool", lambda e: e.memset(rmask[:, :], 1.0), writes=[rmask])
    for tt in range(NTB):
        P.op("pool", lambda e, tt=tt: e.memset(rmask[:, tt * 128:tt * 128 + 1], 0.0), writes=[rmask])
    P.op("pool", lambda e: e.memset(eps6[:, :], 1e-6), writes=[eps6])
    P.op("pool", lambda e: e.memset(eps5[:, :], 1e-5), writes=[eps5])
    P.op("pool", lambda e: e.memset(epsg[:, :], 64e-5), writes=[epsg])
    P.op("pool", lambda e: e.memset(eps18[:, :], 1e-18), writes=[eps18])
    ident_bf = lambda: cstb[:, CS_ID:CS_ID + 128]
    msu = lambda: cst[:, CS_MSU:CS_MSU + 128]
    miu = lambda: cst[:, CS_MIU:CS_MIU + 128]
    msl = lambda: cst[:, CS_MSL:CS_MSL + 128]
    bones_bf = lambda: cstb[:, CS_BO:CS_BO + 128]
    perm_bf = lambda: cstb[:, CS_PERM:CS_PERM + 128]
    dmask_bf = lambda: cstb[:, CS_DM:CS_DM + 128]

    def conv_chunks(l):
        ch = []
        for src, dst, rows in ((w_in[l], win_bf[l], D), (w_branch[l], wbr_bf[l], 1536), (w_out[l], wout_bf[l], D),
                               (ffn_up[l], up_bf[l], D), (ffn_down[l], dn_bf[l], DFF)):
            for r0 in range(0, rows, 128):
                ch.append((src, dst, r0, min(rows, r0 + 128)))
        return ch

    def issue_conv(l, chunks):
        for (src, dst, r0, r1) in chunks:
            P.dma("pool", "cv%d" % l, lambda e, src=src, dst=dst, r0=r0, r1=r1: e.dma_start(out=dst[r0:r1, :], in_=src[r0:r1, :]),
                  writes=[DR("cv%d" % l)])

    issue_conv(0, conv_chunks(0))
    P.mute = False
    pp = S("pp", [128, NPP], F32)
    poolw_bf = S("poolw_bf", [128, 4, 128], BF16)
    w2_bf = S("w2_bf", [128, 512], BF16)
    g2_bf = S("g2_bf", [128, 512], BF16)
    rkones = S("rkones", [128, 4, 128], BF16)
    omka = S("omka", [128, 4], F32)
    neglam = S("neglam", [128, 1], F32)
    lamt = S("lamt", [128, 8], F32)
    sublnw = S("sublnw", [128, 128], F32)

    xt = S("xt", [128, 8, n], F32)
    sq = S("sq", [128, 8, n], BF16)
    hT = S("hT", [128, 8, n], BF16)
    lnv = S("lnv", [128, n], F32)
    rstd = S("rstd", [128, n], F32)
    wg = [S("wg%d" % i, [128, 8, 512], BF16) for i in range(2)]
    gates = S("gates", [128, 24, n], BF16)
    brP = S("brP", [128, 4, n], BF16)
    brA = S("brA", [128, 4, n], BF16)
    brR = S("brR", [128, 4, n], BF16)
    qT = S("qT", [128, 4, n], BF16)
    kTb = S("kTb", [128, 4, n], BF16)
    vaug = S("vaug", [128, NTB, 4, 129], BF16)
    rC = S("rC", [128, n], F32)
    rS = S("rS", [128, n], F32)
    qraw = [S("qraw%d" % i, [128, n], BF16) for i in range(2)]
    scr = [S("scr%d" % i, [128, n], F32) for i in range(8)]
    pu = S("pu", [128, 4, 16 + n], F32)
    pta = S("pta", [128, 16 + n], F32)
    ptb = S("ptb", [128, 16 + n], F32)
    pooled = S("pooled", [128, 4, n], BF16)
    rp = S("rp", [128, 12, n], BF16)
    ltmp = [S("ltmp%d" % i, [128, 1 + n], F32) for i in range(2)]
    ldt = [S("ldt%d" % i, [128, n], F32) for i in range(2)]
    lora = S("lora", [128, 2, n], F32)
    rhalo = S("rhalo", [128, 14], F32)
    tw_bf = S("tw_bf", [128, n], BF16)
    sg_bf = S("sg_bf", [128, n], BF16)
    kk2_bf = S("kk2_bf", [128, n], BF16)
    rk_bf = S("rk_bf", [128, n], BF16)
    AR = S("AR", [128, 4, NTB, 2, 128], BF16)
    kt_bf = S("kt_bf", [128, 4, n], BF16)
    bt_bf = S("bt_bf", [128, 4, n], BF16)
    tokm = S("tokm", [128, NTB, 4, 3, 128], BF16)
    gT = S("gT", [128, 4, n], BF16)
    bonT = S("bonT", [128, 4, n], BF16)
    emt = S("emt", [128, 4, NTB], F32)
    eet = S("eet", [128, 4, NTB], F32)
    emet = S("emet", [128, 4, NTB], F32)
    Qp = [S("Qp%d" % i, [128, 8, 128], BF16) for i in range(2)]
    Np = [S("Np%d" % i, [128, 8, 128], BF16) for i in range(2)]
    Sb = [S("Sb%d" % i, [128, 8, 128], BF16) for i in range(2)]
    ABm = S("ABm", [128, 8, 128], BF16)
    AKm = S("AKm", [128, 8, 128], BF16)
    RKm = S("RKm", [128, 8, 128], BF16)
    Xs = S("Xs", [128, 512], BF16)
    Us = S("Us", [128, 512], BF16)
    Ysb = S("Ysb", [128, 512], F32)
    ysq = S("ysq", [128, 512], F32)
    yc = S("yc", [128, 512], F32)
    ynb = S("ynb", [128, 512], BF16)
    yst = S("yst", [128, 40], F32)
    Hs32 = [S("Hs32_%d" % j, [128, 64], F32) for j in range(4)]
    Hb = [S("Hb_%d" % j, [128, 64], BF16) for j in range(4)]
    Hpe = [S("Hpe_%d" % j, [128, 64], F32) for j in range(4)]
    kTk = [S("kTk%d" % i, [128, n], BF16) for i in range(2)]
    vk = [S("vk%d" % i, [128, NTB, 129], BF16) for i in range(2)]
    ET = [S("ET%d" % i, [128, n], BF16) for i in range(3)]
    ao = S("ao", [128, 128], F32)
    at = S("at", [128, 128], F32)
    aj = S("aj", [128, 128], F32)
    aon = S("aon", [128, 128], BF16)
    ast = S("ast", [128, 8], F32)
    merged = S("merged", [128, 8, n], BF16)
    mtmp = [S("mtmp%d" % i, [128, n], BF16) for i in range(2)]
    wbr = S("wbr", [128, 4, 1024], BF16)
    fact = S("fact", [128, 22, n], BF16)
    dnw = [S("dnw%d" % i, [128, 22, 128], BF16) for i in range(2)]
    chalo = S("chalo", [128, 44, 2], F32)
    cgs = gates
    invc = lambda g: cst[:, CS_IC + g * 128:CS_IC + (g + 1) * 128]

    slot_res = {}

    def oslot(c, qi):
        idx = c * NTB + qi
        bk = 3 + idx
        return pb[bk], 0, pb[bk].res
    assert 2 * NTB <= 4

    nblocks = (NT + NTB - 1) // NTB

    def norm_stage(gcol, nb_):
        P.op("act", lambda e: e.activation(out=sq[:, :, :nb_], in_=xt[:, :, :nb_], func=AF.Square), reads=[xt], writes=[sq])
        ps = bank()
        for c in range(8):
            P.op("pe", lambda e, c=c: e.matmul(ps[:, :nb_], lhsT=ones_bf[:, :], rhs=sq[:, c, :nb_], start=(c == 0), stop=(c == 7)),
                 reads=[ones_bf, sq], writes=[ps])
        P.op("act", lambda e: e.activation(out=lnv[:, :nb_], in_=ps[:, :nb_], func=AF.Ln, bias=eps6[:, 0:1], scale=1.0 / D),
             reads=[ps, eps6], writes=[lnv])
        P.op("act", lambda e: e.activation(out=rstd[:, :nb_], in_=lnv[:, :nb_], func=AF.Exp, scale=-0.5), reads=[lnv], writes=[rstd])
        for c in range(8):
            P.op("dve", lambda e, c=c: e.scalar_tensor_tensor(out=hT[:, c, :nb_], in0=xt[:, c, :nb_], scalar=pp[:, gcol + c:gcol + c + 1],
                                                              in1=rstd[:, :nb_], op0=ALU.mult, op1=ALU.mult),
                 reads=[xt, pp, rstd], writes=[hT])

    wgi = [0]

    def load_wg(src, c0, gc, key):
        wgi[0] += 1
        w = wg[wgi[0] % 2]
        P.dma("sp", w.res.name, lambda e: e.dma_start(out=w[:, :, :gc], in_=src[:, c0:c0 + gc].rearrange("(c p) m -> p c m", p=128)),
              reads=[DR(key)], writes=[w])
        return w

    def proj_fm(w, ml, nb_, ps):
        for c in range(8):
            P.op("pe", lambda e, c=c: e.matmul(ps[:, :nb_], lhsT=w[:, c, ml * 128:(ml + 1) * 128], rhs=hT[:, c, :nb_], start=(c == 0), stop=(c == 7)),
                 reads=[w, hT], writes=[ps])

    def layer(l, xsrc, xdst, last):
        P.dma("sp", "pp", lambda e: e.dma_start(out=pp[:, :], in_=pp_in[l]), writes=[pp])
        P.dma("pool", "poolw", lambda e: e.dma_start(out=poolw_bf[:, :, :], in_=pool_w[l].rearrange("g c d -> c g d")), writes=[poolw_bf])
        P.dma("pool", "w2", lambda e: e.dma_start(out=w2_bf[0:64, :], in_=rw_w2[l]), writes=[w2_bf])
        P.dma("pool", "w2", lambda e: e.dma_start(out=w2_bf[64:128, :], in_=rw_a2[l]), writes=[w2_bf])
        P.dma("pool", "g2", lambda e: e.dma_start(out=g2_bf[:, :], in_=rw_g2[l]), writes=[g2_bf])
        for j in range(4):
            P.op("dve", lambda e, j=j: e.tensor_scalar(out=rkones[:, j, :], in0=cst[:, CS_BO:CS_BO + 128], scalar1=pp[:, PP_RK + j:PP_RK + j + 1],
                                                      scalar2=None, op0=ALU.mult), reads=[cst, pp], writes=[rkones])
        P.op("dve", lambda e: e.tensor_scalar(out=omka[:, :], in0=pp[:, PP_KA:PP_KA + 4], scalar1=-1.0, scalar2=1.0, op0=ALU.mult, op1=ALU.add),
             reads=[pp], writes=[omka])
        P.op("dve", lambda e: e.tensor_scalar(out=sublnw[:, :], in0=pp[:, PP_SUBLN:PP_SUBLN + 128], scalar1=pp[:, PP_OML:PP_OML + 1], scalar2=None, op0=ALU.mult),
             reads=[pp], writes=[sublnw])
        P.op("dve", lambda e: e.tensor_tensor(out=aj[:, 0:64], in0=pp[:, PP_LAM:PP_LAM + 64], in1=pp[:, PP_LAM + 64:PP_LAM + 128], op=ALU.mult), reads=[pp], writes=[aj])
        P.op("dve", lambda e: e.tensor_tensor(out=aj[:, 64:128], in0=pp[:, PP_LAM + 128:PP_LAM + 192], in1=pp[:, PP_LAM + 192:PP_LAM + 256], op=ALU.mult), reads=[pp], writes=[aj])
        P.op("dve", lambda e: e.tensor_reduce(out=lamt[:, 0:2], in_=aj[:, :].rearrange("p (a b) -> p a b", b=64), axis=AX.X, op=ALU.add), reads=[aj], writes=[lamt])
        P.op("act", lambda e: e.activation(out=lamt[:, 2:4], in_=lamt[:, 0:2], func=AF.Exp), reads=[lamt], writes=[lamt])
        P.op("dve", lambda e: e.tensor_tensor(out=lamt[:, 4:5], in0=lamt[:, 3:4], in1=lamt[:, 2:3], op=ALU.subtract), reads=[lamt], writes=[lamt])
        P.op("dve", lambda e: e.tensor_scalar(out=neglam[:, :], in0=lamt[:, 4:5], scalar1=pp[:, PP_NLI:PP_NLI + 1], scalar2=None, op0=ALU.add), reads=[lamt, pp], writes=[neglam])
        P.op("pool", lambda e: e.memset(rhalo[:, :], 0.0), writes=[rhalo])
        P.op("pool", lambda e: e.memset(pu[:, :, 0:16], 0.0), writes=[pu])
        P.op("pool", lambda e: e.memset(chalo[:, :, :], 0.0), writes=[chalo])
        for j in range(4):
            P.op("pool", lambda e, j=j: e.memset(Hs32[j][:, :], 0.0), writes=[Hs32[j]])

        WIN = win_bf[l]
        nxt = conv_chunks(l + 1) if l + 1 < NL else []
        per_blk = (len(nxt) + nblocks - 2) // max(1, nblocks - 1) if nxt else 0
        for b in range(nblocks):
            if nxt:
                issue_conv(l + 1, nxt[b * per_blk:(b + 1) * per_blk])
            t0 = b * n
            nt = min(NTB, NT - b * NTB)
            nb_ = nt * 128
            tsl = slice(t0, t0 + nb_)
            xin_key = "x%d_%d_%d" % (l, 0, b)
            P.dma("sp", "xt", lambda e: e.dma_start(out=xt[:, :, :nb_], in_=xsrc[:, tsl].rearrange("(c p) t -> p c t", p=128)),
                  reads=[DR("xs%d_%d" % (l, b))], writes=[xt])
            P.dma("sp", "rC", lambda e: e.dma_start(out=rC[:, :nb_], in_=ropeC[:, tsl]), writes=[rC])
            P.dma("sp", "rS", lambda e: e.dma_start(out=rS[:, :nb_], in_=ropeS[:, tsl]), writes=[rS])
            norm_stage(PP_GMIX, nb_)

            P.mute = ("pool" in SKIP)
            w = load_wg(WIN, 0, 512, "cv%d" % l)
            for g in range(4):
                ps = bank()
                proj_fm(w, g, nb_, ps)
                P.op("act", lambda e, g=g, ps=ps: e.activation(out=pu[:, g, 16:16 + nb_], in_=ps[:, :nb_], func=AF.Copy), reads=[ps], writes=[pu])
                src = pu
                bufs = [pta, ptb]
                cur = None
                for lev in range(g + 1):
                    sh = 1 << lev
                    lo = 2 * sh - 1
                    dst = bufs[lev % 2]
                    if lev == 0:
                        P.op("dve", lambda e, dst=dst, g=g, lo=lo, sh=sh: e.tensor_tensor(out=dst[:, lo:16 + nb_], in0=pu[:, g, lo:16 + nb_], in1=pu[:, g, lo - sh:16 + nb_ - sh], op=ALU.add),
                             reads=[pu], writes=[dst])
                    else:
                        P.op("dve", lambda e, dst=dst, cur=cur, lo=lo, sh=sh: e.tensor_tensor(out=dst[:, lo:16 + nb_], in0=cur[:, lo:16 + nb_], in1=cur[:, lo - sh:16 + nb_ - sh], op=ALU.add),
                             reads=[cur], writes=[dst])
                    cur = dst
                wv = float(2 << g)
                P.op("dve", lambda e, g=g, cur=cur, wv=wv: e.scalar_tensor_tensor(out=pooled[:, g, :nb_], in0=cur[:, 16:16 + nb_], scalar=1.0 / wv, in1=pu[:, g, 16:16 + nb_],
                                                                                  op0=ALU.mult, op1=ALU.subtract), reads=[cur, pu], writes=[pooled])
                if b == 0:
                    P.op("dve", lambda e, g=g, cur=cur: e.tensor_tensor(out=cur[:, 16:144], in0=cur[:, 16:144], in1=invc(g), op=ALU.mult), reads=[cur, cst], writes=[cur])
                    P.op("dve", lambda e, g=g, cur=cur: e.tensor_tensor(out=pooled[:, g, 0:128], in0=cur[:, 16:144], in1=pu[:, g, 16:144], op=ALU.subtract),
                         reads=[cur, pu], writes=[pooled])
                P.op("pool", lambda e, g=g: e.tensor_copy(out=pu[:, g, 0:16], in_=pu[:, g, nb_:nb_ + 16]), reads=[pu], writes=[pu])
                ps2 = bank(AUXB)
                P.op("pe", lambda e, g=g, ps2=ps2: e.matmul(ps2[:, :nb_], lhsT=poolw_bf[:, g, :], rhs=pooled[:, g, :nb_], start=True, stop=True),
                     reads=[poolw_bf, pooled], writes=[ps2])
                P.op("act", lambda e, g=g, ps2=ps2: e.activation(out=brP[:, g, :nb_], in_=ps2[:, :nb_], func=AF.Identity, scale=pp[:, PP_PSC + g:PP_PSC + g + 1]),
                     reads=[ps2, pp], writes=[brP])

            P.mute = ("qk" in SKIP)
            for which in range(2):
                w = load_wg(WIN, 512 + which * 512, 512, "cv%d" % l)
                dstb = qT if which == 0 else kTb
                for m in range(4):
                    ps = bank()
                    proj_fm(w, m, nb_, ps)
                    qr = qraw[m % 2]
                    P.op("act", lambda e, ps=ps, qr=qr: e.activation(out=qr[:, :nb_], in_=ps[:, :nb_], func=AF.Copy), reads=[ps], writes=[qr])
                    ps2 = bank(AUXB)
                    P.op("pe", lambda e, ps2=ps2, qr=qr: e.matmul(ps2[:, :nb_], lhsT=perm_bf(), rhs=qr[:, :nb_], start=True, stop=True), reads=[cstb, qr], writes=[ps2])
                    s1 = scr[(2 * m) % 8]
                    s2 = scr[(2 * m + 1) % 8]
                    P.op("dve", lambda e, qr=qr, s1=s1: e.tensor_tensor(out=s1[:, :nb_], in0=qr[:, :nb_], in1=rC[:, :nb_], op=ALU.mult), reads=[qr, rC], writes=[s1])
                    P.op("dve", lambda e, ps2=ps2, s2=s2: e.tensor_tensor(out=s2[:, :nb_], in0=ps2[:, :nb_], in1=rS[:, :nb_], op=ALU.mult), reads=[ps2, rS], writes=[s2])
                    P.op("pool", lambda e, s1=s1, s2=s2, m=m, dstb=dstb: e.tensor_tensor(out=dstb[:, m, :nb_], in0=s1[:, :nb_], in1=s2[:, :nb_], op=ALU.add),
                         reads=[s1, s2], writes=[dstb])
            P.dma("pool", "kst", lambda e: e.dma_start(out=kT_hist[:, tsl].rearrange("(h p) t -> p h t", p=128), in_=kTb[:, :, :nb_]),
                  reads=[kTb], writes=[DR("kh%d" % b)])

            P.mute = ("v" in SKIP)
            w = load_wg(WIN, 1536, 512, "cv%d" % l)
            P.op("pool", lambda e: e.memset(vaug[:, :, :, 128:129], 1.0), writes=[vaug])
            if b == 0:
                P.op("pool", lambda e: e.memset(vaug[0:NPAD, 0, :, 128:129], 0.0), writes=[vaug])
            for tt in range(nt):
                ps = bank()
                for c in range(8):
                    P.op("pe", lambda e, c=c, ps=ps, tt=tt, w=w: e.matmul(ps[:, :], lhsT=hT[:, c, tt * 128:(tt + 1) * 128], rhs=w[:, c, :], start=(c == 0), stop=(c == 7)),
                         reads=[hT, w], writes=[ps])
                P.op("act", lambda e, ps=ps, tt=tt: e.activation(out=vaug[:, tt, :, 0:128], in_=ps[:, :].rearrange("p (h e) -> p h e", h=4), func=AF.Copy),
                     reads=[ps], writes=[vaug])
            P.dma("pool", "vst", lambda e: e.dma_start(out=v_hist[tsl, :].rearrange("(t p) f -> p t f", p=128), in_=vaug[:, :nt, :, :].rearrange("p t h e -> p t (h e)")),
                  reads=[vaug], writes=[DR("vh%d" % b)])

            P.mute = ("rwproj" in SKIP)
            def lerp_tile(ps, mi, out_ap_fn, outbuf):
                lt = ltmp[mi % 2]
                ld = ldt[mi % 2]
                P.op("pool", lambda e: e.tensor_copy(out=lt[:, 0:1], in_=rhalo[:, mi:mi + 1]), reads=[rhalo], writes=[lt])
                P.op("act", lambda e: e.activation(out=lt[:, 1:1 + nb_], in_=ps[:, :nb_], func=AF.Copy), reads=[ps], writes=[lt])
                P.op("pool", lambda e: e.tensor_copy(out=rhalo[:, mi:mi + 1], in_=lt[:, nb_:nb_ + 1]), reads=[lt], writes=[rhalo])
                P.op("dve", lambda e: e.tensor_tensor(out=ld[:, :nb_], in0=lt[:, 0:nb_], in1=lt[:, 1:1 + nb_], op=ALU.subtract), reads=[lt], writes=[ld])
                P.op("dve", lambda e: e.scalar_tensor_tensor(out=out_ap_fn(), in0=ld[:, :nb_], scalar=pp[:, PP_MU + mi:PP_MU + mi + 1], in1=lt[:, 1:1 + nb_],
                                                             op0=ALU.mult, op1=ALU.add), reads=[ld, lt, pp], writes=[outbuf])
            w = load_wg(WIN, 3584, 256, "cv%d" % l)
            for m in range(2):
                ps = bank()
                proj_fm(w, m, nb_, ps)
                lerp_tile(ps, 12 + m, lambda m=m: lora[:, m, :nb_], lora)
            P.op("act", lambda e: e.activation(out=tw_bf[0:64, :nb_], in_=lora[0:64, 0, :nb_], func=AF.Tanh), reads=[lora], writes=[tw_bf])
            P.op("act", lambda e: e.activation(out=tw_bf[64:128, :nb_], in_=lora[64:128, 0, :nb_], func=AF.Copy), reads=[lora], writes=[tw_bf])
            P.op("act", lambda e: e.activation(out=sg_bf[:, :nb_], in_=lora[:, 1, :nb_], func=AF.Sigmoid), reads=[lora], writes=[sg_bf])
            for which in range(3):
                w = load_wg(WIN, 2048 + which * 512, 512, "cv%d" % l)
                for m in range(4):
                    ps = bank()
                    proj_fm(w, m, nb_, ps)
                    mi = which * 4 + m
                    lerp_tile(ps, mi, lambda mi=mi: rp[:, mi, :nb_], rp)

            P.mute = ("gates" in SKIP)
            for gg in range(6):
                w = load_wg(WIN, 3840 + gg * 512, 512, "cv%d" % l)
                for m in range(4):
                    ps = bank()
                    proj_fm(w, m, nb_, ps)
                    gi = gg * 4 + m
                    P.op("act", lambda e, ps=ps, gi=gi: e.activation(out=gates[:, gi, :nb_], in_=ps[:, :nb_], func=AF.Sigmoid), reads=[ps], writes=[gates])

            P.mute = ("attn" in SKIP)
            cur_banks[0] = ATTB
            ei = 0
            for h in range(4):
                for kb in range(b + 1):
                    nkt = min(NTB, NT - kb * NTB)
                    nk = nkt * 128
                    kk_ = kTk[(h * 64 + kb) % 2]
                    vv = vk[(h * 64 + kb) % 2]
                    ksl = slice(kb * n, kb * n + nk)
                    P.dma("sp", kk_.res.name, lambda e, kk_=kk_, ksl=ksl, nk=nk: e.dma_start(out=kk_[:, :nk], in_=kT_hist[h * 128:(h + 1) * 128, ksl]),
                          reads=[DR("kh%d" % kb)], writes=[kk_])
                    P.dma("sp", vv.res.name, lambda e, vv=vv, ksl=ksl, nkt=nkt: e.dma_start(out=vv[:, :nkt, :], in_=v_hist[ksl, h * 129:(h + 1) * 129].rearrange("(t p) f -> p t f", p=128)),
                          reads=[DR("vh%d" % kb)], writes=[vv])
                    for jj in range(nkt):
                        jg = kb * NTB + jj
                        q0 = 0 if kb < b else jj
                        ncol = (nt - q0) * 128
                        if ncol <= 0:
                            continue
                        for c in range(2):
                            st = bank()
                            et = ET[ei % 3]
                            ei += 1
                            P.op("pe", lambda e, st=st, kk_=kk_, c=c, jj=jj, q0=q0, ncol=ncol: e.matmul(st[:, :ncol], lhsT=kk_[c * 64:(c + 1) * 64, jj * 128:(jj + 1) * 128],
                                                                                                    rhs=qT[c * 64:(c + 1) * 64, h, q0 * 128:q0 * 128 + ncol], start=True, stop=True),
                                 reads=[kk_, qT], writes=[st])
                            P.op("act", lambda e, st=st, et=et, ncol=ncol: e.activation(out=et[:, :ncol], in_=st[:, :ncol], func=AF.Exp, scale=0.125), reads=[st], writes=[et])
                            if kb == b and jg > 0:
                                P.op("pool", lambda e, et=et: e.tensor_tensor(out=et[:, 0:128], in0=et[:, 0:128], in1=dmask_bf(), op=ALU.mult), reads=[et, cstb], writes=[et])
                            for qi in range(q0, nt):
                                ob, off, ores = oslot(c, qi)
                                ig = b * NTB + qi
                                P.op("pe", lambda e, ob=ob, off=off, et=et, qi=qi, q0=q0, vv=vv, jj=jj, jg=jg, ig=ig: e.matmul(
                                    ob[:, off:off + 129], lhsT=et[:, (qi - q0) * 128:(qi - q0 + 1) * 128], rhs=vv[:, jj, :], start=(jg == 0), stop=(jg == ig)),
                                    reads=[et, vv], writes=[ores])
                        if kb == b:
                            qi = jj
                            o0b, o0off, o0r = oslot(0, qi)
                            o1b, o1off, o1r = oslot(1, qi)
                            P.op("dve", lambda e, o0b=o0b, o0off=o0off: e.reciprocal(out=ast[:, 0:1], in_=o0b[:, o0off + 128:o0off + 129]), reads=[o0r], writes=[ast])
                            P.op("dve", lambda e, o1b=o1b, o1off=o1off: e.reciprocal(out=ast[:, 1:2], in_=o1b[:, o1off + 128:o1off + 129]), reads=[o1r], writes=[ast])
                            P.op("dve", lambda e: e.tensor_tensor(out=ast[:, 2:3], in0=ast[:, 1:2], in1=neglam[:, 0:1], op=ALU.mult), reads=[ast, neglam], writes=[ast])
                            P.op("dve", lambda e, o1b=o1b, o1off=o1off: e.tensor_scalar(out=at[:, :], in0=o1b[:, o1off:o1off + 128], scalar1=ast[:, 2:3], scalar2=None, op0=ALU.mult),
                                 reads=[o1r, ast], writes=[at])
                            P.op("dve", lambda e, o0b=o0b, o0off=o0off: e.scalar_tensor_tensor(out=ao[:, :], in0=o0b[:, o0off:o0off + 128], scalar=ast[:, 0:1], in1=at[:, :],
                                                                                              op0=ALU.mult, op1=ALU.add), reads=[o0r, ast, at], writes=[ao])
                            P.op("act", lambda e: e.activation(out=aj[:, :], in_=ao[:, :], func=AF.Square, accum_out=ast[:, 3:4]), reads=[ao], writes=[aj, ast])
                            P.op("act", lambda e: e.activation(out=ast[:, 4:5], in_=ast[:, 3:4], func=AF.Ln, bias=eps5[:, 0:1], scale=1.0 / 128), reads=[ast, eps5], writes=[ast])
                            P.op("act", lambda e: e.activation(out=ast[:, 5:6], in_=ast[:, 4:5], func=AF.Exp, scale=-0.5), reads=[ast], writes=[ast])
                            P.op("dve", lambda e: e.scalar_tensor_tensor(out=aon[:, :], in0=ao[:, :], scalar=ast[:, 5:6], in1=sublnw[:, :], op0=ALU.mult, op1=ALU.mult),
                                 reads=[ao, ast, sublnw], writes=[aon])
                            P.op("pe", lambda e: e.transpose(pbT[:, 0:128], aon[:, :], ident_bf()), reads=[aon, cstb], writes=[pbT])
                            P.op("act", lambda e, qi=qi: e.activation(out=brA[:, h, qi * 128:(qi + 1) * 128], in_=pbT[:, 0:128], func=AF.Copy), reads=[pbT], writes=[brA])

            P.mute = ("rwprep" in SKIP)
            cur_banks[0] = ALLB
            A_, B_, C_, D_, E_, F_, G_ = scr[0], scr[1], scr[2], scr[3], scr[4], scr[5], scr[6]
            v3 = lambda buf: buf[:, :nb_].rearrange("p (t s) -> p t s", s=128)
            for j in range(4):
                jc = slice(j * 128, (j + 1) * 128)
                r_ap = lambda: rp[:, j, :nb_]
                k_ap = lambda: rp[:, 4 + j, :nb_]
                v_ap = lambda: rp[:, 8 + j, :nb_]
                ps = bank()
                P.op("pe", lambda e, ps=ps, jc=jc: e.matmul(ps[:, :nb_], lhsT=w2_bf[0:64, jc], rhs=tw_bf[0:64, :nb_], start=True, stop=True), reads=[w2_bf, tw_bf], writes=[ps])
                P.op("act", lambda e, ps=ps, j=j: e.activation(out=A_[:, :nb_], in_=ps[:, :nb_], func=AF.Sigmoid, bias=pp[:, PP_W0 + j:PP_W0 + j + 1]), reads=[ps, pp], writes=[A_])
                ps = bank()
                P.op("pe", lambda e, ps=ps, jc=jc: e.matmul(ps[:, :nb_], lhsT=w2_bf[64:128, jc], rhs=tw_bf[64:128, :nb_], start=True, stop=True), reads=[w2_bf, tw_bf], writes=[ps])
                P.op("act", lambda e, ps=ps, j=j: e.activation(out=B_[:, :nb_], in_=ps[:, :nb_], func=AF.Sigmoid, bias=pp[:, PP_A0 + j:PP_A0 + j + 1]), reads=[ps, pp], writes=[B_])
                ps = bank()
                P.op("pe", lambda e, ps=ps, jc=jc: e.matmul(ps[:, :nb_], lhsT=g2_bf[:, jc], rhs=sg_bf[:, :nb_], start=True, stop=True), reads=[g2_bf, sg_bf], writes=[ps])
                P.op("act", lambda e, ps=ps, j=j: e.activation(out=gT[:, j, :nb_], in_=ps[:, :nb_], func=AF.Copy), reads=[ps], writes=[gT])
                P.op("dve", lambda e: e.tensor_tensor_scan(out=C_[:, :nb_], data0=rmask[:, :nb_], data1=A_[:, :nb_], initial=0.0, op0=ALU.mult, op1=ALU.add),
                     reads=[rmask, A_], writes=[C_])
                P.op("dve", lambda e: e.tensor_tensor(out=D_[:, :nb_], in0=C_[:, :nb_], in1=A_[:, :nb_], op=ALU.subtract), reads=[C_, A_], writes=[D_])
                P.op("dve", lambda e: e.tensor_tensor(out=v3(A_), in0=v3(C_), in1=v3(C_)[:, :, 63:64].broadcast_to([128, nt, 128]), op=ALU.subtract), reads=[C_], writes=[A_])
                P.op("dve", lambda e: e.tensor_tensor(out=v3(E_), in0=v3(D_), in1=v3(C_)[:, :, 63:64].broadcast_to([128, nt, 128]), op=ALU.subtract), reads=[C_, D_], writes=[E_])
                P.op("act", lambda e, j=j: e.activation(out=emt[:, j, :nt], in_=v3(C_)[:, :, 63], func=AF.Exp, scale=C1), reads=[C_], writes=[emt])
                P.op("act", lambda e, j=j: e.activation(out=eet[:, j, :nt], in_=v3(A_)[:, :, 127], func=AF.Exp, scale=C1), reads=[A_], writes=[eet])
                P.op("act", lambda e, j=j: e.activation(out=emet[:, j, :nt], in_=v3(C_)[:, :, 127], func=AF.Exp, scale=C1), reads=[C_], writes=[emet])
                P.op("act", lambda e: e.activation(out=D_[:, :nb_], in_=A_[:, :nb_], func=AF.Exp, scale=C1), reads=[A_], writes=[D_])
                P.op("act", lambda e: e.activation(out=F_[:, :nb_], in_=A_[:, :nb_], func=AF.Exp, scale=-C1), reads=[A_], writes=[F_])
                P.op("act", lambda e: e.activation(out=G_[:, :nb_], in_=E_[:, :nb_], func=AF.Exp, scale=C1), reads=[E_], writes=[G_])
                P.op("act", lambda e, j=j: e.activation(out=kk2_bf[:, :nb_], in_=k_ap(), func=AF.Square, scale=pp[:, PP_KK + j:PP_KK + j + 1]), reads=[rp, pp], writes=[kk2_bf])
                ps = bank()
                P.op("pe", lambda e, ps=ps: e.matmul(ps[:, :nb_], lhsT=bones_bf(), rhs=kk2_bf[:, :nb_], start=True, stop=True), reads=[cstb, kk2_bf], writes=[ps])
                P.op("act", lambda e, ps=ps: e.activation(out=C_[:, :nb_], in_=ps[:, :nb_], func=AF.Ln, bias=eps18[:, 0:1]), reads=[ps, eps18], writes=[C_])
                P.op("act", lambda e: e.activation(out=C_[:, :nb_], in_=C_[:, :nb_], func=AF.Exp, scale=-0.5), reads=[C_], writes=[C_])
                P.op("dve", lambda e, j=j: e.scalar_tensor_tensor(out=A_[:, :nb_], in0=k_ap(), scalar=pp[:, PP_KK + j:PP_KK + j + 1], in1=C_[:, :nb_], op0=ALU.mult, op1=ALU.mult),
                     reads=[rp, pp, C_], writes=[A_])
                P.op("dve", lambda e, j=j: e.tensor_scalar(out=E_[:, :nb_], in0=B_[:, :nb_], scalar1=pp[:, PP_KA + j:PP_KA + j + 1], scalar2=omka[:, j:j + 1], op0=ALU.mult, op1=ALU.add),
                     reads=[B_, pp, omka], writes=[E_])
                P.op("dve", lambda e: e.tensor_tensor(out=E_[:, :nb_], in0=E_[:, :nb_], in1=k_ap(), op=ALU.mult), reads=[E_, rp], writes=[E_])
                P.op("dve", lambda e: e.tensor_tensor(out=C_[:, :nb_], in0=A_[:, :nb_], in1=B_[:, :nb_], op=ALU.mult), reads=[A_, B_], writes=[C_])
                P.op("dve", lambda e, j=j: e.scalar_tensor_tensor(out=AR[:, j, :nt, 0, :], in0=v3(A_), scalar=-1.0, in1=v3(G_), op0=ALU.mult, op1=ALU.mult), reads=[A_, G_], writes=[AR])
                P.op("dve", lambda e, j=j: e.tensor_tensor(out=AR[:, j, :nt, 1, :], in0=rp[:, j, :nb_].rearrange("p (t s) -> p t s", s=128), in1=v3(D_), op=ALU.mult), reads=[rp, D_], writes=[AR])
                P.op("dve", lambda e, j=j: e.tensor_tensor(out=kt_bf[:, j, :nb_], in0=E_[:, :nb_], in1=F_[:, :nb_], op=ALU.mult), reads=[E_, F_], writes=[kt_bf])
                P.op("dve", lambda e, j=j: e.tensor_tensor(out=bt_bf[:, j, :nb_], in0=C_[:, :nb_], in1=F_[:, :nb_], op=ALU.mult), reads=[C_, F_], writes=[bt_bf])
                P.op("dve", lambda e: e.tensor_tensor(out=rk_bf[:, :nb_], in0=r_ap(), in1=E_[:, :nb_], op=ALU.mult), reads=[rp, E_], writes=[rk_bf])
                ps = bank()
                P.op("pe", lambda e, ps=ps, j=j: e.matmul(ps[:, :nb_], lhsT=rkones[:, j, :], rhs=rk_bf[:, :nb_], start=True, stop=True), reads=[rkones, rk_bf], writes=[ps])
                P.op("dve", lambda e, ps=ps, j=j: e.tensor_tensor(out=bonT[:, j, :nb_], in0=ps[:, :nb_], in1=v_ap(), op=ALU.mult), reads=[ps, rp], writes=[bonT])
                for tt in range(nt):
                    tsl2 = slice(tt * 128, (tt + 1) * 128)
                    P.op("pe", lambda e, j=j, tsl2=tsl2: e.transpose(pbT[:, 0:128], bt_bf[:, j, tsl2], ident_bf()), reads=[bt_bf, cstb], writes=[pbT])
                    P.op("pe", lambda e, j=j, tsl2=tsl2: e.transpose(pbT[:, 128:256], kt_bf[:, j, tsl2], ident_bf()), reads=[kt_bf, cstb], writes=[pbT])
                    P.op("pe", lambda e, j=j, tsl2=tsl2: e.transpose(pbT[:, 256:384], rp[:, 8 + j, tsl2], ident_bf()), reads=[rp, cstb], writes=[pbT])
                    P.op("act", lambda e, j=j, tt=tt: e.activation(out=tokm[:, tt, j, :, :], in_=pbT[:, 0:384].rearrange("p (a b) -> p a b", a=3), func=AF.Copy), reads=[pbT], writes=[tokm])

            P.mute = ("rwchain" in SKIP)
            for tt in range(nt):
                tsl2 = slice(tt * 128, (tt + 1) * 128)
                for j in range(4):
                    P.op("dve", lambda e, j=j, tt=tt: e.tensor_scalar(out=Hb[j][:, :], in0=Hs32[j][:, :], scalar1=emt[:, j, tt:tt + 1], scalar2=None, op0=ALU.mult), reads=[Hs32[j], emt], writes=[Hb[j]])
                    P.op("pool", lambda e, j=j, tt=tt: e.tensor_scalar(out=Hpe[j][:, :], in0=Hs32[j][:, :], scalar1=emet[:, j, tt:tt + 1], scalar2=None, op0=ALU.mult), reads=[Hs32[j], emet], writes=[Hpe[j]])
                P.mute = ("rwchain" in SKIP) or ("rc_gram" in SKIP)
                def gram(lhs_buf, lhs_fn, rhs_fn, rhs_bufs, mask_fn, dst, eng):
                    for half in range(2):
                        ps = bank()
                        for hh in range(4):
                            j, hp = hh, half
                            bp = slice(hp * 64, hp * 64 + 64)
                            P.op("pe", lambda e, ps=ps, hh=hh, j=j, bp=bp: e.matmul(ps[:, hh * 128:(hh + 1) * 128], lhsT=lhs_fn(j, bp), rhs=rhs_fn(j, bp), start=True, stop=True),
                                 reads=lhs_buf + rhs_bufs, writes=[ps])
                        P.op(eng, lambda e, ps=ps, half=half: e.tensor_tensor(out=dst[:, half * 4:half * 4 + 4, :], in0=ps[:, :].rearrange("p (h s) -> p h s", h=4),
                                                                            in1=mask_fn().unsqueeze(1).broadcast_to([128, 4, 128]), op=ALU.mult), reads=[ps, cst], writes=[dst])
                bt_l = lambda j, bp: bt_bf[bp, j, tsl2]
                kt_l = lambda j, bp: kt_bf[bp, j, tsl2]
                a_r = lambda j, bp: AR[bp, j, tt, 0, :]
                r_r = lambda j, bp: AR[bp, j, tt, 1, :]
                gram([bt_bf], bt_l, a_r, [AR], msu, Qp[0], "dve")
                gram([AR], a_r, bt_l, [bt_bf], msl, Np[0], "dve")
                gram([kt_bf], kt_l, a_r, [AR], msu, AKm, "dve")
                gram([bt_bf], bt_l, r_r, [AR], miu, ABm, "dve")
                gram([kt_bf], kt_l, r_r, [AR], miu, RKm, "dve")
                P.mute = ("rwchain" in SKIP) or ("rc_inv" in SKIP)
                for jp in range(4):
                    P.op("pool", lambda e, jp=jp: e.tensor_tensor(out=Sb[0][:, 2 * jp:2 * jp + 2, :], in0=Qp[0][:, 2 * jp:2 * jp + 2, :], in1=ident2[:, :, :], op=ALU.add),
                         reads=[Qp[0], ident2], writes=[Sb[0]])
                qc, ncur, sc = 0, 0, 0
                for lev in range(6):
                    qn, nn, sn = 1 - qc, 1 - ncur, 1 - sc
                    lastlev = (lev == 5)
                    if not lastlev:
                        for half in range(2):
                            ps = bank()
                            for hh in range(4):
                                h8 = half * 4 + hh
                                P.op("pe", lambda e, ps=ps, hh=hh, h8=h8, qc=qc, ncur=ncur: e.matmul(ps[:, hh * 128:(hh + 1) * 128], lhsT=Np[ncur][:, h8, :], rhs=Qp[qc][:, h8, :], start=True, stop=True),
                                     reads=[Np[ncur], Qp[qc]], writes=[ps])
                            P.op("act", lambda e, ps=ps, half=half, qn=qn: e.activation(out=Qp[qn][:, half * 4:half * 4 + 4, :], in_=ps[:, :].rearrange("p (h s) -> p h s", h=4), func=AF.Copy),
                                 reads=[ps], writes=[Qp[qn]])
                    for half in range(2):
                        ps = bank()
                        for hh in range(4):
                            h8 = half * 4 + hh
                            P.op("pe", lambda e, ps=ps, hh=hh, h8=h8, qc=qc, ncur=ncur: e.matmul(ps[:, hh * 128:(hh + 1) * 128], lhsT=Qp[qc][:, h8, :], rhs=Np[ncur][:, h8, :], start=True, stop=True),
                                 reads=[Np[ncur], Qp[qc]], writes=[ps])
                        P.op("act", lambda e, ps=ps, half=half, nn=nn: e.activation(out=Np[nn][:, half * 4:half * 4 + 4, :], in_=ps[:, :].rearrange("p (h s) -> p h s", h=4), func=AF.Copy),
                             reads=[ps], writes=[Np[nn]])
                    for half in range(2):
                        ps = bank()
                        for hh in range(4):
                            h8 = half * 4 + hh
                            P.op("pe", lambda e, ps=ps, hh=hh, h8=h8, nn=nn, sc=sc: e.matmul(ps[:, hh * 128:(hh + 1) * 128], lhsT=Np[nn][:, h8, :], rhs=Sb[sc][:, h8, :], start=True, stop=True),
                                 reads=[Np[nn], Sb[sc]], writes=[ps])
                        P.op("dve", lambda e, ps=ps, half=half, sc=sc, sn=sn: e.tensor_tensor(out=Sb[sn][:, half * 4:half * 4 + 4, :], in0=ps[:, :].rearrange("p (h s) -> p h s", h=4),
                                                                                           in1=Sb[sc][:, half * 4:half * 4 + 4, :], op=ALU.add), reads=[ps, Sb[sc]], writes=[Sb[sn]])
                    qc, ncur, sc = qn, nn, sn
                P.mute = ("rwchain" in SKIP) or ("rc_xuy" in SKIP)
                Sf = Sb[sc]
                vtok = lambda j, hp: tokm[:, tt, j, 2, hp * 64:hp * 64 + 64]
                nat4 = lambda buf, hp: buf[:, :].rearrange("p (j h v) -> p j h v", j=4, h=2)[:, :, hp, :]
                psx = [bank(), bank()]
                for h8 in range(8):
                    j, hp = h8 // 2, h8 % 2
                    hidx = hp * 4 + j
                    bp = slice(hp * 64, hp * 64 + 64)
                    ps = psx[hp]
                    P.op("pe", lambda e, ps=ps, j=j, bp=bp: e.matmul(ps[:, j * 64:(j + 1) * 64], lhsT=AR[bp, j, tt, 0, :], rhs=Hb[j][bp, :], start=True, stop=False), reads=[AR, Hb[j]], writes=[ps])
                    P.op("pe", lambda e, ps=ps, j=j, hp=hp, hidx=hidx: e.matmul(ps[:, j * 64:(j + 1) * 64], lhsT=AKm[:, hidx, :], rhs=vtok(j, hp), start=False, stop=True), reads=[AKm, tokm], writes=[ps])
                for hp in range(2):
                    P.op("act", lambda e, hp=hp: e.activation(out=nat4(Xs, hp), in_=psx[hp][:, 0:256].rearrange("p (j v) -> p j v", j=4), func=AF.Copy), reads=[psx[hp]], writes=[Xs])
                ps = bank()
                for h8 in range(8):
                    j, hp = h8 // 2, h8 % 2
                    hidx = hp * 4 + j
                    P.op("pe", lambda e, ps=ps, h8=h8, hidx=hidx: e.matmul(ps[:, h8 * 64:(h8 + 1) * 64], lhsT=Sf[:, hidx, :], rhs=Xs[:, h8 * 64:(h8 + 1) * 64], start=True, stop=True), reads=[Sf, Xs], writes=[ps])
                P.op("act", lambda e, ps=ps: e.activation(out=Us[:, :], in_=ps[:, :], func=AF.Copy), reads=[ps], writes=[Us])
                psy = [bank(), bank()]
                for h8 in range(8):
                    j, hp = h8 // 2, h8 % 2
                    hidx = hp * 4 + j
                    bp = slice(hp * 64, hp * 64 + 64)
                    ps = psy[hp]
                    P.op("pe", lambda e, ps=ps, j=j, bp=bp: e.matmul(ps[:, j * 64:(j + 1) * 64], lhsT=AR[bp, j, tt, 1, :], rhs=Hb[j][bp, :], start=True, stop=False), reads=[AR, Hb[j]], writes=[ps])
                    P.op("pe", lambda e, ps=ps, j=j, h8=h8, hidx=hidx: e.matmul(ps[:, j * 64:(j + 1) * 64], lhsT=ABm[:, hidx, :], rhs=Us[:, h8 * 64:(h8 + 1) * 64], start=False, stop=False), reads=[ABm, Us], writes=[ps])
                    P.op("pe", lambda e, ps=ps, j=j, hp=hp, hidx=hidx: e.matmul(ps[:, j * 64:(j + 1) * 64], lhsT=RKm[:, hidx, :], rhs=vtok(j, hp), start=False, stop=True), reads=[RKm, tokm], writes=[ps])
                for hp in range(2):
                    P.op("act", lambda e, hp=hp: e.activation(out=nat4(Ysb, hp), in_=psy[hp][:, 0:256].rearrange("p (j v) -> p j v", j=4), func=AF.Copy), reads=[psy[hp]], writes=[Ysb])
                P.mute = ("rwchain" in SKIP) or ("rc_state" in SKIP)
                ps = bank()
                for j in range(4):
                    P.op("pe", lambda e, ps=ps, j=j: e.matmul(ps[:, j * 128:(j + 1) * 128], lhsT=tokm[:, tt, j, 0, :], rhs=Us[:, j * 128:(j + 1) * 128], start=True, stop=False), reads=[tokm, Us], writes=[ps])
                    P.op("pe", lambda e, ps=ps, j=j: e.matmul(ps[:, j * 128:(j + 1) * 128], lhsT=tokm[:, tt, j, 1, :], rhs=tokm[:, tt, j, 2, :], start=False, stop=True), reads=[tokm], writes=[ps])
                for j in range(4):
                    for hp in range(2):
                        bp = slice(hp * 64, hp * 64 + 64)
                        P.op("dve", lambda e, ps=ps, j=j, hp=hp, bp=bp: e.scalar_tensor_tensor(out=Hs32[j][bp, :], in0=ps[bp, j * 128 + hp * 64:j * 128 + hp * 64 + 64], scalar=eet[bp, j, tt:tt + 1],
                                                                                           in1=Hpe[j][bp, :], op0=ALU.mult, op1=ALU.add), reads=[ps, eet, Hpe[j]], writes=[Hs32[j]])
                P.mute = ("rwchain" in SKIP) or ("rc_gn" in SKIP)
                y3 = lambda buf: buf[:, :].rearrange("p (h v) -> p h v", h=8)
                bc8 = lambda ap: ap.unsqueeze(2).broadcast_to([128, 8, 64])
                P.op("dve", lambda e: e.tensor_reduce(out=yst[:, 0:8], in_=y3(Ysb), axis=AX.X, op=ALU.add), reads=[Ysb], writes=[yst])
                P.op("act", lambda e: e.activation(out=ysq[:, :], in_=Ysb[:, :], func=AF.Square), reads=[Ysb], writes=[ysq])
                P.op("dve", lambda e: e.tensor_reduce(out=yst[:, 8:16], in_=y3(ysq), axis=AX.X, op=ALU.add), reads=[ysq], writes=[yst])
                P.op("dve", lambda e: e.tensor_scalar(out=yst[:, 16:24], in0=yst[:, 0:8], scalar1=1.0 / 64, scalar2=None, op0=ALU.mult), reads=[yst], writes=[yst])
                P.op("dve", lambda e: e.tensor_tensor(out=yst[:, 24:32], in0=yst[:, 16:24], in1=yst[:, 16:24], op=ALU.mult), reads=[yst], writes=[yst])
                P.op("dve", lambda e: e.scalar_tensor_tensor(out=yst[:, 32:40], in0=yst[:, 8:16], scalar=1.0 / 64, in1=yst[:, 24:32], op0=ALU.mult, op1=ALU.subtract), reads=[yst], writes=[yst])
                P.op("act", lambda e: e.activation(out=yst[:, 24:32], in_=yst[:, 32:40], func=AF.Ln, bias=epsg[:, 0:1]), reads=[yst, epsg], writes=[yst])
                P.op("act", lambda e: e.activation(out=yst[:, 32:40], in_=yst[:, 24:32], func=AF.Exp, scale=-0.5), reads=[yst], writes=[yst])
                P.op("dve", lambda e: e.tensor_tensor(out=y3(yc), in0=y3(Ysb), in1=bc8(yst[:, 16:24]), op=ALU.subtract), reads=[Ysb, yst], writes=[yc])
                P.op("dve", lambda e: e.tensor_tensor(out=y3(yc), in0=y3(yc), in1=bc8(yst[:, 32:40]), op=ALU.mult), reads=[yc, yst], writes=[yc])
                P.op("pool", lambda e: e.tensor_tensor(out=yc[:, :], in0=yc[:, :], in1=pp[:, PP_LNW:PP_LNW + 512], op=ALU.mult), reads=[yc, pp], writes=[yc])
                P.op("pool", lambda e: e.tensor_tensor(out=ynb[:, :], in0=yc[:, :], in1=pp[:, PP_LNB:PP_LNB + 512], op=ALU.add), reads=[yc, pp], writes=[ynb])
                for j in range(4):
                    P.op("pe", lambda e, j=j: e.transpose(pbT[:, j * 128:(j + 1) * 128], ynb[:, j * 128:(j + 1) * 128], ident_bf()), reads=[ynb, cstb], writes=[pbT])
                P.op("dve", lambda e: e.tensor_tensor(out=yc[:, :].rearrange("p (j s) -> p j s", j=4), in0=pbT[:, 0:512].rearrange("p (j s) -> p j s", j=4), in1=bonT[:, :, tsl2], op=ALU.add),
                     reads=[pbT, bonT], writes=[yc])
                P.op("dve", lambda e: e.tensor_tensor(out=brR[:, :, tsl2], in0=yc[:, :].rearrange("p (j s) -> p j s", j=4), in1=gT[:, :, tsl2], op=ALU.mult), reads=[yc, gT], writes=[brR])

            P.mute = ("merge" in SKIP)
            brs = [brP, brA, brR]
            for nbi in range(3):
                P.dma("sp", "wbr", lambda e, nbi=nbi: e.dma_start(out=wbr[:, :, :], in_=wbr_bf[l][nbi * 512:(nbi + 1) * 512, :].rearrange("(c p) d -> p c d", p=128)),
                      reads=[DR("cv%d" % l)], writes=[wbr])
                for dm in range(8):
                    ps = bank()
                    for c in range(4):
                        P.op("pe", lambda e, ps=ps, c=c, dm=dm, nbi=nbi: e.matmul(ps[:, :nb_], lhsT=wbr[:, c, dm * 128:(dm + 1) * 128], rhs=brs[nbi][:, c, :nb_], start=(c == 0), stop=(c == 3)),
                             reads=[wbr, brs[nbi]], writes=[ps])
                    if nbi == 0:
                        P.op("dve", lambda e, ps=ps, dm=dm, nbi=nbi: e.tensor_tensor(out=merged[:, dm, :nb_], in0=ps[:, :nb_], in1=gates[:, nbi * 8 + dm, :nb_], op=ALU.mult),
                             reads=[ps, gates], writes=[merged])
                    else:
                        mt = mtmp[dm % 2]
                        P.op("dve", lambda e, ps=ps, dm=dm, nbi=nbi, mt=mt: e.tensor_tensor(out=mt[:, :nb_], in0=ps[:, :nb_], in1=gates[:, nbi * 8 + dm, :nb_], op=ALU.mult),
                             reads=[ps, gates], writes=[mt])
                        P.op("pool", lambda e, dm=dm, mt=mt: e.tensor_tensor(out=merged[:, dm, :nb_], in0=merged[:, dm, :nb_], in1=mt[:, :nb_], op=ALU.add),
                             reads=[merged, mt], writes=[merged])
            for gg in range(2):
                w = load_wg(wout_bf[l], gg * 512, 512, "cv%d" % l)
                for m in range(4):
                    dm = gg * 4 + m
                    ps = bank()
                    for c in range(8):
                        P.op("pe", lambda e, ps=ps, c=c, m=m, w=w: e.matmul(ps[:, :nb_], lhsT=w[:, c, m * 128:(m + 1) * 128], rhs=merged[:, c, :nb_], start=(c == 0), stop=(c == 7)),
                             reads=[w, merged], writes=[ps])
                    P.op("dve", lambda e, ps=ps, dm=dm: e.tensor_tensor(out=xt[:, dm, :nb_], in0=xt[:, dm, :nb_], in1=ps[:, :nb_], op=ALU.add), reads=[xt, ps], writes=[xt])
            if b == 0:
                P.op("pool", lambda e: e.memset(xt[:, :, 0:NPAD], 0.0), writes=[xt])

            P.mute = ("ffn" in SKIP)
            norm_stage(PP_GFFN, nb_)
            for gg in range(11):
                w = load_wg(up_bf[l], gg * 512, 512, "cv%d" % l)
                for m in range(4):
                    mi = gg * 4 + m
                    ps = bank()
                    proj_fm(w, m, nb_, ps)
                    cw = lambda i, mi=mi: pp[:, PP_CONV + i * 44 + mi:PP_CONV + i * 44 + mi + 1]
                    cv = scr[mi % 4]
                    P.op("act", lambda e, ps=ps, cv=cv, cw=cw: e.activation(out=cv[:, :nb_], in_=ps[:, :nb_], func=AF.Identity, scale=cw(2)), reads=[ps, pp], writes=[cv])
                    P.op("dve", lambda e, ps=ps, cv=cv, cw=cw: e.scalar_tensor_tensor(out=cv[:, 1:nb_], in0=ps[:, 0:nb_ - 1], scalar=cw(1), in1=cv[:, 1:nb_], op0=ALU.mult, op1=ALU.add),
                         reads=[ps, pp, cv], writes=[cv])
                    P.op("dve", lambda e, ps=ps, cv=cv, cw=cw: e.scalar_tensor_tensor(out=cv[:, 2:nb_], in0=ps[:, 0:nb_ - 2], scalar=cw(0), in1=cv[:, 2:nb_], op0=ALU.mult, op1=ALU.add),
                         reads=[ps, pp, cv], writes=[cv])
                    P.op("dve", lambda e, cv=cv, cw=cw, mi=mi: e.scalar_tensor_tensor(out=cv[:, 0:1], in0=chalo[:, mi, 1:2], scalar=cw(1), in1=cv[:, 0:1], op0=ALU.mult, op1=ALU.add),
                         reads=[chalo, pp, cv], writes=[cv])
                    P.op("dve", lambda e, cv=cv, cw=cw, mi=mi: e.scalar_tensor_tensor(out=cv[:, 0:2], in0=chalo[:, mi, 0:2], scalar=cw(0), in1=cv[:, 0:2], op0=ALU.mult, op1=ALU.add),
                         reads=[chalo, pp, cv], writes=[cv])
                    P.op("dve", lambda e, ps=ps, mi=mi: e.tensor_copy(out=chalo[:, mi, :], in_=ps[:, nb_ - 2:nb_]), reads=[ps], writes=[chalo])
                    if mi < 22:
                        P.op("act", lambda e, cv=cv, mi=mi: e.activation(out=cgs[:, mi, :nb_], in_=cv[:, :nb_], func=AF.Silu), reads=[cv], writes=[cgs])
                    else:
                        P.op("pool", lambda e, cv=cv, mi=mi: e.tensor_tensor(out=fact[:, mi - 22, :nb_], in0=cgs[:, mi - 22, :nb_], in1=cv[:, :nb_], op=ALU.mult), reads=[cgs, cv], writes=[fact])
            for dm in range(8):
                dw = dnw[dm % 2]
                P.dma("sp", dw.res.name, lambda e, dw=dw, dm=dm: e.dma_start(out=dw[:, :, :], in_=dn_bf[l][:, dm * 128:(dm + 1) * 128].rearrange("(c p) d -> p c d", p=128)),
                      reads=[DR("cv%d" % l)], writes=[dw])
                ps = bank()
                for c in range(22):
                    P.op("pe", lambda e, ps=ps, c=c, dw=dw: e.matmul(ps[:, :nb_], lhsT=dw[:, c, :], rhs=fact[:, c, :nb_], start=(c == 0), stop=(c == 21)),
                         reads=[dw, fact], writes=[ps])
                P.op("dve", lambda e, ps=ps, dm=dm: e.tensor_tensor(out=xt[:, dm, :nb_], in0=xt[:, dm, :nb_], in1=ps[:, :nb_], op=ALU.add), reads=[xt, ps], writes=[xt])
            if b == 0:
                P.op("pool", lambda e: e.memset(xt[:, :, 0:NPAD], 0.0), writes=[xt])
            P.mute = False
            if not last:
                P.dma("pool", "xst", lambda e: e.dma_start(out=xdst[:, tsl].rearrange("(c p) t -> p c t", p=128), in_=xt[:, :, :nb_]),
                      reads=[xt], writes=[DR("xs%d_%d" % (l + 1, b))])
            else:
                P.dma("pool", "xst", lambda e: e.dma_start(out=xTo[:, tsl].rearrange("(c p) t -> p c t", p=128), in_=xt[:, :, :nb_]),
                      reads=[xt], writes=[DR("xo")])
                norm_stage(PP_GFIN, nb_)
                for c in range(8):
                    P.op("dve", lambda e, c=c: e.scalar_tensor_tensor(out=xt[:, c, :nb_], in0=xt[:, c, :nb_], scalar=pp[:, PP_GFIN + c:PP_GFIN + c + 1], in1=rstd[:, :nb_],
                                                                      op0=ALU.mult, op1=ALU.mult), reads=[xt, pp, rstd], writes=[xt])
                P.dma("pool", "xst", lambda e: e.dma_start(out=outT[:, tsl].rearrange("(c p) t -> p c t", p=128), in_=xt[:, :, :nb_]),
                      reads=[xt], writes=[DR("out")])

    for l in range(NL):
        src = xT_in if l == 0 else xbufs[l % 2]
        layer(l, src, xbufs[(l + 1) % 2], last=(l == NL - 1))
    if os.environ.get("KDBG"):
        print("op counts", {e: len(P.q[e]) for e in P.ENG}, "sems", len(P.sems))
    P.emit(final_res=[DR("out"), DR("xo")])
    return nc


def _host_consts(Lp):
    cst = np.zeros((128, NCS), np.float32)
    idx = np.arange(128)
    cst[:, CS_ID:CS_ID + 128] = np.eye(128, dtype=np.float32)
    cst[:, CS_MSU:CS_MSU + 128] = (idx[:, None] < idx[None, :])
    cst[:, CS_MIU:CS_MIU + 128] = (idx[:, None] <= idx[None, :])
    cst[:, CS_MSL:CS_MSL + 128] = (idx[:, None] > idx[None, :])
    cst[:, CS_BO:CS_BO + 128] = ((idx[:, None] // 64) == (idx[None, :] // 64))
    perm = np.zeros((128, 128), np.float32)
    for m in range(128):
        d = m % 64
        if d < 8:
            perm[m + 8, m] = 1.0
        elif d < 16:
            perm[m - 8, m] = 1.0
    cst[:, CS_PERM:CS_PERM + 128] = perm
    cst[:, CS_DM:CS_DM + 128] = ((idx[:, None] // 64) <= (idx[None, :] // 64))
    for g, w in enumerate((2, 4, 8, 16)):
        p = idx - NPAD
        cnt = np.where(p >= 0, np.minimum(p + 1, w), w).astype(np.float32)
        cst[:, CS_IC + g * 128:CS_IC + (g + 1) * 128] = (1.0 / cnt)[None, :]
    pos = (np.arange(Lp) - NPAD).astype(np.float32)
    inv = (np.float32(500000.0) ** (-np.arange(0, 16, 2, dtype=np.float32) / np.float32(16))).astype(np.float32)
    ang = (pos[:, None] * inv[None, :]).astype(np.float32)
    cos = np.cos(ang).astype(np.float32).T
    sin = np.sin(ang).astype(np.float32).T
    rc = np.ones((128, Lp), np.float32)
    rs = np.zeros((128, Lp), np.float32)
    for p in range(128):
        d = p % 64
        if d < 8:
            rc[p] = cos[d]
            rs[p] = -sin[d]
        elif d < 16:
            rc[p] = cos[d - 8]
            rs[p] = sin[d - 8]
    return cst, rc, rs


def _pack_pp(l, norm_mix, norm_ffn, norm_final, pool_scale, rw_mu, rw_w0, rw_a0, rw_k_k, rw_k_a, rw_r_k,
             ffn_conv, da_subln, rw_lnx_w, rw_lnx_b, da_lambda):
    pp = np.zeros((128, NPP), np.float32)
    col = lambda v: np.asarray(v, np.float32).reshape(-1, 128).T
    pp[:, PP_GMIX:PP_GMIX + 8] = col(norm_mix[l])
    pp[:, PP_GFFN:PP_GFFN + 8] = col(norm_ffn[l])
    pp[:, PP_PSC:PP_PSC + 4] = col(pool_scale[l])
    pp[:, PP_MU:PP_MU + 14] = col(rw_mu[l])
    pp[:, PP_W0:PP_W0 + 4] = col(rw_w0[l])
    pp[:, PP_A0:PP_A0 + 4] = col(rw_a0[l])
    pp[:, PP_KK:PP_KK + 4] = col(rw_k_k[l])
    pp[:, PP_KA:PP_KA + 4] = col(rw_k_a[l])
    pp[:, PP_RK:PP_RK + 4] = col(rw_r_k[l].reshape(-1))
    for i in range(3):
        pp[:, PP_CONV + i * 44:PP_CONV + (i + 1) * 44] = col(ffn_conv[l, i])
    pp[:, PP_SUBLN:PP_SUBLN + 128] = np.asarray(da_subln[l], np.float32)[None, :]
    pp[:, PP_LNW:PP_LNW + 512] = np.asarray(rw_lnx_w[l], np.float32)[None, :]
    pp[:, PP_LNB:PP_LNB + 512] = np.asarray(rw_lnx_b[l], np.float32)[None, :]
    pp[:, PP_LAM:PP_LAM + 256] = np.asarray(da_lambda[l], np.float32).reshape(1, 256)
    pp[:, PP_GFIN:PP_GFIN + 8] = col(norm_final)
    lam_init = 0.8 - 0.6 * math.exp(-0.3 * l)
    pp[:, PP_OML] = 1.0 - lam_init
    pp[:, PP_NLI] = -lam_init
    return pp


_NC_CACHE = {}


def _run(xT_list, NT, layers, lam_ids, weights, cst, rc, rs):
    key = (NT, len(layers))
    if key not in _NC_CACHE:
        _NC_CACHE[key] = build(NT, len(layers))
    nc = _NC_CACHE[key]
    W = weights
    ls = list(layers)
    f = lambda a: np.ascontiguousarray(np.asarray(a, np.float32))
    shared = {
        "w_in": f(W["w_in"][ls]),
        "w_branch": f(W["w_branch"][ls].reshape(len(ls), 1536, D)),
        "w_out": f(W["w_out"][ls]),
        "ffn_up": f(W["ffn_up"][ls]),
        "ffn_down": f(W["ffn_down"][ls]),
        "pool_w": f(W["pool_w"][ls]),
        "rw_w2": f(W["rw_w2"][ls]),
        "rw_a2": f(W["rw_a2"][ls]),
        "rw_g2": f(W["rw_g2"][ls]),
        "pp": f(np.stack([_pack_pp(l, W["norm_mix"], W["norm_ffn"], W["norm_final"], W["pool_scale"], W["rw_mu"], W["rw_w0"], W["rw_a0"],
                                   W["rw_k_k"], W["rw_k_a"], W["rw_r_k"], W["ffn_conv"], W["da_subln"], W["rw_lnx_w"], W["rw_lnx_b"],
                                   W["da_lambda"]) for l in ls])),
        "cst": cst, "ropeC": rc, "ropeS": rs,
    }
    in_maps = []
    for xT in xT_list:
        m = dict(shared)
        m["xT"] = xT
        in_maps.append(m)
    res = run_bass_kernel_spmd(nc, in_maps, core_ids=list(range(len(in_maps))))
    return [(r["outT"], r["xTo"]) for r in res.results]


def kernel(x, meta_tokens, **W):
    x = np.asarray(x, np.float32)
    B, Lq, _ = x.shape
    L = NPAD + NMETA + Lq
    NT = (L + 127) // 128
    Lp = NT * 128
    cst, rc, rs = _host_consts(Lp)
    W = {k: np.asarray(v) for k, v in W.items()}
    NL = W["w_in"].shape[0]
    xTs = []
    for c in range(8):
        b = c % B
        xp = np.zeros((Lp, D), np.float32)
        xp[NPAD:NPAD + NMETA] = np.asarray(meta_tokens, np.float32)
        xp[NPAD + NMETA:NPAD + NMETA + Lq] = x[b]
        xTs.append(np.ascontiguousarray(xp.T))
    if FUSED:
        outs = _run(xTs, NT, list(range(NL)), list(range(NL)), W, cst, rc, rs)
    else:
        for l in range(NL):
            outs = _run(xTs, NT, [l], [l], W, cst, rc, rs)
            xTs = [np.ascontiguousarray(o[1]) for o in outs]
    out = np.stack([np.ascontiguousarray(outs[b][0].T[NPAD + NMETA:NPAD + NMETA + Lq]) for b in range(B)])
    return out.astype(np.float32)
```

```python
import math
import os
import numpy as np
import concourse.bass as bass
import concourse.mybir as mybir
from concourse.bass_utils import run_bass_kernel_spmd

F32 = mybir.dt.float32
BF16 = mybir.dt.bfloat16
AF = mybir.ActivationFunctionType
ALU = mybir.AluOpType
AX = mybir.AxisListType

D = 1024
INW = 6912
DFF = 2816
NPAD = 112
NMETA = 16
SEQ = 8192
C1 = -math.exp(-0.5)
SAME_ENGINE_SYNC = os.environ.get("KSES", "1") == "1"
FUSED = True
SKIP = set(os.environ.get("KSKIP", "").split(","))

PP_GMIX, PP_GFFN, PP_PSC, PP_MU, PP_W0, PP_A0, PP_KK, PP_KA, PP_RK = 0, 8, 16, 20, 34, 38, 42, 46, 50
PP_CONV = 54
PP_SUBLN = 186
PP_LNW = 314
PP_LNB = 826
PP_LAM = 1338
PP_GFIN = 1594
PP_OML = 1602
PP_NLI = 1603
NPP = 1604
CS_ID, CS_MSU, CS_MIU, CS_MSL, CS_BO, CS_PERM, CS_DM, CS_IC = 0, 128, 256, 384, 512, 640, 768, 896
NCS = 896 + 512


class Res:
    __slots__ = ("name", "w", "r")

    def __init__(self, name):
        self.name = name
        self.w = None
        self.r = {}


class Buf:
    def __init__(self, t, name):
        self.t = t
        self.res = Res(name)

    def __getitem__(self, k):
        return self.t[k]


def _res(x):
    return x.res if isinstance(x, Buf) else x


class _Rec:
    def __init__(self):
        self.call = None

    def __getattr__(self, name):
        def f(*a, **kw):
            self.call = (name, a, kw)
            return None
        return f


class Prog:
    ENG = ["pe", "act", "dve", "pool", "sp"]

    def __init__(self, nc):
        self.nc = nc
        self.q = {e: [] for e in self.ENG}
        self.cnt = {e: 0 for e in self.ENG}
        self.sems = {}
        self.dcnt = {}
        self.waited = {e: {} for e in self.ENG}
        self.mute = False
        for e in self.ENG:
            self.sems[e] = nc.alloc_semaphore("sem_" + e)

    def _deps(self, eng, reads, writes):
        deps = {}

        def add(ev):
            if ev is None:
                return
            k, v = ev
            if deps.get(k, 0) < v:
                deps[k] = v
        for r in reads:
            add(r.w)
        for w in writes:
            add(w.w)
            for k, v in w.r.items():
                add((k, v))
        out = []
        for k, v in deps.items():
            if k == eng and (eng == "pe" or not SAME_ENGINE_SYNC):
                continue
            if self.waited[eng].get(k, 0) >= v:
                continue
            self.waited[eng][k] = v
            out.append((k, v))
        return out

    def _mark(self, ev, reads, writes):
        k, v = ev
        for r in reads:
            if r.r.get(k, 0) < v:
                r.r[k] = v
        for w in writes:
            w.w = ev
            w.r = {}

    def op(self, eng, fn, reads=(), writes=()):
        if self.mute:
            return
        rec = _Rec()
        fn(rec)
        fn = rec.call
        reads = [_res(x) for x in reads]
        writes = [_res(x) for x in writes]
        waits = self._deps(eng, reads, writes)
        self.cnt[eng] += 1
        self.q[eng].append((waits, fn, (eng, 1)))
        self._mark((eng, self.cnt[eng]), reads, writes)

    def dma(self, qeng, key, fn, reads=(), writes=()):
        if self.mute:
            return
        rec = _Rec()
        fn(rec)
        fn = rec.call
        reads = [_res(x) for x in reads]
        writes = [_res(x) for x in writes]
        waits = self._deps(qeng, reads, writes)
        if key not in self.sems:
            self.sems[key] = self.nc.alloc_semaphore("dsem_" + key)
            self.dcnt[key] = 0
        self.dcnt[key] += 16
        self.q[qeng].append((waits, fn, (key, 16)))
        self._mark((key, self.dcnt[key]), reads, writes)

    def emit(self, final_res=()):
        nc = self.nc
        fin = {}
        for r in final_res:
            r = _res(r)
            if r.w is not None:
                k, v = r.w
                fin[k] = max(fin.get(k, 0), v)
        with nc.Block() as block:
            def mk(e):
                def body(eng):
                    for waits, fn, (k, amt) in self.q[e]:
                        for (wk, wv) in waits:
                            eng.wait_ge(self.sems[wk], wv)
                        name, a, kw = fn
                        getattr(eng, name)(*a, **kw).then_inc(self.sems[k], amt)
                    if e == "sp":
                        for wk, wv in fin.items():
                            eng.wait_ge(self.sems[wk], wv)
                return body
            block.tensor(mk("pe"))
            block.scalar(mk("act"))
            block.vector(mk("dve"))
            block.gpsimd(mk("pool"))
            block.sync(mk("sp"))


def build(NT, NL, NTB=2, lam_inits=None, dbg=False):
    nc = bass.Bass("TRN2", target_bir_lowering=False)
    Lp = NT * 128
    P = Prog(nc)

    def dram_in(name, shape, dt=F32):
        return nc.dram_tensor(name, list(shape), dt, kind="ExternalInput").ap()

    def dram_tmp(name, shape, dt):
        return nc.dram_tensor(name, list(shape), dt, kind="Internal").ap()

    xT_in = dram_in("xT", [D, Lp])
    w_in = dram_in("w_in", [NL, D, INW])
    w_branch = dram_in("w_branch", [NL, 1536, D])
    w_out = dram_in("w_out", [NL, D, D])
    ffn_up = dram_in("ffn_up", [NL, D, 2 * DFF])
    ffn_down = dram_in("ffn_down", [NL, DFF, D])
    pool_w = dram_in("pool_w", [NL, 4, 128, 128])
    rw_w2 = dram_in("rw_w2", [NL, 64, 512])
    rw_a2 = dram_in("rw_a2", [NL, 64, 512])
    rw_g2 = dram_in("rw_g2", [NL, 128, 512])
    pp_in = dram_in("pp", [NL, 128, NPP])
    cst_in = dram_in("cst", [128, NCS])
    ropeC = dram_in("ropeC", [128, Lp])
    ropeS = dram_in("ropeS", [128, Lp])
    outT = nc.dram_tensor("outT", [D, Lp], F32, kind="ExternalOutput").ap()
    xTo = nc.dram_tensor("xTo", [D, Lp], F32, kind="ExternalOutput").ap()

    win_bf = dram_tmp("win_bf", [NL, D, INW], BF16)
    wbr_bf = dram_tmp("wbr_bf", [NL, 1536, D], BF16)
    wout_bf = dram_tmp("wout_bf", [NL, D, D], BF16)
    up_bf = dram_tmp("up_bf", [NL, D, 2 * DFF], BF16)
    dn_bf = dram_tmp("dn_bf", [NL, DFF, D], BF16)
    xbufs = [dram_tmp("xT_a", [D, Lp], F32), dram_tmp("xT_b", [D, Lp], F32)]
    kT_hist = dram_tmp("kT_hist", [512, Lp], BF16)
    v_hist = dram_tmp("v_hist", [Lp, 516], BF16)
    dres = {}

    def DR(key):
        if key not in dres:
            dres[key] = Res(key)
        return dres[key]

    def S(name, shape, dt):
        return Buf(nc.alloc_sbuf_tensor("s_" + name, list(shape), dt), name)

    n = NTB * 128
    pb = [Buf(nc.alloc_psum_tensor("pb%d" % i, [128, 512], F32), "pb%d" % i) for i in range(7)]
    pbT = Buf(nc.alloc_psum_tensor("pbT", [128, 1024], BF16), "pbT")
    rr = [0]

    ALLB = [0, 1, 2, 3, 4, 5, 6]
    ATTB = [0, 1, 2]
    AUXB = ALLB
    cur_banks = [ALLB]

    def bank(lst=None):
        lst = lst or cur_banks[0]
        rr[0] += 1
        return pb[lst[rr[0] % len(lst)]]

    cst = S("cst", [128, NCS], F32)
    cstb = S("cstb", [128, 896], BF16)
    ident2 = S("ident2", [128, 2, 128], BF16)
    ones_bf = S("ones_bf", [128, 128], BF16)
    rmask = S("rmask", [128, n], F32)
    eps6 = S("eps6", [128, 1], F32)
    eps5 = S("eps5", [128, 1], F32)
    epsg = S("epsg", [128, 1], F32)
    eps18 = S("eps18", [128, 1], F32)
    P.dma("sp", "cst", lambda e: e.dma_start(out=cst[:, :], in_=cst_in[:, :]), writes=[cst])
    P.op("dve", lambda e: e.tensor_copy(out=cstb[:, :], in_=cst[:, 0:896]), reads=[cst], writes=[cstb])
    P.op("pool", lambda e: e.tensor_copy(out=ident2[:, 0, :], in_=cst[:, CS_ID:CS_ID + 128]), reads=[cst], writes=[ident2])
    P.op("pool", lambda e: e.tensor_copy(out=ident2[:, 1, :], in_=cst[:, CS_ID:CS_ID + 128]), reads=[cst], writes=[ident2])
    P.op("pool", lambda e: e.memset(ones_bf[:, :], 1.0), writes=[ones_bf])
    P.op("pool", lambda e: e.memset(rmask[:, :], 1.0), writes=[rmask])
    for tt in range(NTB):
        P.op("pool", lambda e, tt=tt: e.memset(rmask[:, tt * 128:tt * 128 + 1], 0.0), writes=[rmask])
    P.op("pool", lambda e: e.memset(eps6[:, :], 1e-6), writes=[eps6])
    P.op("pool", lambda e: e.memset(eps5[:, :], 1e-5), writes=[eps5])
    P.op("pool", lambda e: e.memset(epsg[:, :], 64e-5), writes=[epsg])
    P.op("pool", lambda e: e.memset(eps18[:, :], 1e-18), writes=[eps18])
    ident_bf = lambda: cstb[:, CS_ID:CS_ID + 128]
    msu = lambda: cst[:, CS_MSU:CS_MSU + 128]
    miu = lambda: cst[:, CS_MIU:CS_MIU + 128]
    msl = lambda: cst[:, CS_MSL:CS_MSL + 128]
    bones_bf = lambda: cstb[:, CS_BO:CS_BO + 128]
    perm_bf = lambda: cstb[:, CS_PERM:CS_PERM + 128]
    dmask_bf = lambda: cstb[:, CS_DM:CS_DM + 128]

    def conv_chunks(l):
        ch = []
        for src, dst, rows in ((w_in[l], win_bf[l], D), (w_branch[l], wbr_bf[l], 1536), (w_out[l], wout_bf[l], D),
                               (ffn_up[l], up_bf[l], D), (ffn_down[l], dn_bf[l], DFF)):
            for r0 in range(0, rows, 128):
                ch.append((src, dst, r0, min(rows, r0 + 128)))
        return ch

    def issue_conv(l, chunks):
        for (src, dst, r0, r1) in chunks:
            P.dma("pool", "cv%d" % l, lambda e, src=src, dst=dst, r0=r0, r1=r1: e.dma_start(out=dst[r0:r1, :], in_=src[r0:r1, :]),
                  writes=[DR("cv%d" % l)])

    issue_conv(0, conv_chunks(0))
    P.mute = False
    pp = S("pp", [128, NPP], F32)
    poolw_bf = S("poolw_bf", [128, 4, 128], BF16)
    w2_bf = S("w2_bf", [128, 512], BF16)
    g2_bf = S("g2_bf", [128, 512], BF16)
    rkones = S("rkones", [128, 4, 128], BF16)
    omka = S("omka", [128, 4], F32)
    neglam = S("neglam", [128, 1], F32)
    lamt = S("lamt", [128, 8], F32)
    sublnw = S("sublnw", [128, 128], F32)

    xt = S("xt", [128, 8, n], F32)
    sq = S("sq", [128, 8, n], BF16)
    hT = S("hT", [128, 8, n], BF16)
    lnv = S("lnv", [128, n], F32)
    rstd = S("rstd", [128, n], F32)
    wg = [S("wg%d" % i, [128, 8, 512], BF16) for i in range(2)]
    gates = S("gates", [128, 24, n], BF16)
    brP = S("brP", [128, 4, n], BF16)
    brA = S("brA", [128, 4, n], BF16)
    brR = S("brR", [128, 4, n], BF16)
    qT = S("qT", [128, 4, n], BF16)
    kTb = S("kTb", [128, 4, n], BF16)
    vaug = S("vaug", [128, NTB, 4, 129], BF16)
    rC = S("rC", [128, n], F32)
    rS = S("rS", [128, n], F32)
    qraw = [S("qraw%d" % i, [128, n], BF16) for i in range(2)]
    scr = [S("scr%d" % i, [128, n], F32) for i in range(8)]
    pu = S("pu", [128, 4, 16 + n], F32)
    pta = S("pta", [128, 16 + n], F32)
    ptb = S("ptb", [128, 16 + n], F32)
    pooled = S("pooled", [128, 4, n], BF16)
    rp = S("rp", [128, 12, n], BF16)
    ltmp = [S("ltmp%d" % i, [128, 1 + n], F32) for i in range(2)]
    ldt = [S("ldt%d" % i, [128, n], F32) for i in range(2)]
    lora = S("lora", [128, 2, n], F32)
    rhalo = S("rhalo", [128, 14], F32)
    tw_bf = S("tw_bf", [128, n], BF16)
    sg_bf = S("sg_bf", [128, n], BF16)
    kk2_bf = S("kk2_bf", [128, n], BF16)
    rk_bf = S("rk_bf", [128, n], BF16)
    AR = S("AR", [128, 4, NTB, 2, 128], BF16)
    kt_bf = S("kt_bf", [128, 4, n], BF16)
    bt_bf = S("bt_bf", [128, 4, n], BF16)
    tokm = S("tokm", [128, NTB, 4, 3, 128], BF16)
    gT = S("gT", [128, 4, n], BF16)
    bonT = S("bonT", [128, 4, n], BF16)
    emt = S("emt", [128, 4, NTB], F32)
    eet = S("eet", [128, 4, NTB], F32)
    emet = S("emet", [128, 4, NTB], F32)
    Qp = [S("Qp%d" % i, [128, 8, 128], BF16) for i in range(2)]
    Np = [S("Np%d" % i, [128, 8, 128], BF16) for i in range(2)]
    Sb = [S("Sb%d" % i, [128, 8, 128], BF16) for i in range(2)]
    ABm = S("ABm", [128, 8, 128], BF16)
    AKm = S("AKm", [128, 8, 128], BF16)
    RKm = S("RKm", [128, 8, 128], BF16)
    Xs = S("Xs", [128, 512], BF16)
    Us = S("Us", [128, 512], BF16)
    Ysb = S("Ysb", [128, 512], F32)
    ysq = S("ysq", [128, 512], F32)
    yc = S("yc", [128, 512], F32)
    ynb = S("ynb", [128, 512], BF16)
    yst = S("yst", [128, 40], F32)
    Hs32 = [S("Hs32_%d" % j, [128, 64], F32) for j in range(4)]
    Hb = [S("Hb_%d" % j, [128, 64], BF16) for j in range(4)]
    Hpe = [S("Hpe_%d" % j, [128, 64], F32) for j in range(4)]
    kTk = [S("kTk%d" % i, [128, n], BF16) for i in range(2)]
    vk = [S("vk%d" % i, [128, NTB, 129], BF16) for i in range(2)]
    ET = [S("ET%d" % i, [128, n], BF16) for i in range(3)]
    ao = S("ao", [128, 128], F32)
    at = S("at", [128, 128], F32)
    aj = S("aj", [128, 128], F32)
    aon = S("aon", [128, 128], BF16)
    ast = S("ast", [128, 8], F32)
    merged = S("merged", [128, 8, n], BF16)
    mtmp = [S("mtmp%d" % i, [128, n], BF16) for i in range(2)]
    wbr = S("wbr", [128, 4, 1024], BF16)
    fact = S("fact", [128, 22, n], BF16)
    dnw = [S("dnw%d" % i, [128, 22, 128], BF16) for i in range(2)]
    chalo = S("chalo", [128, 44, 2], F32)
    cgs = gates
    invc = lambda g: cst[:, CS_IC + g * 128:CS_IC + (g + 1) * 128]

    slot_res = {}

    def oslot(c, qi):
        idx = c * NTB + qi
        bk = 3 + idx
        return pb[bk], 0, pb[bk].res
    assert 2 * NTB <= 4

    nblocks = (NT + NTB - 1) // NTB

    def norm_stage(gcol, nb_):
        P.op("act", lambda e: e.activation(out=sq[:, :, :nb_], in_=xt[:, :, :nb_], func=AF.Square), reads=[xt], writes=[sq])
        ps = bank()
        for c in range(8):
            P.op("pe", lambda e, c=c: e.matmul(ps[:, :nb_], lhsT=ones_bf[:, :], rhs=sq[:, c, :nb_], start=(c == 0), stop=(c == 7)),
                 reads=[ones_bf, sq], writes=[ps])
        P.op("act", lambda e: e.activation(out=lnv[:, :nb_], in_=ps[:, :nb_], func=AF.Ln, bias=eps6[:, 0:1], scale=1.0 / D),
             reads=[ps, eps6], writes=[lnv])
        P.op("act", lambda e: e.activation(out=rstd[:, :nb_], in_=lnv[:, :nb_], func=AF.Exp, scale=-0.5), reads=[lnv], writes=[rstd])
        for c in range(8):
            P.op("dve", lambda e, c=c: e.scalar_tensor_tensor(out=hT[:, c, :nb_], in0=xt[:, c, :nb_], scalar=pp[:, gcol + c:gcol + c + 1],
                                                              in1=rstd[:, :nb_], op0=ALU.mult, op1=ALU.mult),
                 reads=[xt, pp, rstd], writes=[hT])

    wgi = [0]

    def load_wg(src, c0, gc, key):
        wgi[0] += 1
        w = wg[wgi[0] % 2]
        P.dma("sp", w.res.name, lambda e: e.dma_start(out=w[:, :, :gc], in_=src[:, c0:c0 + gc].rearrange("(c p) m -> p c m", p=128)),
              reads=[DR(key)], writes=[w])
        return w

    def proj_fm(w, ml, nb_, ps):
        for c in range(8):
            P.op("pe", lambda e, c=c: e.matmul(ps[:, :nb_], lhsT=w[:, c, ml * 128:(ml + 1) * 128], rhs=hT[:, c, :nb_], start=(c == 0), stop=(c == 7)),
                 reads=[w, hT], writes=[ps])

    def layer(l, xsrc, xdst, last):
        P.dma("sp", "pp", lambda e: e.dma_start(out=pp[:, :], in_=pp_in[l]), writes=[pp])
        P.dma("pool", "poolw", lambda e: e.dma_start(out=poolw_bf[:, :, :], in_=pool_w[l].rearrange("g c d -> c g d")), writes=[poolw_bf])
        P.dma("pool", "w2", lambda e: e.dma_start(out=w2_bf[0:64, :], in_=rw_w2[l]), writes=[w2_bf])
        P.dma("pool", "w2", lambda e: e.dma_start(out=w2_bf[64:128, :], in_=rw_a2[l]), writes=[w2_bf])
        P.dma("pool", "g2", lambda e: e.dma_start(out=g2_bf[:, :], in_=rw_g2[l]), writes=[g2_bf])
        for j in range(4):
            P.op("dve", lambda e, j=j: e.tensor_scalar(out=rkones[:, j, :], in0=cst[:, CS_BO:CS_BO + 128], scalar1=pp[:, PP_RK + j:PP_RK + j + 1],
                                                      scalar2=None, op0=ALU.mult), reads=[cst, pp], writes=[rkones])
        P.op("dve", lambda e: e.tensor_scalar(out=omka[:, :], in0=pp[:, PP_KA:PP_KA + 4], scalar1=-1.0, scalar2=1.0, op0=ALU.mult, op1=ALU.add),
             reads=[pp], writes=[omka])
        P.op("dve", lambda e: e.tensor_scalar(out=sublnw[:, :], in0=pp[:, PP_SUBLN:PP_SUBLN + 128], scalar1=pp[:, PP_OML:PP_OML + 1], scalar2=None, op0=ALU.mult),
             reads=[pp], writes=[sublnw])
        P.op("dve", lambda e: e.tensor_tensor(out=aj[:, 0:64], in0=pp[:, PP_LAM:PP_LAM + 64], in1=pp[:, PP_LAM + 64:PP_LAM + 128], op=ALU.mult), reads=[pp], writes=[aj])
        P.op("dve", lambda e: e.tensor_tensor(out=aj[:, 64:128], in0=pp[:, PP_LAM + 128:PP_LAM + 192], in1=pp[:, PP_LAM + 192:PP_LAM + 256], op=ALU.mult), reads=[pp], writes=[aj])
        P.op("dve", lambda e: e.tensor_reduce(out=lamt[:, 0:2], in_=aj[:, :].rearrange("p (a b) -> p a b", b=64), axis=AX.X, op=ALU.add), reads=[aj], writes=[lamt])
        P.op("act", lambda e: e.activation(out=lamt[:, 2:4], in_=lamt[:, 0:2], func=AF.Exp), reads=[lamt], writes=[lamt])
        P.op("dve", lambda e: e.tensor_tensor(out=lamt[:, 4:5], in0=lamt[:, 3:4], in1=lamt[:, 2:3], op=ALU.subtract), reads=[lamt], writes=[lamt])
        P.op("dve", lambda e: e.tensor_scalar(out=neglam[:, :], in0=lamt[:, 4:5], scalar1=pp[:, PP_NLI:PP_NLI + 1], scalar2=None, op0=ALU.add), reads=[lamt, pp], writes=[neglam])
        P.op("pool", lambda e: e.memset(rhalo[:, :], 0.0), writes=[rhalo])
        P.op("pool", lambda e: e.memset(pu[:, :, 0:16], 0.0), writes=[pu])
        P.op("pool", lambda e: e.memset(chalo[:, :, :], 0.0), writes=[chalo])
        for j in range(4):
            P.op("pool", lambda e, j=j: e.memset(Hs32[j][:, :], 0.0), writes=[Hs32[j]])

        WIN = win_bf[l]
        nxt = conv_chunks(l + 1) if l + 1 < NL else []
        per_blk = (len(nxt) + nblocks - 2) // max(1, nblocks - 1) if nxt else 0
        for b in range(nblocks):
            if nxt:
                issue_conv(l + 1, nxt[b * per_blk:(b + 1) * per_blk])
            t0 = b * n
            nt = min(NTB, NT - b * NTB)
            nb_ = nt * 128
            tsl = slice(t0, t0 + nb_)
            xin_key = "x%d_%d_%d" % (l, 0, b)
            P.dma("sp", "xt", lambda e: e.dma_start(out=xt[:, :, :nb_], in_=xsrc[:, tsl].rearrange("(c p) t -> p c t", p=128)),
                  reads=[DR("xs%d_%d" % (l, b))], writes=[xt])
            P.dma("sp", "rC", lambda e: e.dma_start(out=rC[:, :nb_], in_=ropeC[:, tsl]), writes=[rC])
            P.dma("sp", "rS", lambda e: e.dma_start(out=rS[:, :nb_], in_=ropeS[:, tsl]), writes=[rS])
            norm_stage(PP_GMIX, nb_)

            P.mute = ("pool" in SKIP)
            w = load_wg(WIN, 0, 512, "cv%d" % l)
            for g in range(4):
                ps = bank()
                proj_fm(w, g, nb_, ps)
                P.op("act", lambda e, g=g, ps=ps: e.activation(out=pu[:, g, 16:16 + nb_], in_=ps[:, :nb_], func=AF.Copy), reads=[ps], writes=[pu])
                src = pu
                bufs = [pta, ptb]
                cur = None
                for lev in range(g + 1):
                    sh = 1 << lev
                    lo = 2 * sh - 1
                    dst = bufs[lev % 2]
                    if lev == 0:
                        P.op("dve", lambda e, dst=dst, g=g, lo=lo, sh=sh: e.tensor_tensor(out=dst[:, lo:16 + nb_], in0=pu[:, g, lo:16 + nb_], in1=pu[:, g, lo - sh:16 + nb_ - sh], op=ALU.add),
                             reads=[pu], writes=[dst])
                    else:
                        P.op("dve", lambda e, dst=dst, cur=cur, lo=lo, sh=sh: e.tensor_tensor(out=dst[:, lo:16 + nb_], in0=cur[:, lo:16 + nb_], in1=cur[:, lo - sh:16 + nb_ - sh], op=ALU.add),
                             reads=[cur], writes=[dst])
                    cur = dst
                wv = float(2 << g)
                P.op("dve", lambda e, g=g, cur=cur, wv=wv: e.scalar_tensor_tensor(out=pooled[:, g, :nb_], in0=cur[:, 16:16 + nb_], scalar=1.0 / wv, in1=pu[:, g, 16:16 + nb_],
                                                                                  op0=ALU.mult, op1=ALU.subtract), reads=[cur, pu], writes=[pooled])
                if b == 0:
                    P.op("dve", lambda e, g=g, cur=cur: e.tensor_tensor(out=cur[:, 16:144], in0=cur[:, 16:144], in1=invc(g), op=ALU.mult), reads=[cur, cst], writes=[cur])
                    P.op("dve", lambda e, g=g, cur=cur: e.tensor_tensor(out=pooled[:, g, 0:128], in0=cur[:, 16:144], in1=pu[:, g, 16:144], op=ALU.subtract),
                         reads=[cur, pu], writes=[pooled])
                P.op("pool", lambda e, g=g: e.tensor_copy(out=pu[:, g, 0:16], in_=pu[:, g, nb_:nb_ + 16]), reads=[pu], writes=[pu])
                ps2 = bank(AUXB)
                P.op("pe", lambda e, g=g, ps2=ps2: e.matmul(ps2[:, :nb_], lhsT=poolw_bf[:, g, :], rhs=pooled[:, g, :nb_], start=True, stop=True),
                     reads=[poolw_bf, pooled], writes=[ps2])
                P.op("act", lambda e, g=g, ps2=ps2: e.activation(out=brP[:, g, :nb_], in_=ps2[:, :nb_], func=AF.Identity, scale=pp[:, PP_PSC + g:PP_PSC + g + 1]),
                     reads=[ps2, pp], writes=[brP])

            P.mute = ("qk" in SKIP)
            for which in range(2):
                w = load_wg(WIN, 512 + which * 512, 512, "cv%d" % l)
                dstb = qT if which == 0 else kTb
                for m in range(4):
                    ps = bank()
                    proj_fm(w, m, nb_, ps)
                    qr = qraw[m % 2]
                    P.op("act", lambda e, ps=ps, qr=qr: e.activation(out=qr[:, :nb_], in_=ps[:, :nb_], func=AF.Copy), reads=[ps], writes=[qr])
                    ps2 = bank(AUXB)
                    P.op("pe", lambda e, ps2=ps2, qr=qr: e.matmul(ps2[:, :nb_], lhsT=perm_bf(), rhs=qr[:, :nb_], start=True, stop=True), reads=[cstb, qr], writes=[ps2])
                    s1 = scr[(2 * m) % 8]
                    s2 = scr[(2 * m + 1) % 8]
                    P.op("dve", lambda e, qr=qr, s1=s1: e.tensor_tensor(out=s1[:, :nb_], in0=qr[:, :nb_], in1=rC[:, :nb_], op=ALU.mult), reads=[qr, rC], writes=[s1])
                    P.op("dve", lambda e, ps2=ps2, s2=s2: e.tensor_tensor(out=s2[:, :nb_], in0=ps2[:, :nb_], in1=rS[:, :nb_], op=ALU.mult), reads=[ps2, rS], writes=[s2])
                    P.op("pool", lambda e, s1=s1, s2=s2, m=m, dstb=dstb: e.tensor_tensor(out=dstb[:, m, :nb_], in0=s1[:, :nb_], in1=s2[:, :nb_], op=ALU.add),
                         reads=[s1, s2], writes=[dstb])
            P.dma("pool", "kst", lambda e: e.dma_start(out=kT_hist[:, tsl].rearrange("(h p) t -> p h t", p=128), in_=kTb[:, :, :nb_]),
                  reads=[kTb], writes=[DR("kh%d" % b)])

            P.mute = ("v" in SKIP)
            w = load_wg(WIN, 1536, 512, "cv%d" % l)
            P.op("pool", lambda e: e.memset(vaug[:, :, :, 128:129], 1.0), writes=[vaug])
            if b == 0:
                P.op("pool", lambda e: e.memset(vaug[0:NPAD, 0, :, 128:129], 0.0), writes=[vaug])
            for tt in range(nt):
                ps = bank()
                for c in range(8):
                    P.op("pe", lambda e, c=c, ps=ps, tt=tt, w=w: e.matmul(ps[:, :], lhsT=hT[:, c, tt * 128:(tt + 1) * 128], rhs=w[:, c, :], start=(c == 0), stop=(c == 7)),
                         reads=[hT, w], writes=[ps])
                P.op("act", lambda e, ps=ps, tt=tt: e.activation(out=vaug[:, tt, :, 0:128], in_=ps[:, :].rearrange("p (h e) -> p h e", h=4), func=AF.Copy),
                     reads=[ps], writes=[vaug])
            P.dma("pool", "vst", lambda e: e.dma_start(out=v_hist[tsl, :].rearrange("(t p) f -> p t f", p=128), in_=vaug[:, :nt, :, :].rearrange("p t h e -> p t (h e)")),
                  reads=[vaug], writes=[DR("vh%d" % b)])

            P.mute = ("rwproj" in SKIP)
            def lerp_tile(ps, mi, out_ap_fn, outbuf):
                lt = ltmp[mi % 2]
                ld = ldt[mi % 2]
                P.op("pool", lambda e: e.tensor_copy(out=lt[:, 0:1], in_=rhalo[:, mi:mi + 1]), reads=[rhalo], writes=[lt])
                P.op("act", lambda e: e.activation(out=lt[:, 1:1 + nb_], in_=ps[:, :nb_], func=AF.Copy), reads=[ps], writes=[lt])
                P.op("pool", lambda e: e.tensor_copy(out=rhalo[:, mi:mi + 1], in_=lt[:, nb_:nb_ + 1]), reads=[lt], writes=[rhalo])
                P.op("dve", lambda e: e.tensor_tensor(out=ld[:, :nb_], in0=lt[:, 0:nb_], in1=lt[:, 1:1 + nb_], op=ALU.subtract), reads=[lt], writes=[ld])
                P.op("dve", lambda e: e.scalar_tensor_tensor(out=out_ap_fn(), in0=ld[:, :nb_], scalar=pp[:, PP_MU + mi:PP_MU + mi + 1], in1=lt[:, 1:1 + nb_],
                                                             op0=ALU.mult, op1=ALU.add), reads=[ld, lt, pp], writes=[outbuf])
            w = load_wg(WIN, 3584, 256, "cv%d" % l)
            for m in range(2):
                ps = bank()
                proj_fm(w, m, nb_, ps)
                lerp_tile(ps, 12 + m, lambda m=m: lora[:, m, :nb_], lora)
            P.op("act", lambda e: e.activation(out=tw_bf[0:64, :nb_], in_=lora[0:64, 0, :nb_], func=AF.Tanh), reads=[lora], writes=[tw_bf])
            P.op("act", lambda e: e.activation(out=tw_bf[64:128, :nb_], in_=lora[64:128, 0, :nb_], func=AF.Copy), reads=[lora], writes=[tw_bf])
            P.op("act", lambda e: e.activation(out=sg_bf[:, :nb_], in_=lora[:, 1, :nb_], func=AF.Sigmoid), reads=[lora], writes=[sg_bf])
            for which in range(3):
                w = load_wg(WIN, 2048 + which * 512, 512, "cv%d" % l)
                for m in range(4):
                    ps = bank()
                    proj_fm(w, m, nb_, ps)
                    mi = which * 4 + m
                    lerp_tile(ps, mi, lambda mi=mi: rp[:, mi, :nb_], rp)

            P.mute = ("gates" in SKIP)
            for gg in range(6):
                w = load_wg(WIN, 3840 + gg * 512, 512, "cv%d" % l)
                for m in range(4):
                    ps = bank()
                    proj_fm(w, m, nb_, ps)
                    gi = gg * 4 + m
                    P.op("act", lambda e, ps=ps, gi=gi: e.activation(out=gates[:, gi, :nb_], in_=ps[:, :nb_], func=AF.Sigmoid), reads=[ps], writes=[gates])

            def attn_gen():
                ei = 0
                for h in range(4):
                    for kb in range(b + 1):
                        nkt = min(NTB, NT - kb * NTB)
                        nk = nkt * 128
                        kk_ = kTk[(h * 64 + kb) % 2]
                        vv = vk[(h * 64 + kb) % 2]
                        ksl = slice(kb * n, kb * n + nk)
                        P.dma("sp", kk_.res.name, lambda e, kk_=kk_, ksl=ksl, nk=nk: e.dma_start(out=kk_[:, :nk], in_=kT_hist[h * 128:(h + 1) * 128, ksl]),
                              reads=[DR("kh%d" % kb)], writes=[kk_])
                        P.dma("sp", vv.res.name, lambda e, vv=vv, ksl=ksl, nkt=nkt: e.dma_start(out=vv[:, :nkt, :], in_=v_hist[ksl, h * 129:(h + 1) * 129].rearrange("(t p) f -> p t f", p=128)),
                              reads=[DR("vh%d" % kb)], writes=[vv])
                        for jj in range(nkt):
                            jg = kb * NTB + jj
                            q0 = 0 if kb < b else jj
                            ncol = (nt - q0) * 128
                            if ncol <= 0:
                                continue
                            for c in range(2):
                                st = bank()
                                et = ET[ei % 3]
                                ei += 1
                                P.op("pe", lambda e, st=st, kk_=kk_, c=c, jj=jj, q0=q0, ncol=ncol: e.matmul(st[:, :ncol], lhsT=kk_[c * 64:(c + 1) * 64, jj * 128:(jj + 1) * 128],
                                                                                                        rhs=qT[c * 64:(c + 1) * 64, h, q0 * 128:q0 * 128 + ncol], start=True, stop=True),
                                     reads=[kk_, qT], writes=[st])
                                P.op("act", lambda e, st=st, et=et, ncol=ncol: e.activation(out=et[:, :ncol], in_=st[:, :ncol], func=AF.Exp, scale=0.125), reads=[st], writes=[et])
                                if kb == b and jg > 0:
                                    P.op("pool", lambda e, et=et: e.tensor_tensor(out=et[:, 0:128], in0=et[:, 0:128], in1=dmask_bf(), op=ALU.mult), reads=[et, cstb], writes=[et])
                                for qi in range(q0, nt):
                                    ob, off, ores = oslot(c, qi)
                                    ig = b * NTB + qi
                                    P.op("pe", lambda e, ob=ob, off=off, et=et, qi=qi, q0=q0, vv=vv, jj=jj, jg=jg, ig=ig: e.matmul(
                                        ob[:, off:off + 129], lhsT=et[:, (qi - q0) * 128:(qi - q0 + 1) * 128], rhs=vv[:, jj, :], start=(jg == 0), stop=(jg == ig)),
                                        reads=[et, vv], writes=[ores])
                            if kb == b:
                                qi = jj
                                o0b, o0off, o0r = oslot(0, qi)
                                o1b, o1off, o1r = oslot(1, qi)
                                P.op("dve", lambda e, o0b=o0b, o0off=o0off: e.reciprocal(out=ast[:, 0:1], in_=o0b[:, o0off + 128:o0off + 129]), reads=[o0r], writes=[ast])
                                P.op("dve", lambda e, o1b=o1b, o1off=o1off: e.reciprocal(out=ast[:, 1:2], in_=o1b[:, o1off + 128:o1off + 129]), reads=[o1r], writes=[ast])
                                P.op("dve", lambda e: e.tensor_tensor(out=ast[:, 2:3], in0=ast[:, 1:2], in1=neglam[:, 0:1], op=ALU.mult), reads=[ast, neglam], writes=[ast])
                                P.op("dve", lambda e, o1b=o1b, o1off=o1off: e.tensor_scalar(out=at[:, :], in0=o1b[:, o1off:o1off + 128], scalar1=ast[:, 2:3], scalar2=None, op0=ALU.mult),
                                     reads=[o1r, ast], writes=[at])
                                P.op("dve", lambda e, o0b=o0b, o0off=o0off: e.scalar_tensor_tensor(out=ao[:, :], in0=o0b[:, o0off:o0off + 128], scalar=ast[:, 0:1], in1=at[:, :],
                                                                                                  op0=ALU.mult, op1=ALU.add), reads=[o0r, ast, at], writes=[ao])
                                P.op("act", lambda e: e.activation(out=aj[:, :], in_=ao[:, :], func=AF.Square, accum_out=ast[:, 3:4]), reads=[ao], writes=[aj, ast])
                                P.op("act", lambda e: e.activation(out=ast[:, 4:5], in_=ast[:, 3:4], func=AF.Ln, bias=eps5[:, 0:1], scale=1.0 / 128), reads=[ast, eps5], writes=[ast])
                                P.op("act", lambda e: e.activation(out=ast[:, 5:6], in_=ast[:, 4:5], func=AF.Exp, scale=-0.5), reads=[ast], writes=[ast])
                                P.op("dve", lambda e: e.scalar_tensor_tensor(out=aon[:, :], in0=ao[:, :], scalar=ast[:, 5:6], in1=sublnw[:, :], op0=ALU.mult, op1=ALU.mult),
                                     reads=[ao, ast, sublnw], writes=[aon])
                                P.op("pe", lambda e: e.transpose(pbT[:, 0:128], aon[:, :], ident_bf()), reads=[aon, cstb], writes=[pbT])
                                P.op("act", lambda e, qi=qi: e.activation(out=brA[:, h, qi * 128:(qi + 1) * 128], in_=pbT[:, 0:128], func=AF.Copy), reads=[pbT], writes=[brA])
                            yield

            def rw_gen():
                A_, B_, C_, D_, E_, F_, G_ = scr[0], scr[1], scr[2], scr[3], scr[4], scr[5], scr[6]
                v3 = lambda buf: buf[:, :nb_].rearrange("p (t s) -> p t s", s=128)
                for j in range(4):
                    jc = slice(j * 128, (j + 1) * 128)
                    r_ap = lambda: rp[:, j, :nb_]
                    k_ap = lambda: rp[:, 4 + j, :nb_]
                    v_ap = lambda: rp[:, 8 + j, :nb_]
                    ps = bank()
                    P.op("pe", lambda e, ps=ps, jc=jc: e.matmul(ps[:, :nb_], lhsT=w2_bf[0:64, jc], rhs=tw_bf[0:64, :nb_], start=True, stop=True), reads=[w2_bf, tw_bf], writes=[ps])
                    P.op("act", lambda e, ps=ps, j=j: e.activation(out=A_[:, :nb_], in_=ps[:, :nb_], func=AF.Sigmoid, bias=pp[:, PP_W0 + j:PP_W0 + j + 1]), reads=[ps, pp], writes=[A_])
                    ps = bank()
                    P.op("pe", lambda e, ps=ps, jc=jc: e.matmul(ps[:, :nb_], lhsT=w2_bf[64:128, jc], rhs=tw_bf[64:128, :nb_], start=True, stop=True), reads=[w2_bf, tw_bf], writes=[ps])
                    P.op("act", lambda e, ps=ps, j=j: e.activation(out=B_[:, :nb_], in_=ps[:, :nb_], func=AF.Sigmoid, bias=pp[:, PP_A0 + j:PP_A0 + j + 1]), reads=[ps, pp], writes=[B_])
                    ps = bank()
                    P.op("pe", lambda e, ps=ps, jc=jc: e.matmul(ps[:, :nb_], lhsT=g2_bf[:, jc], rhs=sg_bf[:, :nb_], start=True, stop=True), reads=[g2_bf, sg_bf], writes=[ps])
                    P.op("act", lambda e, ps=ps, j=j: e.activation(out=gT[:, j, :nb_], in_=ps[:, :nb_], func=AF.Copy), reads=[ps], writes=[gT])
                    P.op("dve", lambda e: e.tensor_tensor_scan(out=C_[:, :nb_], data0=rmask[:, :nb_], data1=A_[:, :nb_], initial=0.0, op0=ALU.mult, op1=ALU.add),
                         reads=[rmask, A_], writes=[C_])
                    P.op("dve", lambda e: e.tensor_tensor(out=D_[:, :nb_], in0=C_[:, :nb_], in1=A_[:, :nb_], op=ALU.subtract), reads=[C_, A_], writes=[D_])
                    P.op("dve", lambda e: e.tensor_tensor(out=v3(A_), in0=v3(C_), in1=v3(C_)[:, :, 63:64].broadcast_to([128, nt, 128]), op=ALU.subtract), reads=[C_], writes=[A_])
                    P.op("dve", lambda e: e.tensor_tensor(out=v3(E_), in0=v3(D_), in1=v3(C_)[:, :, 63:64].broadcast_to([128, nt, 128]), op=ALU.subtract), reads=[C_, D_], writes=[E_])
                    P.op("act", lambda e, j=j: e.activation(out=emt[:, j, :nt], in_=v3(C_)[:, :, 63], func=AF.Exp, scale=C1), reads=[C_], writes=[emt])
                    P.op("act", lambda e, j=j: e.activation(out=eet[:, j, :nt], in_=v3(A_)[:, :, 127], func=AF.Exp, scale=C1), reads=[A_], writes=[eet])
                    P.op("act", lambda e, j=j: e.activation(out=emet[:, j, :nt], in_=v3(C_)[:, :, 127], func=AF.Exp, scale=C1), reads=[C_], writes=[emet])
                    P.op("act", lambda e: e.activation(out=D_[:, :nb_], in_=A_[:, :nb_], func=AF.Exp, scale=C1), reads=[A_], writes=[D_])
                    P.op("act", lambda e: e.activation(out=F_[:, :nb_], in_=A_[:, :nb_], func=AF.Exp, scale=-C1), reads=[A_], writes=[F_])
                    P.op("act", lambda e: e.activation(out=G_[:, :nb_], in_=E_[:, :nb_], func=AF.Exp, scale=C1), reads=[E_], writes=[G_])
                    P.op("act", lambda e, j=j: e.activation(out=kk2_bf[:, :nb_], in_=k_ap(), func=AF.Square, scale=pp[:, PP_KK + j:PP_KK + j + 1]), reads=[rp, pp], writes=[kk2_bf])
                    ps = bank()
                    P.op("pe", lambda e, ps=ps: e.matmul(ps[:, :nb_], lhsT=bones_bf(), rhs=kk2_bf[:, :nb_], start=True, stop=True), reads=[cstb, kk2_bf], writes=[ps])
                    P.op("act", lambda e, ps=ps: e.activation(out=C_[:, :nb_], in_=ps[:, :nb_], func=AF.Ln, bias=eps18[:, 0:1]), reads=[ps, eps18], writes=[C_])
                    P.op("act", lambda e: e.activation(out=C_[:, :nb_], in_=C_[:, :nb_], func=AF.Exp, scale=-0.5), reads=[C_], writes=[C_])
                    P.op("dve", lambda e, j=j: e.scalar_tensor_tensor(out=A_[:, :nb_], in0=k_ap(), scalar=pp[:, PP_KK + j:PP_KK + j + 1], in1=C_[:, :nb_], op0=ALU.mult, op1=ALU.mult),
                         reads=[rp, pp, C_], writes=[A_])
                    P.op("dve", lambda e, j=j: e.tensor_scalar(out=E_[:, :nb_], in0=B_[:, :nb_], scalar1=pp[:, PP_KA + j:PP_KA + j + 1], scalar2=omka[:, j:j + 1], op0=ALU.mult, op1=ALU.add),
                         reads=[B_, pp, omka], writes=[E_])
                    P.op("dve", lambda e: e.tensor_tensor(out=E_[:, :nb_], in0=E_[:, :nb_], in1=k_ap(), op=ALU.mult), reads=[E_, rp], writes=[E_])
                    P.op("dve", lambda e: e.tensor_tensor(out=C_[:, :nb_], in0=A_[:, :nb_], in1=B_[:, :nb_], op=ALU.mult), reads=[A_, B_], writes=[C_])
                    P.op("dve", lambda e, j=j: e.scalar_tensor_tensor(out=AR[:, j, :nt, 0, :], in0=v3(A_), scalar=-1.0, in1=v3(G_), op0=ALU.mult, op1=ALU.mult), reads=[A_, G_], writes=[AR])
                    P.op("dve", lambda e, j=j: e.tensor_tensor(out=AR[:, j, :nt, 1, :], in0=rp[:, j, :nb_].rearrange("p (t s) -> p t s", s=128), in1=v3(D_), op=ALU.mult), reads=[rp, D_], writes=[AR])
                    P.op("dve", lambda e, j=j: e.tensor_tensor(out=kt_bf[:, j, :nb_], in0=E_[:, :nb_], in1=F_[:, :nb_], op=ALU.mult), reads=[E_, F_], writes=[kt_bf])
                    P.op("dve", lambda e, j=j: e.tensor_tensor(out=bt_bf[:, j, :nb_], in0=C_[:, :nb_], in1=F_[:, :nb_], op=ALU.mult), reads=[C_, F_], writes=[bt_bf])
                    P.op("dve", lambda e: e.tensor_tensor(out=rk_bf[:, :nb_], in0=r_ap(), in1=E_[:, :nb_], op=ALU.mult), reads=[rp, E_], writes=[rk_bf])
                    ps = bank()
                    P.op("pe", lambda e, ps=ps, j=j: e.matmul(ps[:, :nb_], lhsT=rkones[:, j, :], rhs=rk_bf[:, :nb_], start=True, stop=True), reads=[rkones, rk_bf], writes=[ps])
                    P.op("dve", lambda e, ps=ps, j=j: e.tensor_tensor(out=bonT[:, j, :nb_], in0=ps[:, :nb_], in1=v_ap(), op=ALU.mult), reads=[ps, rp], writes=[bonT])
                    for tt in range(nt):
                        tsl2 = slice(tt * 128, (tt + 1) * 128)
                        P.op("pe", lambda e, j=j, tsl2=tsl2: e.transpose(pbT[:, 0:128], bt_bf[:, j, tsl2], ident_bf()), reads=[bt_bf, cstb], writes=[pbT])
                        P.op("pe", lambda e, j=j, tsl2=tsl2: e.transpose(pbT[:, 128:256], kt_bf[:, j, tsl2], ident_bf()), reads=[kt_bf, cstb], writes=[pbT])
                        P.op("pe", lambda e, j=j, tsl2=tsl2: e.transpose(pbT[:, 256:384], rp[:, 8 + j, tsl2], ident_bf()), reads=[rp, cstb], writes=[pbT])
                        P.op("act", lambda e, j=j, tt=tt: e.activation(out=tokm[:, tt, j, :, :], in_=pbT[:, 0:384].rearrange("p (a b) -> p a b", a=3), func=AF.Copy), reads=[pbT], writes=[tokm])
                        yield

                for tt in range(nt):
                    tsl2 = slice(tt * 128, (tt + 1) * 128)
                    for j in range(4):
                        P.op("dve", lambda e, j=j, tt=tt: e.tensor_scalar(out=Hb[j][:, :], in0=Hs32[j][:, :], scalar1=emt[:, j, tt:tt + 1], scalar2=None, op0=ALU.mult), reads=[Hs32[j], emt], writes=[Hb[j]])
                        P.op("pool", lambda e, j=j, tt=tt: e.tensor_scalar(out=Hpe[j][:, :], in0=Hs32[j][:, :], scalar1=emet[:, j, tt:tt + 1], scalar2=None, op0=ALU.mult), reads=[Hs32[j], emet], writes=[Hpe[j]])
                    def gram(lhs_buf, lhs_fn, rhs_fn, rhs_bufs, mask_fn, dst, eng):
                        for half in range(2):
                            ps = bank()
                            for hh in range(4):
                                j, hp = hh, half
                                bp = slice(hp * 64, hp * 64 + 64)
                                P.op("pe", lambda e, ps=ps, hh=hh, j=j, bp=bp: e.matmul(ps[:, hh * 128:(hh + 1) * 128], lhsT=lhs_fn(j, bp), rhs=rhs_fn(j, bp), start=True, stop=True),
                                     reads=lhs_buf + rhs_bufs, writes=[ps])
                            P.op(eng, lambda e, ps=ps, half=half: e.tensor_tensor(out=dst[:, half * 4:half * 4 + 4, :], in0=ps[:, :].rearrange("p (h s) -> p h s", h=4),
                                                                                in1=mask_fn().unsqueeze(1).broadcast_to([128, 4, 128]), op=ALU.mult), reads=[ps, cst], writes=[dst])
                    bt_l = lambda j, bp: bt_bf[bp, j, tsl2]
                    kt_l = lambda j, bp: kt_bf[bp, j, tsl2]
                    a_r = lambda j, bp: AR[bp, j, tt, 0, :]
                    r_r = lambda j, bp: AR[bp, j, tt, 1, :]
                    gram([bt_bf], bt_l, a_r, [AR], msu, Qp[0], "dve")
                    yield
                    gram([AR], a_r, bt_l, [bt_bf], msl, Np[0], "dve")
                    gram([kt_bf], kt_l, a_r, [AR], msu, AKm, "dve")
                    yield
                    gram([bt_bf], bt_l, r_r, [AR], miu, ABm, "dve")
                    gram([kt_bf], kt_l, r_r, [AR], miu, RKm, "dve")
                    yield
                    for jp in range(4):
                        P.op("pool", lambda e, jp=jp: e.tensor_tensor(out=Sb[0][:, 2 * jp:2 * jp + 2, :], in0=Qp[0][:, 2 * jp:2 * jp + 2, :], in1=ident2[:, :, :], op=ALU.add),
                             reads=[Qp[0], ident2], writes=[Sb[0]])
                    qc, ncur, sc = 0, 0, 0
                    for lev in range(6):
                        qn, nn, sn = 1 - qc, 1 - ncur, 1 - sc
                        lastlev = (lev == 5)
                        if not lastlev:
                            for half in range(2):
                                ps = bank()
                                for hh in range(4):
                                    h8 = half * 4 + hh
                                    P.op("pe", lambda e, ps=ps, hh=hh, h8=h8, qc=qc, ncur=ncur: e.matmul(ps[:, hh * 128:(hh + 1) * 128], lhsT=Np[ncur][:, h8, :], rhs=Qp[qc][:, h8, :], start=True, stop=True),
                                         reads=[Np[ncur], Qp[qc]], writes=[ps])
                                P.op("act", lambda e, ps=ps, half=half, qn=qn: e.activation(out=Qp[qn][:, half * 4:half * 4 + 4, :], in_=ps[:, :].rearrange("p (h s) -> p h s", h=4), func=AF.Copy),
                                     reads=[ps], writes=[Qp[qn]])
                        for half in range(2):
                            ps = bank()
                            for hh in range(4):
                                h8 = half * 4 + hh
                                P.op("pe", lambda e, ps=ps, hh=hh, h8=h8, qc=qc, ncur=ncur: e.matmul(ps[:, hh * 128:(hh + 1) * 128], lhsT=Qp[qc][:, h8, :], rhs=Np[ncur][:, h8, :], start=True, stop=True),
                                     reads=[Np[ncur], Qp[qc]], writes=[ps])
                            P.op("act", lambda e, ps=ps, half=half, nn=nn: e.activation(out=Np[nn][:, half * 4:half * 4 + 4, :], in_=ps[:, :].rearrange("p (h s) -> p h s", h=4), func=AF.Copy),
                                 reads=[ps], writes=[Np[nn]])
                        for half in range(2):
                            ps = bank()
                            for hh in range(4):
                                h8 = half * 4 + hh
                                P.op("pe", lambda e, ps=ps, hh=hh, h8=h8, nn=nn, sc=sc: e.matmul(ps[:, hh * 128:(hh + 1) * 128], lhsT=Np[nn][:, h8, :], rhs=Sb[sc][:, h8, :], start=True, stop=True),
                                     reads=[Np[nn], Sb[sc]], writes=[ps])
                            P.op("dve", lambda e, ps=ps, half=half, sc=sc, sn=sn: e.tensor_tensor(out=Sb[sn][:, half * 4:half * 4 + 4, :], in0=ps[:, :].rearrange("p (h s) -> p h s", h=4),
                                                                                               in1=Sb[sc][:, half * 4:half * 4 + 4, :], op=ALU.add), reads=[ps, Sb[sc]], writes=[Sb[sn]])
                        qc, ncur, sc = qn, nn, sn
                        yield
                    Sf = Sb[sc]
                    vtok = lambda j, hp: tokm[:, tt, j, 2, hp * 64:hp * 64 + 64]
                    nat4 = lambda buf, hp: buf[:, :].rearrange("p (j h v) -> p j h v", j=4, h=2)[:, :, hp, :]
                    psx = [bank(), bank()]
                    for h8 in range(8):
                        j, hp = h8 // 2, h8 % 2
                        hidx = hp * 4 + j
                        bp = slice(hp * 64, hp * 64 + 64)
                        ps = psx[hp]
                        P.op("pe", lambda e, ps=ps, j=j, bp=bp: e.matmul(ps[:, j * 64:(j + 1) * 64], lhsT=AR[bp, j, tt, 0, :], rhs=Hb[j][bp, :], start=True, stop=False), reads=[AR, Hb[j]], writes=[ps])
                        P.op("pe", lambda e, ps=ps, j=j, hp=hp, hidx=hidx: e.matmul(ps[:, j * 64:(j + 1) * 64], lhsT=AKm[:, hidx, :], rhs=vtok(j, hp), start=False, stop=True), reads=[AKm, tokm], writes=[ps])
                    for hp in range(2):
                        P.op("act", lambda e, hp=hp: e.activation(out=nat4(Xs, hp), in_=psx[hp][:, 0:256].rearrange("p (j v) -> p j v", j=4), func=AF.Copy), reads=[psx[hp]], writes=[Xs])
                    yield
                    ps = bank()
                    for h8 in range(8):
                        j, hp = h8 // 2, h8 % 2
                        hidx = hp * 4 + j
                        P.op("pe", lambda e, ps=ps, h8=h8, hidx=hidx: e.matmul(ps[:, h8 * 64:(h8 + 1) * 64], lhsT=Sf[:, hidx, :], rhs=Xs[:, h8 * 64:(h8 + 1) * 64], start=True, stop=True), reads=[Sf, Xs], writes=[ps])
                    P.op("act", lambda e, ps=ps: e.activation(out=Us[:, :], in_=ps[:, :], func=AF.Copy), reads=[ps], writes=[Us])
                    yield
                    psy = [bank(), bank()]
                    for h8 in range(8):
                        j, hp = h8 // 2, h8 % 2
                        hidx = hp * 4 + j
                        bp = slice(hp * 64, hp * 64 + 64)
                        ps = psy[hp]
                        P.op("pe", lambda e, ps=ps, j=j, bp=bp: e.matmul(ps[:, j * 64:(j + 1) * 64], lhsT=AR[bp, j, tt, 1, :], rhs=Hb[j][bp, :], start=True, stop=False), reads=[AR, Hb[j]], writes=[ps])
                        P.op("pe", lambda e, ps=ps, j=j, h8=h8, hidx=hidx: e.matmul(ps[:, j * 64:(j + 1) * 64], lhsT=ABm[:, hidx, :], rhs=Us[:, h8 * 64:(h8 + 1) * 64], start=False, stop=False), reads=[ABm, Us], writes=[ps])
                        P.op("pe", lambda e, ps=ps, j=j, hp=hp, hidx=hidx: e.matmul(ps[:, j * 64:(j + 1) * 64], lhsT=RKm[:, hidx, :], rhs=vtok(j, hp), start=False, stop=True), reads=[RKm, tokm], writes=[ps])
                    for hp in range(2):
                        P.op("act", lambda e, hp=hp: e.activation(out=nat4(Ysb, hp), in_=psy[hp][:, 0:256].rearrange("p (j v) -> p j v", j=4), func=AF.Copy), reads=[psy[hp]], writes=[Ysb])
                    yield
                    ps = bank()
                    for j in range(4):
                        P.op("pe", lambda e, ps=ps, j=j: e.matmul(ps[:, j * 128:(j + 1) * 128], lhsT=tokm[:, tt, j, 0, :], rhs=Us[:, j * 128:(j + 1) * 128], start=True, stop=False), reads=[tokm, Us], writes=[ps])
                        P.op("pe", lambda e, ps=ps, j=j: e.matmul(ps[:, j * 128:(j + 1) * 128], lhsT=tokm[:, tt, j, 1, :], rhs=tokm[:, tt, j, 2, :], start=False, stop=True), reads=[tokm], writes=[ps])
                    for j in range(4):
                        for hp in range(2):
                            bp = slice(hp * 64, hp * 64 + 64)
                            P.op("dve", lambda e, ps=ps, j=j, hp=hp, bp=bp: e.scalar_tensor_tensor(out=Hs32[j][bp, :], in0=ps[bp, j * 128 + hp * 64:j * 128 + hp * 64 + 64], scalar=eet[bp, j, tt:tt + 1],
                                                                                               in1=Hpe[j][bp, :], op0=ALU.mult, op1=ALU.add), reads=[ps, eet, Hpe[j]], writes=[Hs32[j]])
                    yield
                    y3 = lambda buf: buf[:, :].rearrange("p (h v) -> p h v", h=8)
                    bc8 = lambda ap: ap.unsqueeze(2).broadcast_to([128, 8, 64])
                    P.op("dve", lambda e: e.tensor_reduce(out=yst[:, 0:8], in_=y3(Ysb), axis=AX.X, op=ALU.add), reads=[Ysb], writes=[yst])
                    P.op("act", lambda e: e.activation(out=ysq[:, :], in_=Ysb[:, :], func=AF.Square), reads=[Ysb], writes=[ysq])
                    P.op("dve", lambda e: e.tensor_reduce(out=yst[:, 8:16], in_=y3(ysq), axis=AX.X, op=ALU.add), reads=[ysq], writes=[yst])
                    P.op("dve", lambda e: e.tensor_scalar(out=yst[:, 16:24], in0=yst[:, 0:8], scalar1=1.0 / 64, scalar2=None, op0=ALU.mult), reads=[yst], writes=[yst])
                    P.op("dve", lambda e: e.tensor_tensor(out=yst[:, 24:32], in0=yst[:, 16:24], in1=yst[:, 16:24], op=ALU.mult), reads=[yst], writes=[yst])
                    P.op("dve", lambda e: e.scalar_tensor_tensor(out=yst[:, 32:40], in0=yst[:, 8:16], scalar=1.0 / 64, in1=yst[:, 24:32], op0=ALU.mult, op1=ALU.subtract), reads=[yst], writes=[yst])
                    P.op("act", lambda e: e.activation(out=yst[:, 24:32], in_=yst[:, 32:40], func=AF.Ln, bias=epsg[:, 0:1]), reads=[yst, epsg], writes=[yst])
                    P.op("act", lambda e: e.activation(out=yst[:, 32:40], in_=yst[:, 24:32], func=AF.Exp, scale=-0.5), reads=[yst], writes=[yst])
                    P.op("dve", lambda e: e.tensor_tensor(out=y3(yc), in0=y3(Ysb), in1=bc8(yst[:, 16:24]), op=ALU.subtract), reads=[Ysb, yst], writes=[yc])
                    P.op("dve", lambda e: e.tensor_tensor(out=y3(yc), in0=y3(yc), in1=bc8(yst[:, 32:40]), op=ALU.mult), reads=[yc, yst], writes=[yc])
                    P.op("pool", lambda e: e.tensor_tensor(out=yc[:, :], in0=yc[:, :], in1=pp[:, PP_LNW:PP_LNW + 512], op=ALU.mult), reads=[yc, pp], writes=[yc])
                    P.op("pool", lambda e: e.tensor_tensor(out=ynb[:, :], in0=yc[:, :], in1=pp[:, PP_LNB:PP_LNB + 512], op=ALU.add), reads=[yc, pp], writes=[ynb])
                    for j in range(4):
                        P.op("pe", lambda e, j=j: e.transpose(pbT[:, j * 128:(j + 1) * 128], ynb[:, j * 128:(j + 1) * 128], ident_bf()), reads=[ynb, cstb], writes=[pbT])
                    P.op("dve", lambda e: e.tensor_tensor(out=yc[:, :].rearrange("p (j s) -> p j s", j=4), in0=pbT[:, 0:512].rearrange("p (j s) -> p j s", j=4), in1=bonT[:, :, tsl2], op=ALU.add),
                         reads=[pbT, bonT], writes=[yc])
                    P.op("dve", lambda e: e.tensor_tensor(out=brR[:, :, tsl2], in0=yc[:, :].rearrange("p (j s) -> p j s", j=4), in1=gT[:, :, tsl2], op=ALU.mult), reads=[yc, gT], writes=[brR])

                yield
            P.mute = False
            cur_banks[0] = ATTB
            ga, gb = attn_gen(), rw_gen()
            na = 8 * (b + 1) * 1.0
            nbk = 4.0 * nt + nt * 16.0
            da = db = 0
            alive_a = alive_b = True
            while alive_a or alive_b:
                pick_a = alive_a and (not alive_b or da / na <= db / nbk)
                if pick_a:
                    try:
                        next(ga)
                        da += 1
                    except StopIteration:
                        alive_a = False
                else:
                    try:
                        next(gb)
                        db += 1
                    except StopIteration:
                        alive_b = False
            cur_banks[0] = ALLB
            P.mute = ("merge" in SKIP)
            brs = [brP, brA, brR]
            for nbi in range(3):
                P.dma("sp", "wbr", lambda e, nbi=nbi: e.dma_start(out=wbr[:, :, :], in_=wbr_bf[l][nbi * 512:(nbi + 1) * 512, :].rearrange("(c p) d -> p c d", p=128)),
                      reads=[DR("cv%d" % l)], writes=[wbr])
                for dm in range(8):
                    ps = bank()
                    for c in range(4):
                        P.op("pe", lambda e, ps=ps, c=c, dm=dm, nbi=nbi: e.matmul(ps[:, :nb_], lhsT=wbr[:, c, dm * 128:(dm + 1) * 128], rhs=brs[nbi][:, c, :nb_], start=(c == 0), stop=(c == 3)),
                             reads=[wbr, brs[nbi]], writes=[ps])
                    if nbi == 0:
                        P.op("dve", lambda e, ps=ps, dm=dm, nbi=nbi: e.tensor_tensor(out=merged[:, dm, :nb_], in0=ps[:, :nb_], in1=gates[:, nbi * 8 + dm, :nb_], op=ALU.mult),
                             reads=[ps, gates], writes=[merged])
                    else:
                        mt = mtmp[dm % 2]
                        P.op("dve", lambda e, ps=ps, dm=dm, nbi=nbi, mt=mt: e.tensor_tensor(out=mt[:, :nb_], in0=ps[:, :nb_], in1=gates[:, nbi * 8 + dm, :nb_], op=ALU.mult),
                             reads=[ps, gates], writes=[mt])
                        P.op("pool", lambda e, dm=dm, mt=mt: e.tensor_tensor(out=merged[:, dm, :nb_], in0=merged[:, dm, :nb_], in1=mt[:, :nb_], op=ALU.add),
                             reads=[merged, mt], writes=[merged])
            for gg in range(2):
                w = load_wg(wout_bf[l], gg * 512, 512, "cv%d" % l)
                for m in range(4):
                    dm = gg * 4 + m
                    ps = bank()
                    for c in range(8):
                        P.op("pe", lambda e, ps=ps, c=c, m=m, w=w: e.matmul(ps[:, :nb_], lhsT=w[:, c, m * 128:(m + 1) * 128], rhs=merged[:, c, :nb_], start=(c == 0), stop=(c == 7)),
                             reads=[w, merged], writes=[ps])
                    P.op("dve", lambda e, ps=ps, dm=dm: e.tensor_tensor(out=xt[:, dm, :nb_], in0=xt[:, dm, :nb_], in1=ps[:, :nb_], op=ALU.add), reads=[xt, ps], writes=[xt])
            if b == 0:
                P.op("pool", lambda e: e.memset(xt[:, :, 0:NPAD], 0.0), writes=[xt])

            P.mute = ("ffn" in SKIP)
            norm_stage(PP_GFFN, nb_)
            for gg in range(11):
                w = load_wg(up_bf[l], gg * 512, 512, "cv%d" % l)
                for m in range(4):
                    mi = gg * 4 + m
                    ps = bank()
                    proj_fm(w, m, nb_, ps)
                    cw = lambda i, mi=mi: pp[:, PP_CONV + i * 44 + mi:PP_CONV + i * 44 + mi + 1]
                    cv = scr[mi % 4]
                    P.op("act", lambda e, ps=ps, cv=cv, cw=cw: e.activation(out=cv[:, :nb_], in_=ps[:, :nb_], func=AF.Identity, scale=cw(2)), reads=[ps, pp], writes=[cv])
                    P.op("dve", lambda e, ps=ps, cv=cv, cw=cw: e.scalar_tensor_tensor(out=cv[:, 1:nb_], in0=ps[:, 0:nb_ - 1], scalar=cw(1), in1=cv[:, 1:nb_], op0=ALU.mult, op1=ALU.add),
                         reads=[ps, pp, cv], writes=[cv])
                    P.op("dve", lambda e, ps=ps, cv=cv, cw=cw: e.scalar_tensor_tensor(out=cv[:, 2:nb_], in0=ps[:, 0:nb_ - 2], scalar=cw(0), in1=cv[:, 2:nb_], op0=ALU.mult, op1=ALU.add),
                         reads=[ps, pp, cv], writes=[cv])
                    P.op("dve", lambda e, cv=cv, cw=cw, mi=mi: e.scalar_tensor_tensor(out=cv[:, 0:1], in0=chalo[:, mi, 1:2], scalar=cw(1), in1=cv[:, 0:1], op0=ALU.mult, op1=ALU.add),
                         reads=[chalo, pp, cv], writes=[cv])
                    P.op("dve", lambda e, cv=cv, cw=cw, mi=mi: e.scalar_tensor_tensor(out=cv[:, 0:2], in0=chalo[:, mi, 0:2], scalar=cw(0), in1=cv[:, 0:2], op0=ALU.mult, op1=ALU.add),
                         reads=[chalo, pp, cv], writes=[cv])
                    P.op("dve", lambda e, ps=ps, mi=mi: e.tensor_copy(out=chalo[:, mi, :], in_=ps[:, nb_ - 2:nb_]), reads=[ps], writes=[chalo])
                    if mi < 22:
                        P.op("act", lambda e, cv=cv, mi=mi: e.activation(out=cgs[:, mi, :nb_], in_=cv[:, :nb_], func=AF.Silu), reads=[cv], writes=[cgs])
                    else:
                        P.op("pool", lambda e, cv=cv, mi=mi: e.tensor_tensor(out=fact[:, mi - 22, :nb_], in0=cgs[:, mi - 22, :nb_], in1=cv[:, :nb_], op=ALU.mult), reads=[cgs, cv], writes=[fact])
            for dm in range(8):
                dw = dnw[dm % 2]
                P.dma("sp", dw.res.name, lambda e, dw=dw, dm=dm: e.dma_start(out=dw[:, :, :], in_=dn_bf[l][:, dm * 128:(dm + 1) * 128].rearrange("(c p) d -> p c d", p=128)),
                      reads=[DR("cv%d" % l)], writes=[dw])
                ps = bank()
                for c in range(22):
                    P.op("pe", lambda e, ps=ps, c=c, dw=dw: e.matmul(ps[:, :nb_], lhsT=dw[:, c, :], rhs=fact[:, c, :nb_], start=(c == 0), stop=(c == 21)),
                         reads=[dw, fact], writes=[ps])
                P.op("dve", lambda e, ps=ps, dm=dm: e.tensor_tensor(out=xt[:, dm, :nb_], in0=xt[:, dm, :nb_], in1=ps[:, :nb_], op=ALU.add), reads=[xt, ps], writes=[xt])
            if b == 0:
                P.op("pool", lambda e: e.memset(xt[:, :, 0:NPAD], 0.0), writes=[xt])
            P.mute = False
            if not last:
                P.dma("pool", "xst", lambda e: e.dma_start(out=xdst[:, tsl].rearrange("(c p) t -> p c t", p=128), in_=xt[:, :, :nb_]),
                      reads=[xt], writes=[DR("xs%d_%d" % (l + 1, b))])
            else:
                P.dma("pool", "xst", lambda e: e.dma_start(out=xTo[:, tsl].rearrange("(c p) t -> p c t", p=128), in_=xt[:, :, :nb_]),
                      reads=[xt], writes=[DR("xo")])
                norm_stage(PP_GFIN, nb_)
                for c in range(8):
                    P.op("dve", lambda e, c=c: e.scalar_tensor_tensor(out=xt[:, c, :nb_], in0=xt[:, c, :nb_], scalar=pp[:, PP_GFIN + c:PP_GFIN + c + 1], in1=rstd[:, :nb_],
                                                                      op0=ALU.mult, op1=ALU.mult), reads=[xt, pp, rstd], writes=[xt])
                P.dma("pool", "xst", lambda e: e.dma_start(out=outT[:, tsl].rearrange("(c p) t -> p c t", p=128), in_=xt[:, :, :nb_]),
                      reads=[xt], writes=[DR("out")])

    for l in range(NL):
        src = xT_in if l == 0 else xbufs[l % 2]
        layer(l, src, xbufs[(l + 1) % 2], last=(l == NL - 1))
    if os.environ.get("KDBG"):
        print("op counts", {e: len(P.q[e]) for e in P.ENG}, "sems", len(P.sems))
    P.emit(final_res=[DR("out"), DR("xo")])
    return nc


def _host_consts(Lp):
    cst = np.zeros((128, NCS), np.float32)
    idx = np.arange(128)
    cst[:, CS_ID:CS_ID + 128] = np.eye(128, dtype=np.float32)
    cst[:, CS_MSU:CS_MSU + 128] = (idx[:, None] < idx[None, :])
    cst[:, CS_MIU:CS_MIU + 128] = (idx[:, None] <= idx[None, :])
    cst[:, CS_MSL:CS_MSL + 128] = (idx[:, None] > idx[None, :])
    cst[:, CS_BO:CS_BO + 128] = ((idx[:, None] // 64) == (idx[None, :] // 64))
    perm = np.zeros((128, 128), np.float32)
    for m in range(128):
        d = m % 64
        if d < 8:
            perm[m + 8, m] = 1.0
        elif d < 16:
            perm[m - 8, m] = 1.0
    cst[:, CS_PERM:CS_PERM + 128] = perm
    cst[:, CS_DM:CS_DM + 128] = ((idx[:, None] // 64) <= (idx[None, :] // 64))
    for g, w in enumerate((2, 4, 8, 16)):
        p = idx - NPAD
        cnt = np.where(p >= 0, np.minimum(p + 1, w), w).astype(np.float32)
        cst[:, CS_IC + g * 128:CS_IC + (g + 1) * 128] = (1.0 / cnt)[None, :]
    pos = (np.arange(Lp) - NPAD).astype(np.float32)
    inv = (np.float32(500000.0) ** (-np.arange(0, 16, 2, dtype=np.float32) / np.float32(16))).astype(np.float32)
    ang = (pos[:, None] * inv[None, :]).astype(np.float32)
    cos = np.cos(ang).astype(np.float32).T
    sin = np.sin(ang).astype(np.float32).T
    rc = np.ones((128, Lp), np.float32)
    rs = np.zeros((128, Lp), np.float32)
    for p in range(128):
        d = p % 64
        if d < 8:
            rc[p] = cos[d]
            rs[p] = -sin[d]
        elif d < 16:
            rc[p] = cos[d - 8]
            rs[p] = sin[d - 8]
    return cst, rc, rs


def _pack_pp(l, norm_mix, norm_ffn, norm_final, pool_scale, rw_mu, rw_w0, rw_a0, rw_k_k, rw_k_a, rw_r_k,
             ffn_conv, da_subln, rw_lnx_w, rw_lnx_b, da_lambda):
    pp = np.zeros((128, NPP), np.float32)
    col = lambda v: np.asarray(v, np.float32).reshape(-1, 128).T
    pp[:, PP_GMIX:PP_GMIX + 8] = col(norm_mix[l])
    pp[:, PP_GFFN:PP_GFFN + 8] = col(norm_ffn[l])
    pp[:, PP_PSC:PP_PSC + 4] = col(pool_scale[l])
    pp[:, PP_MU:PP_MU + 14] = col(rw_mu[l])
    pp[:, PP_W0:PP_W0 + 4] = col(rw_w0[l])
    pp[:, PP_A0:PP_A0 + 4] = col(rw_a0[l])
    pp[:, PP_KK:PP_KK + 4] = col(rw_k_k[l])
    pp[:, PP_KA:PP_KA + 4] = col(rw_k_a[l])
    pp[:, PP_RK:PP_RK + 4] = col(rw_r_k[l].reshape(-1))
    for i in range(3):
        pp[:, PP_CONV + i * 44:PP_CONV + (i + 1) * 44] = col(ffn_conv[l, i])
    pp[:, PP_SUBLN:PP_SUBLN + 128] = np.asarray(da_subln[l], np.float32)[None, :]
    pp[:, PP_LNW:PP_LNW + 512] = np.asarray(rw_lnx_w[l], np.float32)[None, :]
    pp[:, PP_LNB:PP_LNB + 512] = np.asarray(rw_lnx_b[l], np.float32)[None, :]
    pp[:, PP_LAM:PP_LAM + 256] = np.asarray(da_lambda[l], np.float32).reshape(1, 256)
    pp[:, PP_GFIN:PP_GFIN + 8] = col(norm_final)
    lam_init = 0.8 - 0.6 * math.exp(-0.3 * l)
    pp[:, PP_OML] = 1.0 - lam_init
    pp[:, PP_NLI] = -lam_init
    return pp


_NC_CACHE = {}


def _run(xT_list, NT, layers, lam_ids, weights, cst, rc, rs):
    key = (NT, len(layers))
    if key not in _NC_CACHE:
        _NC_CACHE[key] = build(NT, len(layers))
    nc = _NC_CACHE[key]
    W = weights
    ls = list(layers)
    f = lambda a: np.ascontiguousarray(np.asarray(a, np.float32))
    shared = {
        "w_in": f(W["w_in"][ls]),
        "w_branch": f(W["w_branch"][ls].reshape(len(ls), 1536, D)),
        "w_out": f(W["w_out"][ls]),
        "ffn_up": f(W["ffn_up"][ls]),
        "ffn_down": f(W["ffn_down"][ls]),
        "pool_w": f(W["pool_w"][ls]),
        "rw_w2": f(W["rw_w2"][ls]),
        "rw_a2": f(W["rw_a2"][ls]),
        "rw_g2": f(W["rw_g2"][ls]),
        "pp": f(np.stack([_pack_pp(l, W["norm_mix"], W["norm_ffn"], W["norm_final"], W["pool_scale"], W["rw_mu"], W["rw_w0"], W["rw_a0"],
                                   W["rw_k_k"], W["rw_k_a"], W["rw_r_k"], W["ffn_conv"], W["da_subln"], W["rw_lnx_w"], W["rw_lnx_b"],
                                   W["da_lambda"]) for l in ls])),
        "cst": cst, "ropeC": rc, "ropeS": rs,
    }
    in_maps = []
    for xT in xT_list:
        m = dict(shared)
        m["xT"] = xT
        in_maps.append(m)
    res = run_bass_kernel_spmd(nc, in_maps, core_ids=list(range(len(in_maps))))
    return [(r["outT"], r["xTo"]) for r in res.results]


def kernel(x, meta_tokens, **W):
    x = np.asarray(x, np.float32)
    B, Lq, _ = x.shape
    L = NPAD + NMETA + Lq
    NT = (L + 127) // 128
    Lp = NT * 128
    cst, rc, rs = _host_consts(Lp)
    W = {k: np.asarray(v) for k, v in W.items()}
    NL = W["w_in"].shape[0]
    xTs = []
    for c in range(8):
        b = c % B
        xp = np.zeros((Lp, D), np.float32)
        xp[NPAD:NPAD + NMETA] = np.asarray(meta_tokens, np.float32)
        xp[NPAD + NMETA:NPAD + NMETA + Lq] = x[b]
        xTs.append(np.ascontiguousarray(xp.T))
    if FUSED:
        outs = _run(xTs, NT, list(range(NL)), list(range(NL)), W, cst, rc, rs)
    else:
        for l in range(NL):
            outs = _run(xTs, NT, [l], [l], W, cst, rc, rs)
            xTs = [np.ascontiguousarray(o[1]) for o in outs]
    out = np.stack([np.ascontiguousarray(outs[b][0].T[NPAD + NMETA:NPAD + NMETA + Lq]) for b in range(B)])
    return out.astype(np.float32)
```

```python
import math
import os
import numpy as np
import concourse.bass as bass
import concourse.mybir as mybir
from concourse.bass_utils import run_bass_kernel_spmd

F32 = mybir.dt.float32
BF16 = mybir.dt.bfloat16
AF = mybir.ActivationFunctionType
ALU = mybir.AluOpType
AX = mybir.AxisListType

D = 1024
INW = 6912
DFF = 2816
NPAD = 112
NMETA = 16
SEQ = 8192
C1 = -math.exp(-0.5)
SAME_ENGINE_SYNC = os.environ.get("KSES", "1") == "1"
FUSED = True
SKIP = set(os.environ.get("KSKIP", "").split(","))

PP_GMIX, PP_GFFN, PP_PSC, PP_MU, PP_W0, PP_A0, PP_KK, PP_KA, PP_RK = 0, 8, 16, 20, 34, 38, 42, 46, 50
PP_CONV = 54
PP_SUBLN = 186
PP_LNW = 314
PP_LNB = 826
PP_LAM = 1338
PP_GFIN = 1594
PP_OML = 1602
PP_NLI = 1603
NPP = 1604
CS_ID, CS_MSU, CS_MIU, CS_MSL, CS_BO, CS_PERM, CS_DM, CS_IC = 0, 128, 256, 384, 512, 640, 768, 896
NCS = 896 + 512


class Res:
    __slots__ = ("name", "w", "r")

    def __init__(self, name):
        self.name = name
        self.w = None
        self.r = {}


class Buf:
    def __init__(self, t, name):
        self.t = t
        self.res = Res(name)

    def __getitem__(self, k):
        return self.t[k]


def _res(x):
    return x.res if isinstance(x, Buf) else x


class _Rec:
    def __init__(self):
        self.call = None

    def __getattr__(self, name):
        def f(*a, **kw):
            self.call = (name, a, kw)
            return None
        return f


class Prog:
    ENG = ["pe", "act", "dve", "pool", "sp"]

    def __init__(self, nc):
        self.nc = nc
        self.q = {e: [] for e in self.ENG}
        self.cnt = {e: 0 for e in self.ENG}
        self.sems = {}
        self.dcnt = {}
        self.waited = {e: {} for e in self.ENG}
        self.mute = False
        for e in self.ENG:
            self.sems[e] = nc.alloc_semaphore("sem_" + e)

    def _deps(self, eng, reads, writes):
        deps = {}

        def add(ev):
            if ev is None:
                return
            k, v = ev
            if deps.get(k, 0) < v:
                deps[k] = v
        for r in reads:
            add(r.w)
        for w in writes:
            add(w.w)
            for k, v in w.r.items():
                add((k, v))
        out = []
        for k, v in deps.items():
            if k == eng and (eng == "pe" or not SAME_ENGINE_SYNC):
                continue
            if self.waited[eng].get(k, 0) >= v:
                continue
            self.waited[eng][k] = v
            out.append((k, v))
        return out

    def _mark(self, ev, reads, writes):
        k, v = ev
        for r in reads:
            if r.r.get(k, 0) < v:
                r.r[k] = v
        for w in writes:
            w.w = ev
            w.r = {}

    def op(self, eng, fn, reads=(), writes=()):
        if self.mute:
            return
        rec = _Rec()
        fn(rec)
        fn = rec.call
        reads = [_res(x) for x in reads]
        writes = [_res(x) for x in writes]
        waits = self._deps(eng, reads, writes)
        self.cnt[eng] += 1
        self.q[eng].append((waits, fn, (eng, 1)))
        self._mark((eng, self.cnt[eng]), reads, writes)

    def dma(self, qeng, key, fn, reads=(), writes=()):
        if self.mute:
            return
        rec = _Rec()
        fn(rec)
        fn = rec.call
        reads = [_res(x) for x in reads]
        writes = [_res(x) for x in writes]
        waits = self._deps(qeng, reads, writes)
        if key not in self.sems:
            self.sems[key] = self.nc.alloc_semaphore("dsem_" + key)
            self.dcnt[key] = 0
        self.dcnt[key] += 16
        self.q[qeng].append((waits, fn, (key, 16)))
        self._mark((key, self.dcnt[key]), reads, writes)

    def emit(self, final_res=()):
        nc = self.nc
        fin = {}
        for r in final_res:
            r = _res(r)
            if r.w is not None:
                k, v = r.w
                fin[k] = max(fin.get(k, 0), v)
        with nc.Block() as block:
            def mk(e):
                def body(eng):
                    for waits, fn, (k, amt) in self.q[e]:
                        for (wk, wv) in waits:
                            eng.wait_ge(self.sems[wk], wv)
                        name, a, kw = fn
                        getattr(eng, name)(*a, **kw).then_inc(self.sems[k], amt)
                    if e == "sp":
                        for wk, wv in fin.items():
                            eng.wait_ge(self.sems[wk], wv)
                return body
            block.tensor(mk("pe"))
            block.scalar(mk("act"))
            block.vector(mk("dve"))
            block.gpsimd(mk("pool"))
            block.sync(mk("sp"))


def build(NT, NL, NTB=2, lam_inits=None, dbg=False):
    nc = bass.Bass("TRN2", target_bir_lowering=False)
    Lp = NT * 128
    P = Prog(nc)

    def dram_in(name, shape, dt=F32):
        return nc.dram_tensor(name, list(shape), dt, kind="ExternalInput").ap()

    def dram_tmp(name, shape, dt):
        return nc.dram_tensor(name, list(shape), dt, kind="Internal").ap()

    xT_in = dram_in("xT", [D, Lp])
    w_in = dram_in("w_in", [NL, D, INW])
    w_branch = dram_in("w_branch", [NL, 1536, D])
    w_out = dram_in("w_out", [NL, D, D])
    ffn_up = dram_in("ffn_up", [NL, D, 2 * DFF])
    ffn_down = dram_in("ffn_down", [NL, DFF, D])
    pool_w = dram_in("pool_w", [NL, 4, 128, 128])
    rw_w2 = dram_in("rw_w2", [NL, 64, 512])
    rw_a2 = dram_in("rw_a2", [NL, 64, 512])
    rw_g2 = dram_in("rw_g2", [NL, 128, 512])
    pp_in = dram_in("pp", [NL, 128, NPP])
    cst_in = dram_in("cst", [128, NCS])
    ropeC = dram_in("ropeC", [128, Lp])
    ropeS = dram_in("ropeS", [128, Lp])
    outT = nc.dram_tensor("outT", [D, Lp], F32, kind="ExternalOutput").ap()
    xTo = nc.dram_tensor("xTo", [D, Lp], F32, kind="ExternalOutput").ap()

    win_bf = dram_tmp("win_bf", [NL, D, INW], BF16)
    wbr_bf = dram_tmp("wbr_bf", [NL, 1536, D], BF16)
    wout_bf = dram_tmp("wout_bf", [NL, D, D], BF16)
    up_bf = dram_tmp("up_bf", [NL, D, 2 * DFF], BF16)
    dn_bf = dram_tmp("dn_bf", [NL, DFF, D], BF16)
    xbufs = [dram_tmp("xT_a", [D, Lp], F32), dram_tmp("xT_b", [D, Lp], F32)]
    kT_hist = dram_tmp("kT_hist", [512, Lp], BF16)
    v_hist = dram_tmp("v_hist", [Lp, 516], BF16)
    dres = {}

    def DR(key):
        if key not in dres:
            dres[key] = Res(key)
        return dres[key]

    def S(name, shape, dt):
        return Buf(nc.alloc_sbuf_tensor("s_" + name, list(shape), dt), name)

    n = NTB * 128
    pb = [Buf(nc.alloc_psum_tensor("pb%d" % i, [128, 512], F32), "pb%d" % i) for i in range(7)]
    pbT = Buf(nc.alloc_psum_tensor("pbT", [128, 1024], BF16), "pbT")
    rr = [0]

    ALLB = [0, 1, 2, 3, 4, 5, 6]
    ATTB = [0, 1, 2]
    AUXB = ALLB
    cur_banks = [ALLB]

    def bank(lst=None):
        lst = lst or cur_banks[0]
        rr[0] += 1
        return pb[lst[rr[0] % len(lst)]]

    cst = S("cst", [128, NCS], F32)
    cstb = S("cstb", [128, 896], BF16)
    ident2 = S("ident2", [128, 2, 128], BF16)
    ones_bf = S("ones_bf", [128, 128], BF16)
    rmask = S("rmask", [128, n], F32)
    eps6 = S("eps6", [128, 1], F32)
    eps5 = S("eps5", [128, 1], F32)
    epsg = S("epsg", [128, 1], F32)
    eps18 = S("eps18", [128, 1], F32)
    P.dma("sp", "cst", lambda e: e.dma_start(out=cst[:, :], in_=cst_in[:, :]), writes=[cst])
    P.op("dve", lambda e: e.tensor_copy(out=cstb[:, :], in_=cst[:, 0:896]), reads=[cst], writes=[cstb])
    P.op("pool", lambda e: e.tensor_copy(out=ident2[:, 0, :], in_=cst[:, CS_ID:CS_ID + 128]), reads=[cst], writes=[ident2])
    P.op("pool", lambda e: e.tensor_copy(out=ident2[:, 1, :], in_=cst[:, CS_ID:CS_ID + 128]), reads=[cst], writes=[ident2])
    P.op("pool", lambda e: e.memset(ones_bf[:, :], 1.0), writes=[ones_bf])
    P.op("pool", lambda e: e.memset(rmask[:, :], 1.0), writes=[rmask])
    for tt in range(NTB):
        P.op("pool", lambda e, tt=tt: e.memset(rmask[:, tt * 128:tt * 128 + 1], 0.0), writes=[rmask])
    P.op("pool", lambda e: e.memset(eps6[:, :], 1e-6), writes=[eps6])
    P.op("pool", lambda e: e.memset(eps5[:, :], 1e-5), writes=[eps5])
    P.op("pool", lambda e: e.memset(epsg[:, :], 64e-5), writes=[epsg])
    P.op("pool", lambda e: e.memset(eps18[:, :], 1e-18), writes=[eps18])
    ident_bf = lambda: cstb[:, CS_ID:CS_ID + 128]
    msu = lambda: cst[:, CS_MSU:CS_MSU + 128]
    miu = lambda: cst[:, CS_MIU:CS_MIU + 128]
    msl = lambda: cst[:, CS_MSL:CS_MSL + 128]
    bones_bf = lambda: cstb[:, CS_BO:CS_BO + 128]
    perm_bf = lambda: cstb[:, CS_PERM:CS_PERM + 128]
    dmask_bf = lambda: cstb[:, CS_DM:CS_DM + 128]

    def conv_chunks(l):
        ch = []
        for src, dst, rows in ((w_in[l], win_bf[l], D), (w_branch[l], wbr_bf[l], 1536), (w_out[l], wout_bf[l], D),
                               (ffn_up[l], up_bf[l], D), (ffn_down[l], dn_bf[l], DFF)):
            for r0 in range(0, rows, 128):
                ch.append((src, dst, r0, min(rows, r0 + 128)))
        return ch

    def issue_conv(l, chunks):
        for (src, dst, r0, r1) in chunks:
            P.dma("pool", "cv%d" % l, lambda e, src=src, dst=dst, r0=r0, r1=r1: e.dma_start(out=dst[r0:r1, :], in_=src[r0:r1, :]),
                  writes=[DR("cv%d" % l)])

    issue_conv(0, conv_chunks(0))
    P.mute = False
    pp = S("pp", [128, NPP], F32)
    poolw_bf = S("poolw_bf", [128, 4, 128], BF16)
    w2_bf = S("w2_bf", [128, 512], BF16)
    g2_bf = S("g2_bf", [128, 512], BF16)
    rkones = S("rkones", [128, 4, 128], BF16)
    omka = S("omka", [128, 4], F32)
    neglam = S("neglam", [128, 1], F32)
    lamt = S("lamt", [128, 8], F32)
    sublnw = S("sublnw", [128, 128], F32)

    xt = S("xt", [128, 8, n], F32)
    sq = S("sq", [128, 8, n], BF16)
    hT = S("hT", [128, 8, n], BF16)
    lnv = S("lnv", [128, n], F32)
    rstd = S("rstd", [128, n], F32)
    wg = [S("wg%d" % i, [128, 8, 512], BF16) for i in range(2)]
    gates = S("gates", [128, 24, n], BF16)
    brP = S("brP", [128, 4, n], BF16)
    brA = S("brA", [128, 4, n], BF16)
    brR = S("brR", [128, 4, n], BF16)
    qT = S("qT", [128, 4, n], BF16)
    kTb = S("kTb", [128, 4, n], BF16)
    vaug = S("vaug", [128, NTB, 4, 129], BF16)
    rC = S("rC", [128, n], F32)
    rS = S("rS", [128, n], F32)
    qraw = [S("qraw%d" % i, [128, n], BF16) for i in range(2)]
    scr = [S("scr%d" % i, [128, n], F32) for i in range(8)]
    pu = S("pu", [128, 4, 16 + n], F32)
    pta = S("pta", [128, 16 + n], F32)
    ptb = S("ptb", [128, 16 + n], F32)
    pooled = S("pooled", [128, 4, n], BF16)
    rp = S("rp", [128, 12, n], BF16)
    ltmp = [S("ltmp%d" % i, [128, 1 + n], F32) for i in range(2)]
    ldt = [S("ldt%d" % i, [128, n], F32) for i in range(2)]
    lora = S("lora", [128, 2, n], F32)
    rhalo = S("rhalo", [128, 14], F32)
    tw_bf = S("tw_bf", [128, n], BF16)
    sg_bf = S("sg_bf", [128, n], BF16)
    kk2_bf = S("kk2_bf", [128, n], BF16)
    rk_bf = S("rk_bf", [128, n], BF16)
    AR = S("AR", [128, 4, NTB, 2, 128], BF16)
    kt_bf = S("kt_bf", [128, 4, n], BF16)
    bt_bf = S("bt_bf", [128, 4, n], BF16)
    tokm = S("tokm", [128, NTB, 4, 3, 128], BF16)
    gT = S("gT", [128, 4, n], BF16)
    bonT = S("bonT", [128, 4, n], BF16)
    emt = S("emt", [128, 4, NTB], F32)
    eet = S("eet", [128, 4, NTB], F32)
    emet = S("emet", [128, 4, NTB], F32)
    Qp = [S("Qp%d" % i, [128, 8, 128], BF16) for i in range(2)]
    Np = [S("Np%d" % i, [128, 8, 128], BF16) for i in range(2)]
    Sb = [S("Sb%d" % i, [128, 8, 128], BF16) for i in range(2)]
    ABm = S("ABm", [128, 8, 128], BF16)
    AKm = S("AKm", [128, 8, 128], BF16)
    RKm = S("RKm", [128, 8, 128], BF16)
    Xs = S("Xs", [128, 512], BF16)
    Us = S("Us", [128, 512], BF16)
    Ysb = S("Ysb", [128, 512], F32)
    ysq = S("ysq", [128, 512], F32)
    yc = S("yc", [128, 512], F32)
    ynb = S("ynb", [128, 512], BF16)
    yst = S("yst", [128, 40], F32)
    Hs32 = [S("Hs32_%d" % j, [128, 64], F32) for j in range(4)]
    Hb = [S("Hb_%d" % j, [128, 64], BF16) for j in range(4)]
    Hpe = [S("Hpe_%d" % j, [128, 64], F32) for j in range(4)]
    kTk = [S("kTk%d" % i, [128, n], BF16) for i in range(2)]
    vk = [S("vk%d" % i, [128, NTB, 129], BF16) for i in range(2)]
    ET = [S("ET%d" % i, [128, n], BF16) for i in range(3)]
    ao = S("ao", [128, 128], F32)
    at = S("at", [128, 128], F32)
    aj = S("aj", [128, 128], F32)
    aon = S("aon", [128, 128], BF16)
    ast = S("ast", [128, 8], F32)
    merged = S("merged", [128, 8, n], BF16)
    mtmp = [S("mtmp%d" % i, [128, n], BF16) for i in range(2)]
    wbr = S("wbr", [128, 4, 1024], BF16)
    fact = S("fact", [128, 22, n], BF16)
    dnw = [S("dnw%d" % i, [128, 22, 128], BF16) for i in range(2)]
    chalo = S("chalo", [128, 44, 2], F32)
    cgs = gates
    invc = lambda g: cst[:, CS_IC + g * 128:CS_IC + (g + 1) * 128]

    slot_res = {}

    def oslot(c, qi):
        idx = c * NTB + qi
        bk = 3 + idx
        return pb[bk], 0, pb[bk].res
    assert 2 * NTB <= 4

    nblocks = (NT + NTB - 1) // NTB

    def norm_stage(gcol, nb_):
        P.op("act", lambda e: e.activation(out=sq[:, :, :nb_], in_=xt[:, :, :nb_], func=AF.Square), reads=[xt], writes=[sq])
        ps = bank()
        for c in range(8):
            P.op("pe", lambda e, c=c: e.matmul(ps[:, :nb_], lhsT=ones_bf[:, :], rhs=sq[:, c, :nb_], start=(c == 0), stop=(c == 7)),
                 reads=[ones_bf, sq], writes=[ps])
        P.op("act", lambda e: e.activation(out=lnv[:, :nb_], in_=ps[:, :nb_], func=AF.Ln, bias=eps6[:, 0:1], scale=1.0 / D),
             reads=[ps, eps6], writes=[lnv])
        P.op("act", lambda e: e.activation(out=rstd[:, :nb_], in_=lnv[:, :nb_], func=AF.Exp, scale=-0.5), reads=[lnv], writes=[rstd])
        for c in range(8):
            P.op("dve", lambda e, c=c: e.scalar_tensor_tensor(out=hT[:, c, :nb_], in0=xt[:, c, :nb_], scalar=pp[:, gcol + c:gcol + c + 1],
                                                              in1=rstd[:, :nb_], op0=ALU.mult, op1=ALU.mult),
                 reads=[xt, pp, rstd], writes=[hT])

    wgi = [0]

    def load_wg(src, c0, gc, key):
        wgi[0] += 1
        w = wg[wgi[0] % 2]
        P.dma("sp", w.res.name, lambda e: e.dma_start(out=w[:, :, :gc], in_=src[:, c0:c0 + gc].rearrange("(c p) m -> p c m", p=128)),
              reads=[DR(key)], writes=[w])
        return w

    def proj_fm(w, ml, nb_, ps):
        for c in range(8):
            P.op("pe", lambda e, c=c: e.matmul(ps[:, :nb_], lhsT=w[:, c, ml * 128:(ml + 1) * 128], rhs=hT[:, c, :nb_], start=(c == 0), stop=(c == 7)),
                 reads=[w, hT], writes=[ps])

    def layer(l, xsrc, xdst, last):
        P.dma("sp", "pp", lambda e: e.dma_start(out=pp[:, :], in_=pp_in[l]), writes=[pp])
        P.dma("pool", "poolw", lambda e: e.dma_start(out=poolw_bf[:, :, :], in_=pool_w[l].rearrange("g c d -> c g d")), writes=[poolw_bf])
        P.dma("pool", "w2", lambda e: e.dma_start(out=w2_bf[0:64, :], in_=rw_w2[l]), writes=[w2_bf])
        P.dma("pool", "w2", lambda e: e.dma_start(out=w2_bf[64:128, :], in_=rw_a2[l]), writes=[w2_bf])
        P.dma("pool", "g2", lambda e: e.dma_start(out=g2_bf[:, :], in_=rw_g2[l]), writes=[g2_bf])
        for j in range(4):
            P.op("dve", lambda e, j=j: e.tensor_scalar(out=rkones[:, j, :], in0=cst[:, CS_BO:CS_BO + 128], scalar1=pp[:, PP_RK + j:PP_RK + j + 1],
                                                      scalar2=None, op0=ALU.mult), reads=[cst, pp], writes=[rkones])
        P.op("dve", lambda e: e.tensor_scalar(out=omka[:, :], in0=pp[:, PP_KA:PP_KA + 4], scalar1=-1.0, scalar2=1.0, op0=ALU.mult, op1=ALU.add),
             reads=[pp], writes=[omka])
        P.op("dve", lambda e: e.tensor_scalar(out=sublnw[:, :], in0=pp[:, PP_SUBLN:PP_SUBLN + 128], scalar1=pp[:, PP_OML:PP_OML + 1], scalar2=None, op0=ALU.mult),
             reads=[pp], writes=[sublnw])
        P.op("dve", lambda e: e.tensor_tensor(out=aj[:, 0:64], in0=pp[:, PP_LAM:PP_LAM + 64], in1=pp[:, PP_LAM + 64:PP_LAM + 128], op=ALU.mult), reads=[pp], writes=[aj])
        P.op("dve", lambda e: e.tensor_tensor(out=aj[:, 64:128], in0=pp[:, PP_LAM + 128:PP_LAM + 192], in1=pp[:, PP_LAM + 192:PP_LAM + 256], op=ALU.mult), reads=[pp], writes=[aj])
        P.op("dve", lambda e: e.tensor_reduce(out=lamt[:, 0:2], in_=aj[:, :].rearrange("p (a b) -> p a b", b=64), axis=AX.X, op=ALU.add), reads=[aj], writes=[lamt])
        P.op("act", lambda e: e.activation(out=lamt[:, 2:4], in_=lamt[:, 0:2], func=AF.Exp), reads=[lamt], writes=[lamt])
        P.op("dve", lambda e: e.tensor_tensor(out=lamt[:, 4:5], in0=lamt[:, 3:4], in1=lamt[:, 2:3], op=ALU.subtract), reads=[lamt], writes=[lamt])
        P.op("dve", lambda e: e.tensor_scalar(out=neglam[:, :], in0=lamt[:, 4:5], scalar1=pp[:, PP_NLI:PP_NLI + 1], scalar2=None, op0=ALU.add), reads=[lamt, pp], writes=[neglam])
        P.op("pool", lambda e: e.memset(rhalo[:, :], 0.0), writes=[rhalo])
        P.op("pool", lambda e: e.memset(pu[:, :, 0:16], 0.0), writes=[pu])
        P.op("pool", lambda e: e.memset(chalo[:, :, :], 0.0), writes=[chalo])
        for j in range(4):
            P.op("pool", lambda e, j=j: e.memset(Hs32[j][:, :], 0.0), writes=[Hs32[j]])

        WIN = win_bf[l]
        nxt = conv_chunks(l + 1) if l + 1 < NL else []
        per_blk = (len(nxt) + nblocks - 2) // max(1, nblocks - 1) if nxt else 0
        for b in range(nblocks):
            if nxt:
                issue_conv(l + 1, nxt[b * per_blk:(b + 1) * per_blk])
            t0 = b * n
            nt = min(NTB, NT - b * NTB)
            nb_ = nt * 128
            tsl = slice(t0, t0 + nb_)
            xin_key = "x%d_%d_%d" % (l, 0, b)
            P.dma("sp", "xt", lambda e: e.dma_start(out=xt[:, :, :nb_], in_=xsrc[:, tsl].rearrange("(c p) t -> p c t", p=128)),
                  reads=[DR("xs%d_%d" % (l, b))], writes=[xt])
            P.dma("sp", "rC", lambda e: e.dma_start(out=rC[:, :nb_], in_=ropeC[:, tsl]), writes=[rC])
            P.dma("sp", "rS", lambda e: e.dma_start(out=rS[:, :nb_], in_=ropeS[:, tsl]), writes=[rS])
            norm_stage(PP_GMIX, nb_)

            P.mute = ("pool" in SKIP)
            w = load_wg(WIN, 0, 512, "cv%d" % l)
            for g in range(4):
                ps = bank()
                proj_fm(w, g, nb_, ps)
                P.op("act", lambda e, g=g, ps=ps: e.activation(out=pu[:, g, 16:16 + nb_], in_=ps[:, :nb_], func=AF.Copy), reads=[ps], writes=[pu])
                src = pu
                bufs = [pta, ptb]
                cur = None
                for lev in range(g + 1):
                    sh = 1 << lev
                    lo = 2 * sh - 1
                    dst = bufs[lev % 2]
                    if lev == 0:
                        P.op("dve", lambda e, dst=dst, g=g, lo=lo, sh=sh: e.tensor_tensor(out=dst[:, lo:16 + nb_], in0=pu[:, g, lo:16 + nb_], in1=pu[:, g, lo - sh:16 + nb_ - sh], op=ALU.add),
                             reads=[pu], writes=[dst])
                    else:
                        P.op("dve", lambda e, dst=dst, cur=cur, lo=lo, sh=sh: e.tensor_tensor(out=dst[:, lo:16 + nb_], in0=cur[:, lo:16 + nb_], in1=cur[:, lo - sh:16 + nb_ - sh], op=ALU.add),
                             reads=[cur], writes=[dst])
                    cur = dst
                wv = float(2 << g)
                P.op("dve", lambda e, g=g, cur=cur, wv=wv: e.scalar_tensor_tensor(out=pooled[:, g, :nb_], in0=cur[:, 16:16 + nb_], scalar=1.0 / wv, in1=pu[:, g, 16:16 + nb_],
                                                                                  op0=ALU.mult, op1=ALU.subtract), reads=[cur, pu], writes=[pooled])
                if b == 0:
                    P.op("dve", lambda e, g=g, cur=cur: e.tensor_tensor(out=cur[:, 16:144], in0=cur[:, 16:144], in1=invc(g), op=ALU.mult), reads=[cur, cst], writes=[cur])
                    P.op("dve", lambda e, g=g, cur=cur: e.tensor_tensor(out=pooled[:, g, 0:128], in0=cur[:, 16:144], in1=pu[:, g, 16:144], op=ALU.subtract),
                         reads=[cur, pu], writes=[pooled])
                P.op("pool", lambda e, g=g: e.tensor_copy(out=pu[:, g, 0:16], in_=pu[:, g, nb_:nb_ + 16]), reads=[pu], writes=[pu])
                ps2 = bank(AUXB)
                P.op("pe", lambda e, g=g, ps2=ps2: e.matmul(ps2[:, :nb_], lhsT=poolw_bf[:, g, :], rhs=pooled[:, g, :nb_], start=True, stop=True),
                     reads=[poolw_bf, pooled], writes=[ps2])
                P.op("act", lambda e, g=g, ps2=ps2: e.activation(out=brP[:, g, :nb_], in_=ps2[:, :nb_], func=AF.Identity, scale=pp[:, PP_PSC + g:PP_PSC + g + 1]),
                     reads=[ps2, pp], writes=[brP])

            P.mute = ("qk" in SKIP)
            for which in range(2):
                w = load_wg(WIN, 512 + which * 512, 512, "cv%d" % l)
                dstb = qT if which == 0 else kTb
                for m in range(4):
                    ps = bank()
                    proj_fm(w, m, nb_, ps)
                    qr = qraw[m % 2]
                    P.op("act", lambda e, ps=ps, qr=qr: e.activation(out=qr[:, :nb_], in_=ps[:, :nb_], func=AF.Copy), reads=[ps], writes=[qr])
                    ps2 = bank(AUXB)
                    P.op("pe", lambda e, ps2=ps2, qr=qr: e.matmul(ps2[:, :nb_], lhsT=perm_bf(), rhs=qr[:, :nb_], start=True, stop=True), reads=[cstb, qr], writes=[ps2])
                    s1 = scr[(2 * m) % 8]
                    s2 = scr[(2 * m + 1) % 8]
                    P.op("dve", lambda e, qr=qr, s1=s1: e.tensor_tensor(out=s1[:, :nb_], in0=qr[:, :nb_], in1=rC[:, :nb_], op=ALU.mult), reads=[qr, rC], writes=[s1])
                    P.op("dve", lambda e, ps2=ps2, s2=s2: e.tensor_tensor(out=s2[:, :nb_], in0=ps2[:, :nb_], in1=rS[:, :nb_], op=ALU.mult), reads=[ps2, rS], writes=[s2])
                    P.op("pool", lambda e, s1=s1, s2=s2, m=m, dstb=dstb: e.tensor_tensor(out=dstb[:, m, :nb_], in0=s1[:, :nb_], in1=s2[:, :nb_], op=ALU.add),
                         reads=[s1, s2], writes=[dstb])
            P.dma("pool", "kst", lambda e: e.dma_start(out=kT_hist[:, tsl].rearrange("(h p) t -> p h t", p=128), in_=kTb[:, :, :nb_]),
                  reads=[kTb], writes=[DR("kh%d" % b)])

            P.mute = ("v" in SKIP)
            w = load_wg(WIN, 1536, 512, "cv%d" % l)
            P.op("pool", lambda e: e.memset(vaug[:, :, :, 128:129], 1.0), writes=[vaug])
            if b == 0:
                P.op("pool", lambda e: e.memset(vaug[0:NPAD, 0, :, 128:129], 0.0), writes=[vaug])
            for tt in range(nt):
                ps = bank()
                for c in range(8):
                    P.op("pe", lambda e, c=c, ps=ps, tt=tt, w=w: e.matmul(ps[:, :], lhsT=hT[:, c, tt * 128:(tt + 1) * 128], rhs=w[:, c, :], start=(c == 0), stop=(c == 7)),
                         reads=[hT, w], writes=[ps])
                P.op("act", lambda e, ps=ps, tt=tt: e.activation(out=vaug[:, tt, :, 0:128], in_=ps[:, :].rearrange("p (h e) -> p h e", h=4), func=AF.Copy),
                     reads=[ps], writes=[vaug])
            P.dma("pool", "vst", lambda e: e.dma_start(out=v_hist[tsl, :].rearrange("(t p) f -> p t f", p=128), in_=vaug[:, :nt, :, :].rearrange("p t h e -> p t (h e)")),
                  reads=[vaug], writes=[DR("vh%d" % b)])

            P.mute = ("rwproj" in SKIP)
            def lerp_tile(ps, mi, out_ap_fn, outbuf):
                lt = ltmp[mi % 2]
                ld = ldt[mi % 2]
                P.op("pool", lambda e: e.tensor_copy(out=lt[:, 0:1], in_=rhalo[:, mi:mi + 1]), reads=[rhalo], writes=[lt])
                P.op("act", lambda e: e.activation(out=lt[:, 1:1 + nb_], in_=ps[:, :nb_], func=AF.Copy), reads=[ps], writes=[lt])
                P.op("pool", lambda e: e.tensor_copy(out=rhalo[:, mi:mi + 1], in_=lt[:, nb_:nb_ + 1]), reads=[lt], writes=[rhalo])
                P.op("dve", lambda e: e.tensor_tensor(out=ld[:, :nb_], in0=lt[:, 0:nb_], in1=lt[:, 1:1 + nb_], op=ALU.subtract), reads=[lt], writes=[ld])
                P.op("dve", lambda e: e.scalar_tensor_tensor(out=out_ap_fn(), in0=ld[:, :nb_], scalar=pp[:, PP_MU + mi:PP_MU + mi + 1], in1=lt[:, 1:1 + nb_],
                                                             op0=ALU.mult, op1=ALU.add), reads=[ld, lt, pp], writes=[outbuf])
            w = load_wg(WIN, 3584, 256, "cv%d" % l)
            for m in range(2):
                ps = bank()
                proj_fm(w, m, nb_, ps)
                lerp_tile(ps, 12 + m, lambda m=m: lora[:, m, :nb_], lora)
            P.op("act", lambda e: e.activation(out=tw_bf[0:64, :nb_], in_=lora[0:64, 0, :nb_], func=AF.Tanh), reads=[lora], writes=[tw_bf])
            P.op("act", lambda e: e.activation(out=tw_bf[64:128, :nb_], in_=lora[64:128, 0, :nb_], func=AF.Copy), reads=[lora], writes=[tw_bf])
            P.op("act", lambda e: e.activation(out=sg_bf[:, :nb_], in_=lora[:, 1, :nb_], func=AF.Sigmoid), reads=[lora], writes=[sg_bf])
            for which in range(3):
                w = load_wg(WIN, 2048 + which * 512, 512, "cv%d" % l)
                for m in range(4):
                    ps = bank()
                    proj_fm(w, m, nb_, ps)
                    mi = which * 4 + m
                    lerp_tile(ps, mi, lambda mi=mi: rp[:, mi, :nb_], rp)

            P.mute = ("gates" in SKIP)
            for gg in range(6):
                w = load_wg(WIN, 3840 + gg * 512, 512, "cv%d" % l)
                for m in range(4):
                    ps = bank()
                    proj_fm(w, m, nb_, ps)
                    gi = gg * 4 + m
                    P.op("act", lambda e, ps=ps, gi=gi: e.activation(out=gates[:, gi, :nb_], in_=ps[:, :nb_], func=AF.Sigmoid), reads=[ps], writes=[gates])

            def attn_gen():
                ei = 0
                for h in range(4):
                    for kb in range(b + 1):
                        nkt = min(NTB, NT - kb * NTB)
                        nk = nkt * 128
                        kk_ = kTk[(h * 64 + kb) % 2]
                        vv = vk[(h * 64 + kb) % 2]
                        ksl = slice(kb * n, kb * n + nk)
                        P.dma("sp", kk_.res.name, lambda e, kk_=kk_, ksl=ksl, nk=nk: e.dma_start(out=kk_[:, :nk], in_=kT_hist[h * 128:(h + 1) * 128, ksl]),
                              reads=[DR("kh%d" % kb)], writes=[kk_])
                        P.dma("sp", vv.res.name, lambda e, vv=vv, ksl=ksl, nkt=nkt: e.dma_start(out=vv[:, :nkt, :], in_=v_hist[ksl, h * 129:(h + 1) * 129].rearrange("(t p) f -> p t f", p=128)),
                              reads=[DR("vh%d" % kb)], writes=[vv])
                        for jj in range(nkt):
                            jg = kb * NTB + jj
                            q0 = 0 if kb < b else jj
                            ncol = (nt - q0) * 128
                            if ncol <= 0:
                                continue
                            sts = []
                            for c in range(2):
                                st = bank()
                                et = ET[ei % 3]
                                ei += 1
                                P.op("pe", lambda e, st=st, kk_=kk_, c=c, jj=jj, q0=q0, ncol=ncol: e.matmul(st[:, :ncol], lhsT=kk_[c * 64:(c + 1) * 64, jj * 128:(jj + 1) * 128],
                                                                                                        rhs=qT[c * 64:(c + 1) * 64, h, q0 * 128:q0 * 128 + ncol], start=True, stop=True),
                                     reads=[kk_, qT], writes=[st])
                                sts.append((st, et))
                            for c in range(2):
                                st, et = sts[c]
                                P.op("act", lambda e, st=st, et=et, ncol=ncol: e.activation(out=et[:, :ncol], in_=st[:, :ncol], func=AF.Exp, scale=0.125), reads=[st], writes=[et])
                                if kb == b and jg > 0:
                                    P.op("pool", lambda e, et=et: e.tensor_tensor(out=et[:, 0:128], in0=et[:, 0:128], in1=dmask_bf(), op=ALU.mult), reads=[et, cstb], writes=[et])
                                for qi in range(q0, nt):
                                    ob, off, ores = oslot(c, qi)
                                    ig = b * NTB + qi
                                    P.op("pe", lambda e, ob=ob, off=off, et=et, qi=qi, q0=q0, vv=vv, jj=jj, jg=jg, ig=ig: e.matmul(
                                        ob[:, off:off + 129], lhsT=et[:, (qi - q0) * 128:(qi - q0 + 1) * 128], rhs=vv[:, jj, :], start=(jg == 0), stop=(jg == ig)),
                                        reads=[et, vv], writes=[ores])
                            if kb == b:
                                qi = jj
                                o0b, o0off, o0r = oslot(0, qi)
                                o1b, o1off, o1r = oslot(1, qi)
                                P.op("dve", lambda e, o0b=o0b, o0off=o0off: e.reciprocal(out=ast[:, 0:1], in_=o0b[:, o0off + 128:o0off + 129]), reads=[o0r], writes=[ast])
                                P.op("dve", lambda e, o1b=o1b, o1off=o1off: e.reciprocal(out=ast[:, 1:2], in_=o1b[:, o1off + 128:o1off + 129]), reads=[o1r], writes=[ast])
                                P.op("dve", lambda e: e.tensor_tensor(out=ast[:, 2:3], in0=ast[:, 1:2], in1=neglam[:, 0:1], op=ALU.mult), reads=[ast, neglam], writes=[ast])
                                P.op("dve", lambda e, o1b=o1b, o1off=o1off: e.tensor_scalar(out=at[:, :], in0=o1b[:, o1off:o1off + 128], scalar1=ast[:, 2:3], scalar2=None, op0=ALU.mult),
                                     reads=[o1r, ast], writes=[at])
                                P.op("dve", lambda e, o0b=o0b, o0off=o0off: e.scalar_tensor_tensor(out=ao[:, :], in0=o0b[:, o0off:o0off + 128], scalar=ast[:, 0:1], in1=at[:, :],
                                                                                                  op0=ALU.mult, op1=ALU.add), reads=[o0r, ast, at], writes=[ao])
                                P.op("act", lambda e: e.activation(out=aj[:, :], in_=ao[:, :], func=AF.Square, accum_out=ast[:, 3:4]), reads=[ao], writes=[aj, ast])
                                P.op("act", lambda e: e.activation(out=ast[:, 4:5], in_=ast[:, 3:4], func=AF.Ln, bias=eps5[:, 0:1], scale=1.0 / 128), reads=[ast, eps5], writes=[ast])
                                P.op("act", lambda e: e.activation(out=ast[:, 5:6], in_=ast[:, 4:5], func=AF.Exp, scale=-0.5), reads=[ast], writes=[ast])
                                P.op("dve", lambda e: e.scalar_tensor_tensor(out=aon[:, :], in0=ao[:, :], scalar=ast[:, 5:6], in1=sublnw[:, :], op0=ALU.mult, op1=ALU.mult),
                                     reads=[ao, ast, sublnw], writes=[aon])
                                P.op("pe", lambda e: e.transpose(pbT[:, 0:128], aon[:, :], ident_bf()), reads=[aon, cstb], writes=[pbT])
                                P.op("act", lambda e, qi=qi: e.activation(out=brA[:, h, qi * 128:(qi + 1) * 128], in_=pbT[:, 0:128], func=AF.Copy), reads=[pbT], writes=[brA])
                            yield

            def rw_gen():
                A_, B_, C_, D_, E_, F_, G_ = scr[0], scr[1], scr[2], scr[3], scr[4], scr[5], scr[6]
                v3 = lambda buf: buf[:, :nb_].rearrange("p (t s) -> p t s", s=128)
                for j in range(4):
                    jc = slice(j * 128, (j + 1) * 128)
                    r_ap = lambda: rp[:, j, :nb_]
                    k_ap = lambda: rp[:, 4 + j, :nb_]
                    v_ap = lambda: rp[:, 8 + j, :nb_]
                    ps = bank()
                    P.op("pe", lambda e, ps=ps, jc=jc: e.matmul(ps[:, :nb_], lhsT=w2_bf[0:64, jc], rhs=tw_bf[0:64, :nb_], start=True, stop=True), reads=[w2_bf, tw_bf], writes=[ps])
                    P.op("act", lambda e, ps=ps, j=j: e.activation(out=A_[:, :nb_], in_=ps[:, :nb_], func=AF.Sigmoid, bias=pp[:, PP_W0 + j:PP_W0 + j + 1]), reads=[ps, pp], writes=[A_])
                    ps = bank()
                    P.op("pe", lambda e, ps=ps, jc=jc: e.matmul(ps[:, :nb_], lhsT=w2_bf[64:128, jc], rhs=tw_bf[64:128, :nb_], start=True, stop=True), reads=[w2_bf, tw_bf], writes=[ps])
                    P.op("act", lambda e, ps=ps, j=j: e.activation(out=B_[:, :nb_], in_=ps[:, :nb_], func=AF.Sigmoid, bias=pp[:, PP_A0 + j:PP_A0 + j + 1]), reads=[ps, pp], writes=[B_])
                    ps = bank()
                    P.op("pe", lambda e, ps=ps, jc=jc: e.matmul(ps[:, :nb_], lhsT=g2_bf[:, jc], rhs=sg_bf[:, :nb_], start=True, stop=True), reads=[g2_bf, sg_bf], writes=[ps])
                    P.op("act", lambda e, ps=ps, j=j: e.activation(out=gT[:, j, :nb_], in_=ps[:, :nb_], func=AF.Copy), reads=[ps], writes=[gT])
                    P.op("dve", lambda e: e.tensor_tensor_scan(out=C_[:, :nb_], data0=rmask[:, :nb_], data1=A_[:, :nb_], initial=0.0, op0=ALU.mult, op1=ALU.add),
                         reads=[rmask, A_], writes=[C_])
                    P.op("dve", lambda e: e.tensor_tensor(out=D_[:, :nb_], in0=C_[:, :nb_], in1=A_[:, :nb_], op=ALU.subtract), reads=[C_, A_], writes=[D_])
                    P.op("dve", lambda e: e.tensor_tensor(out=v3(A_), in0=v3(C_), in1=v3(C_)[:, :, 63:64].broadcast_to([128, nt, 128]), op=ALU.subtract), reads=[C_], writes=[A_])
                    P.op("dve", lambda e: e.tensor_tensor(out=v3(E_), in0=v3(D_), in1=v3(C_)[:, :, 63:64].broadcast_to([128, nt, 128]), op=ALU.subtract), reads=[C_, D_], writes=[E_])
                    P.op("act", lambda e, j=j: e.activation(out=emt[:, j, :nt], in_=v3(C_)[:, :, 63], func=AF.Exp, scale=C1), reads=[C_], writes=[emt])
                    P.op("act", lambda e, j=j: e.activation(out=eet[:, j, :nt], in_=v3(A_)[:, :, 127], func=AF.Exp, scale=C1), reads=[A_], writes=[eet])
                    P.op("act", lambda e, j=j: e.activation(out=emet[:, j, :nt], in_=v3(C_)[:, :, 127], func=AF.Exp, scale=C1), reads=[C_], writes=[emet])
                    P.op("act", lambda e: e.activation(out=D_[:, :nb_], in_=A_[:, :nb_], func=AF.Exp, scale=C1), reads=[A_], writes=[D_])
                    P.op("act", lambda e: e.activation(out=F_[:, :nb_], in_=A_[:, :nb_], func=AF.Exp, scale=-C1), reads=[A_], writes=[F_])
                    P.op("act", lambda e: e.activation(out=G_[:, :nb_], in_=E_[:, :nb_], func=AF.Exp, scale=C1), reads=[E_], writes=[G_])
                    P.op("act", lambda e, j=j: e.activation(out=kk2_bf[:, :nb_], in_=k_ap(), func=AF.Square, scale=pp[:, PP_KK + j:PP_KK + j + 1]), reads=[rp, pp], writes=[kk2_bf])
                    ps = bank()
                    P.op("pe", lambda e, ps=ps: e.matmul(ps[:, :nb_], lhsT=bones_bf(), rhs=kk2_bf[:, :nb_], start=True, stop=True), reads=[cstb, kk2_bf], writes=[ps])
                    P.op("act", lambda e, ps=ps: e.activation(out=C_[:, :nb_], in_=ps[:, :nb_], func=AF.Ln, bias=eps18[:, 0:1]), reads=[ps, eps18], writes=[C_])
                    P.op("act", lambda e: e.activation(out=C_[:, :nb_], in_=C_[:, :nb_], func=AF.Exp, scale=-0.5), reads=[C_], writes=[C_])
                    P.op("dve", lambda e, j=j: e.scalar_tensor_tensor(out=A_[:, :nb_], in0=k_ap(), scalar=pp[:, PP_KK + j:PP_KK + j + 1], in1=C_[:, :nb_], op0=ALU.mult, op1=ALU.mult),
                         reads=[rp, pp, C_], writes=[A_])
                    P.op("dve", lambda e, j=j: e.tensor_scalar(out=E_[:, :nb_], in0=B_[:, :nb_], scalar1=pp[:, PP_KA + j:PP_KA + j + 1], scalar2=omka[:, j:j + 1], op0=ALU.mult, op1=ALU.add),
                         reads=[B_, pp, omka], writes=[E_])
                    P.op("dve", lambda e: e.tensor_tensor(out=E_[:, :nb_], in0=E_[:, :nb_], in1=k_ap(), op=ALU.mult), reads=[E_, rp], writes=[E_])
                    P.op("dve", lambda e: e.tensor_tensor(out=C_[:, :nb_], in0=A_[:, :nb_], in1=B_[:, :nb_], op=ALU.mult), reads=[A_, B_], writes=[C_])
                    P.op("dve", lambda e, j=j: e.scalar_tensor_tensor(out=AR[:, j, :nt, 0, :], in0=v3(A_), scalar=-1.0, in1=v3(G_), op0=ALU.mult, op1=ALU.mult), reads=[A_, G_], writes=[AR])
                    P.op("dve", lambda e, j=j: e.tensor_tensor(out=AR[:, j, :nt, 1, :], in0=rp[:, j, :nb_].rearrange("p (t s) -> p t s", s=128), in1=v3(D_), op=ALU.mult), reads=[rp, D_], writes=[AR])
                    P.op("dve", lambda e, j=j: e.tensor_tensor(out=kt_bf[:, j, :nb_], in0=E_[:, :nb_], in1=F_[:, :nb_], op=ALU.mult), reads=[E_, F_], writes=[kt_bf])
                    P.op("dve", lambda e, j=j: e.tensor_tensor(out=bt_bf[:, j, :nb_], in0=C_[:, :nb_], in1=F_[:, :nb_], op=ALU.mult), reads=[C_, F_], writes=[bt_bf])
                    P.op("dve", lambda e: e.tensor_tensor(out=rk_bf[:, :nb_], in0=r_ap(), in1=E_[:, :nb_], op=ALU.mult), reads=[rp, E_], writes=[rk_bf])
                    ps = bank()
                    P.op("pe", lambda e, ps=ps, j=j: e.matmul(ps[:, :nb_], lhsT=rkones[:, j, :], rhs=rk_bf[:, :nb_], start=True, stop=True), reads=[rkones, rk_bf], writes=[ps])
                    P.op("dve", lambda e, ps=ps, j=j: e.tensor_tensor(out=bonT[:, j, :nb_], in0=ps[:, :nb_], in1=v_ap(), op=ALU.mult), reads=[ps, rp], writes=[bonT])
                    for tt in range(nt):
                        tsl2 = slice(tt * 128, (tt + 1) * 128)
                        P.op("pe", lambda e, j=j, tsl2=tsl2: e.transpose(pbT[:, 0:128], bt_bf[:, j, tsl2], ident_bf()), reads=[bt_bf, cstb], writes=[pbT])
                        P.op("pe", lambda e, j=j, tsl2=tsl2: e.transpose(pbT[:, 128:256], kt_bf[:, j, tsl2], ident_bf()), reads=[kt_bf, cstb], writes=[pbT])
                        P.op("pe", lambda e, j=j, tsl2=tsl2: e.transpose(pbT[:, 256:384], rp[:, 8 + j, tsl2], ident_bf()), reads=[rp, cstb], writes=[pbT])
                        P.op("act", lambda e, j=j, tt=tt: e.activation(out=tokm[:, tt, j, :, :], in_=pbT[:, 0:384].rearrange("p (a b) -> p a b", a=3), func=AF.Copy), reads=[pbT], writes=[tokm])
                        yield

                for tt in range(nt):
                    tsl2 = slice(tt * 128, (tt + 1) * 128)
                    for j in range(4):
                        P.op("dve", lambda e, j=j, tt=tt: e.tensor_scalar(out=Hb[j][:, :], in0=Hs32[j][:, :], scalar1=emt[:, j, tt:tt + 1], scalar2=None, op0=ALU.mult), reads=[Hs32[j], emt], writes=[Hb[j]])
                        P.op("pool", lambda e, j=j, tt=tt: e.tensor_scalar(out=Hpe[j][:, :], in0=Hs32[j][:, :], scalar1=emet[:, j, tt:tt + 1], scalar2=None, op0=ALU.mult), reads=[Hs32[j], emet], writes=[Hpe[j]])
                    def gram(lhs_buf, lhs_fn, rhs_fn, rhs_bufs, mask_fn, dst, eng):
                        for half in range(2):
                            ps = bank()
                            for hh in range(4):
                                j, hp = hh, half
                                bp = slice(hp * 64, hp * 64 + 64)
                                P.op("pe", lambda e, ps=ps, hh=hh, j=j, bp=bp: e.matmul(ps[:, hh * 128:(hh + 1) * 128], lhsT=lhs_fn(j, bp), rhs=rhs_fn(j, bp), start=True, stop=True),
                                     reads=lhs_buf + rhs_bufs, writes=[ps])
                            P.op(eng, lambda e, ps=ps, half=half: e.tensor_tensor(out=dst[:, half * 4:half * 4 + 4, :], in0=ps[:, :].rearrange("p (h s) -> p h s", h=4),
                                                                                in1=mask_fn().unsqueeze(1).broadcast_to([128, 4, 128]), op=ALU.mult), reads=[ps, cst], writes=[dst])
                    bt_l = lambda j, bp: bt_bf[bp, j, tsl2]
                    kt_l = lambda j, bp: kt_bf[bp, j, tsl2]
                    a_r = lambda j, bp: AR[bp, j, tt, 0, :]
                    r_r = lambda j, bp: AR[bp, j, tt, 1, :]
                    gram([bt_bf], bt_l, a_r, [AR], msu, Qp[0], "dve")
                    yield
                    gram([AR], a_r, bt_l, [bt_bf], msl, Np[0], "dve")
                    gram([kt_bf], kt_l, a_r, [AR], msu, AKm, "dve")
                    yield
                    gram([bt_bf], bt_l, r_r, [AR], miu, ABm, "dve")
                    gram([kt_bf], kt_l, r_r, [AR], miu, RKm, "dve")
                    yield
                    for jp in range(4):
                        P.op("pool", lambda e, jp=jp: e.tensor_tensor(out=Sb[0][:, 2 * jp:2 * jp + 2, :], in0=Qp[0][:, 2 * jp:2 * jp + 2, :], in1=ident2[:, :, :], op=ALU.add),
                             reads=[Qp[0], ident2], writes=[Sb[0]])
                    qc, ncur, sc = 0, 0, 0
                    for lev in range(6):
                        qn, nn, sn = 1 - qc, 1 - ncur, 1 - sc
                        lastlev = (lev == 5)
                        if not lastlev:
                            for half in range(2):
                                ps = bank()
                                for hh in range(4):
                                    h8 = half * 4 + hh
                                    P.op("pe", lambda e, ps=ps, hh=hh, h8=h8, qc=qc, ncur=ncur: e.matmul(ps[:, hh * 128:(hh + 1) * 128], lhsT=Np[ncur][:, h8, :], rhs=Qp[qc][:, h8, :], start=True, stop=True),
                                         reads=[Np[ncur], Qp[qc]], writes=[ps])
                                P.op("act", lambda e, ps=ps, half=half, qn=qn: e.activation(out=Qp[qn][:, half * 4:half * 4 + 4, :], in_=ps[:, :].rearrange("p (h s) -> p h s", h=4), func=AF.Copy),
                                     reads=[ps], writes=[Qp[qn]])
                        for half in range(2):
                            ps = bank()
                            for hh in range(4):
                                h8 = half * 4 + hh
                                P.op("pe", lambda e, ps=ps, hh=hh, h8=h8, qc=qc, ncur=ncur: e.matmul(ps[:, hh * 128:(hh + 1) * 128], lhsT=Qp[qc][:, h8, :], rhs=Np[ncur][:, h8, :], start=True, stop=True),
                                     reads=[Np[ncur], Qp[qc]], writes=[ps])
                            P.op("act", lambda e, ps=ps, half=half, nn=nn: e.activation(out=Np[nn][:, half * 4:half * 4 + 4, :], in_=ps[:, :].rearrange("p (h s) -> p h s", h=4), func=AF.Copy),
                                 reads=[ps], writes=[Np[nn]])
                        for half in range(2):
                            ps = bank()
                            for hh in range(4):
                                h8 = half * 4 + hh
                                P.op("pe", lambda e, ps=ps, hh=hh, h8=h8, nn=nn, sc=sc: e.matmul(ps[:, hh * 128:(hh + 1) * 128], lhsT=Np[nn][:, h8, :], rhs=Sb[sc][:, h8, :], start=True, stop=True),
                                     reads=[Np[nn], Sb[sc]], writes=[ps])
                            P.op("dve", lambda e, ps=ps, half=half, sc=sc, sn=sn: e.tensor_tensor(out=Sb[sn][:, half * 4:half * 4 + 4, :], in0=ps[:, :].rearrange("p (h s) -> p h s", h=4),
                                                                                               in1=Sb[sc][:, half * 4:half * 4 + 4, :], op=ALU.add), reads=[ps, Sb[sc]], writes=[Sb[sn]])
                        qc, ncur, sc = qn, nn, sn
                        yield
                    Sf = Sb[sc]
                    vtok = lambda j, hp: tokm[:, tt, j, 2, hp * 64:hp * 64 + 64]
                    nat4 = lambda buf, hp: buf[:, :].rearrange("p (j h v) -> p j h v", j=4, h=2)[:, :, hp, :]
                    psx = [bank(), bank()]
                    for h8 in range(8):
                        j, hp = h8 // 2, h8 % 2
                        hidx = hp * 4 + j
                        bp = slice(hp * 64, hp * 64 + 64)
                        ps = psx[hp]
                        P.op("pe", lambda e, ps=ps, j=j, bp=bp: e.matmul(ps[:, j * 64:(j + 1) * 64], lhsT=AR[bp, j, tt, 0, :], rhs=Hb[j][bp, :], start=True, stop=False), reads=[AR, Hb[j]], writes=[ps])
                        P.op("pe", lambda e, ps=ps, j=j, hp=hp, hidx=hidx: e.matmul(ps[:, j * 64:(j + 1) * 64], lhsT=AKm[:, hidx, :], rhs=vtok(j, hp), start=False, stop=True), reads=[AKm, tokm], writes=[ps])
                    for hp in range(2):
                        P.op("act", lambda e, hp=hp: e.activation(out=nat4(Xs, hp), in_=psx[hp][:, 0:256].rearrange("p (j v) -> p j v", j=4), func=AF.Copy), reads=[psx[hp]], writes=[Xs])
                    yield
                    ps = bank()
                    for h8 in range(8):
                        j, hp = h8 // 2, h8 % 2
                        hidx = hp * 4 + j
                        P.op("pe", lambda e, ps=ps, h8=h8, hidx=hidx: e.matmul(ps[:, h8 * 64:(h8 + 1) * 64], lhsT=Sf[:, hidx, :], rhs=Xs[:, h8 * 64:(h8 + 1) * 64], start=True, stop=True), reads=[Sf, Xs], writes=[ps])
                    P.op("act", lambda e, ps=ps: e.activation(out=Us[:, :], in_=ps[:, :], func=AF.Copy), reads=[ps], writes=[Us])
                    yield
                    psy = [bank(), bank()]
                    for h8 in range(8):
                        j, hp = h8 // 2, h8 % 2
                        hidx = hp * 4 + j
                        bp = slice(hp * 64, hp * 64 + 64)
                        ps = psy[hp]
                        P.op("pe", lambda e, ps=ps, j=j, bp=bp: e.matmul(ps[:, j * 64:(j + 1) * 64], lhsT=AR[bp, j, tt, 1, :], rhs=Hb[j][bp, :], start=True, stop=False), reads=[AR, Hb[j]], writes=[ps])
                        P.op("pe", lambda e, ps=ps, j=j, h8=h8, hidx=hidx: e.matmul(ps[:, j * 64:(j + 1) * 64], lhsT=ABm[:, hidx, :], rhs=Us[:, h8 * 64:(h8 + 1) * 64], start=False, stop=False), reads=[ABm, Us], writes=[ps])
                        P.op("pe", lambda e, ps=ps, j=j, hp=hp, hidx=hidx: e.matmul(ps[:, j * 64:(j + 1) * 64], lhsT=RKm[:, hidx, :], rhs=vtok(j, hp), start=False, stop=True), reads=[RKm, tokm], writes=[ps])
                    for hp in range(2):
                        P.op("act", lambda e, hp=hp: e.activation(out=nat4(Ysb, hp), in_=psy[hp][:, 0:256].rearrange("p (j v) -> p j v", j=4), func=AF.Copy), reads=[psy[hp]], writes=[Ysb])
                    yield
                    ps = bank()
                    for j in range(4):
                        P.op("pe", lambda e, ps=ps, j=j: e.matmul(ps[:, j * 128:(j + 1) * 128], lhsT=tokm[:, tt, j, 0, :], rhs=Us[:, j * 128:(j + 1) * 128], start=True, stop=False), reads=[tokm, Us], writes=[ps])
                        P.op("pe", lambda e, ps=ps, j=j: e.matmul(ps[:, j * 128:(j + 1) * 128], lhsT=tokm[:, tt, j, 1, :], rhs=tokm[:, tt, j, 2, :], start=False, stop=True), reads=[tokm], writes=[ps])
                    for j in range(4):
                        for hp in range(2):
                            bp = slice(hp * 64, hp * 64 + 64)
                            P.op("dve", lambda e, ps=ps, j=j, hp=hp, bp=bp: e.scalar_tensor_tensor(out=Hs32[j][bp, :], in0=ps[bp, j * 128 + hp * 64:j * 128 + hp * 64 + 64], scalar=eet[bp, j, tt:tt + 1],
                                                                                               in1=Hpe[j][bp, :], op0=ALU.mult, op1=ALU.add), reads=[ps, eet, Hpe[j]], writes=[Hs32[j]])
                    yield
                    y3 = lambda buf: buf[:, :].rearrange("p (h v) -> p h v", h=8)
                    bc8 = lambda ap: ap.unsqueeze(2).broadcast_to([128, 8, 64])
                    P.op("dve", lambda e: e.tensor_reduce(out=yst[:, 0:8], in_=y3(Ysb), axis=AX.X, op=ALU.add), reads=[Ysb], writes=[yst])
                    P.op("act", lambda e: e.activation(out=ysq[:, :], in_=Ysb[:, :], func=AF.Square), reads=[Ysb], writes=[ysq])
                    P.op("dve", lambda e: e.tensor_reduce(out=yst[:, 8:16], in_=y3(ysq), axis=AX.X, op=ALU.add), reads=[ysq], writes=[yst])
                    P.op("dve", lambda e: e.tensor_scalar(out=yst[:, 16:24], in0=yst[:, 0:8], scalar1=1.0 / 64, scalar2=None, op0=ALU.mult), reads=[yst], writes=[yst])
                    P.op("dve", lambda e: e.tensor_tensor(out=yst[:, 24:32], in0=yst[:, 16:24], in1=yst[:, 16:24], op=ALU.mult), reads=[yst], writes=[yst])
                    P.op("dve", lambda e: e.scalar_tensor_tensor(out=yst[:, 32:40], in0=yst[:, 8:16], scalar=1.0 / 64, in1=yst[:, 24:32], op0=ALU.mult, op1=ALU.subtract), reads=[yst], writes=[yst])
                    P.op("act", lambda e: e.activation(out=yst[:, 24:32], in_=yst[:, 32:40], func=AF.Ln, bias=epsg[:, 0:1]), reads=[yst, epsg], writes=[yst])
                    P.op("act", lambda e: e.activation(out=yst[:, 32:40], in_=yst[:, 24:32], func=AF.Exp, scale=-0.5), reads=[yst], writes=[yst])
                    P.op("dve", lambda e: e.tensor_tensor(out=y3(yc), in0=y3(Ysb), in1=bc8(yst[:, 16:24]), op=ALU.subtract), reads=[Ysb, yst], writes=[yc])
                    P.op("dve", lambda e: e.tensor_tensor(out=y3(yc), in0=y3(yc), in1=bc8(yst[:, 32:40]), op=ALU.mult), reads=[yc, yst], writes=[yc])
                    P.op("pool", lambda e: e.tensor_tensor(out=yc[:, :], in0=yc[:, :], in1=pp[:, PP_LNW:PP_LNW + 512], op=ALU.mult), reads=[yc, pp], writes=[yc])
                    P.op("pool", lambda e: e.tensor_tensor(out=ynb[:, :], in0=yc[:, :], in1=pp[:, PP_LNB:PP_LNB + 512], op=ALU.add), reads=[yc, pp], writes=[ynb])
                    for j in range(4):
                        P.op("pe", lambda e, j=j: e.transpose(pbT[:, j * 128:(j + 1) * 128], ynb[:, j * 128:(j + 1) * 128], ident_bf()), reads=[ynb, cstb], writes=[pbT])
                    P.op("dve", lambda e: e.tensor_tensor(out=yc[:, :].rearrange("p (j s) -> p j s", j=4), in0=pbT[:, 0:512].rearrange("p (j s) -> p j s", j=4), in1=bonT[:, :, tsl2], op=ALU.add),
                         reads=[pbT, bonT], writes=[yc])
                    P.op("dve", lambda e: e.tensor_tensor(out=brR[:, :, tsl2], in0=yc[:, :].rearrange("p (j s) -> p j s", j=4), in1=gT[:, :, tsl2], op=ALU.mult), reads=[yc, gT], writes=[brR])

                yield
            P.mute = False
            cur_banks[0] = ATTB
            ga, gb = attn_gen(), rw_gen()
            na = 8 * (b + 1) * 1.0
            nbk = 4.0 * nt + nt * 16.0
            da = db = 0
            alive_a = alive_b = True
            while alive_a or alive_b:
                pick_a = alive_a and (not alive_b or da / na <= db / nbk)
                if pick_a:
                    try:
                        next(ga)
                        da += 1
                    except StopIteration:
                        alive_a = False
                else:
                    try:
                        next(gb)
                        db += 1
                    except StopIteration:
                        alive_b = False
            cur_banks[0] = ALLB
            P.mute = ("merge" in SKIP)
            brs = [brP, brA, brR]
            for nbi in range(3):
                P.dma("sp", "wbr", lambda e, nbi=nbi: e.dma_start(out=wbr[:, :, :], in_=wbr_bf[l][nbi * 512:(nbi + 1) * 512, :].rearrange("(c p) d -> p c d", p=128)),
                      reads=[DR("cv%d" % l)], writes=[wbr])
                for dm in range(8):
                    ps = bank()
                    for c in range(4):
                        P.op("pe", lambda e, ps=ps, c=c, dm=dm, nbi=nbi: e.matmul(ps[:, :nb_], lhsT=wbr[:, c, dm * 128:(dm + 1) * 128], rhs=brs[nbi][:, c, :nb_], start=(c == 0), stop=(c == 3)),
                             reads=[wbr, brs[nbi]], writes=[ps])
                    if nbi == 0:
                        P.op("dve", lambda e, ps=ps, dm=dm, nbi=nbi: e.tensor_tensor(out=merged[:, dm, :nb_], in0=ps[:, :nb_], in1=gates[:, nbi * 8 + dm, :nb_], op=ALU.mult),
                             reads=[ps, gates], writes=[merged])
                    else:
                        mt = mtmp[dm % 2]
                        P.op("dve", lambda e, ps=ps, dm=dm, nbi=nbi, mt=mt: e.tensor_tensor(out=mt[:, :nb_], in0=ps[:, :nb_], in1=gates[:, nbi * 8 + dm, :nb_], op=ALU.mult),
                             reads=[ps, gates], writes=[mt])
                        P.op("pool", lambda e, dm=dm, mt=mt: e.tensor_tensor(out=merged[:, dm, :nb_], in0=merged[:, dm, :nb_], in1=mt[:, :nb_], op=ALU.add),
                             reads=[merged, mt], writes=[merged])
            for gg in range(2):
                w = load_wg(wout_bf[l], gg * 512, 512, "cv%d" % l)
                for m in range(4):
                    dm = gg * 4 + m
                    ps = bank()
                    for c in range(8):
                        P.op("pe", lambda e, ps=ps, c=c, m=m, w=w: e.matmul(ps[:, :nb_], lhsT=w[:, c, m * 128:(m + 1) * 128], rhs=merged[:, c, :nb_], start=(c == 0), stop=(c == 7)),
                             reads=[w, merged], writes=[ps])
                    P.op("dve", lambda e, ps=ps, dm=dm: e.tensor_tensor(out=xt[:, dm, :nb_], in0=xt[:, dm, :nb_], in1=ps[:, :nb_], op=ALU.add), reads=[xt, ps], writes=[xt])
            if b == 0:
                P.op("pool", lambda e: e.memset(xt[:, :, 0:NPAD], 0.0), writes=[xt])

            P.mute = ("ffn" in SKIP)
            norm_stage(PP_GFFN, nb_)
            for gg in range(11):
                w = load_wg(up_bf[l], gg * 512, 512, "cv%d" % l)
                for m in range(4):
                    mi = gg * 4 + m
                    ps = bank()
                    proj_fm(w, m, nb_, ps)
                    cw = lambda i, mi=mi: pp[:, PP_CONV + i * 44 + mi:PP_CONV + i * 44 + mi + 1]
                    cv = scr[mi % 4]
                    P.op("act", lambda e, ps=ps, cv=cv, cw=cw: e.activation(out=cv[:, :nb_], in_=ps[:, :nb_], func=AF.Identity, scale=cw(2)), reads=[ps, pp], writes=[cv])
                    P.op("dve", lambda e, ps=ps, cv=cv, cw=cw: e.scalar_tensor_tensor(out=cv[:, 1:nb_], in0=ps[:, 0:nb_ - 1], scalar=cw(1), in1=cv[:, 1:nb_], op0=ALU.mult, op1=ALU.add),
                         reads=[ps, pp, cv], writes=[cv])
                    P.op("dve", lambda e, ps=ps, cv=cv, cw=cw: e.scalar_tensor_tensor(out=cv[:, 2:nb_], in0=ps[:, 0:nb_ - 2], scalar=cw(0), in1=cv[:, 2:nb_], op0=ALU.mult, op1=ALU.add),
                         reads=[ps, pp, cv], writes=[cv])
                    P.op("dve", lambda e, cv=cv, cw=cw, mi=mi: e.scalar_tensor_tensor(out=cv[:, 0:1], in0=chalo[:, mi, 1:2], scalar=cw(1), in1=cv[:, 0:1], op0=ALU.mult, op1=ALU.add),
                         reads=[chalo, pp, cv], writes=[cv])
                    P.op("dve", lambda e, cv=cv, cw=cw, mi=mi: e.scalar_tensor_tensor(out=cv[:, 0:2], in0=chalo[:, mi, 0:2], scalar=cw(0), in1=cv[:, 0:2], op0=ALU.mult, op1=ALU.add),
                         reads=[chalo, pp, cv], writes=[cv])
                    P.op("dve", lambda e, ps=ps, mi=mi: e.tensor_copy(out=chalo[:, mi, :], in_=ps[:, nb_ - 2:nb_]), reads=[ps], writes=[chalo])
                    if mi < 22:
                        P.op("act", lambda e, cv=cv, mi=mi: e.activation(out=cgs[:, mi, :nb_], in_=cv[:, :nb_], func=AF.Silu), reads=[cv], writes=[cgs])
                    else:
                        P.op("pool", lambda e, cv=cv, mi=mi: e.tensor_tensor(out=fact[:, mi - 22, :nb_], in0=cgs[:, mi - 22, :nb_], in1=cv[:, :nb_], op=ALU.mult), reads=[cgs, cv], writes=[fact])
            for dm in range(8):
                dw = dnw[dm % 2]
                P.dma("sp", dw.res.name, lambda e, dw=dw, dm=dm: e.dma_start(out=dw[:, :, :], in_=dn_bf[l][:, dm * 128:(dm + 1) * 128].rearrange("(c p) d -> p c d", p=128)),
                      reads=[DR("cv%d" % l)], writes=[dw])
                ps = bank()
                for c in range(22):
                    P.op("pe", lambda e, ps=ps, c=c, dw=dw: e.matmul(ps[:, :nb_], lhsT=dw[:, c, :], rhs=fact[:, c, :nb_], start=(c == 0), stop=(c == 21)),
                         reads=[dw, fact], writes=[ps])
                P.op("dve", lambda e, ps=ps, dm=dm: e.tensor_tensor(out=xt[:, dm, :nb_], in0=xt[:, dm, :nb_], in1=ps[:, :nb_], op=ALU.add), reads=[xt, ps], writes=[xt])
            if b == 0:
                P.op("pool", lambda e: e.memset(xt[:, :, 0:NPAD], 0.0), writes=[xt])
            P.mute = False
            if not last:
                P.dma("pool", "xst", lambda e: e.dma_start(out=xdst[:, tsl].rearrange("(c p) t -> p c t", p=128), in_=xt[:, :, :nb_]),
                      reads=[xt], writes=[DR("xs%d_%d" % (l + 1, b))])
            else:
                P.dma("pool", "xst", lambda e: e.dma_start(out=xTo[:, tsl].rearrange("(c p) t -> p c t", p=128), in_=xt[:, :, :nb_]),
                      reads=[xt], writes=[DR("xo")])
                norm_stage(PP_GFIN, nb_)
                for c in range(8):
                    P.op("dve", lambda e, c=c: e.scalar_tensor_tensor(out=xt[:, c, :nb_], in0=xt[:, c, :nb_], scalar=pp[:, PP_GFIN + c:PP_GFIN + c + 1], in1=rstd[:, :nb_],
                                                                      op0=ALU.mult, op1=ALU.mult), reads=[xt, pp, rstd], writes=[xt])
                P.dma("pool", "xst", lambda e: e.dma_start(out=outT[:, tsl].rearrange("(c p) t -> p c t", p=128), in_=xt[:, :, :nb_]),
                      reads=[xt], writes=[DR("out")])

    for l in range(NL):
        src = xT_in if l == 0 else xbufs[l % 2]
        layer(l, src, xbufs[(l + 1) % 2], last=(l == NL - 1))
    if os.environ.get("KDBG"):
        print("op counts", {e: len(P.q[e]) for e in P.ENG}, "sems", len(P.sems))
    P.emit(final_res=[DR("out"), DR("xo")])
    return nc


def _host_consts(Lp):
    cst = np.zeros((128, NCS), np.float32)
    idx = np.arange(128)
    cst[:, CS_ID:CS_ID + 128] = np.eye(128, dtype=np.float32)
    cst[:, CS_MSU:CS_MSU + 128] = (idx[:, None] < idx[None, :])
    cst[:, CS_MIU:CS_MIU + 128] = (idx[:, None] <= idx[None, :])
    cst[:, CS_MSL:CS_MSL + 128] = (idx[:, None] > idx[None, :])
    cst[:, CS_BO:CS_BO + 128] = ((idx[:, None] // 64) == (idx[None, :] // 64))
    perm = np.zeros((128, 128), np.float32)
    for m in range(128):
        d = m % 64
        if d < 8:
            perm[m + 8, m] = 1.0
        elif d < 16:
            perm[m - 8, m] = 1.0
    cst[:, CS_PERM:CS_PERM + 128] = perm
    cst[:, CS_DM:CS_DM + 128] = ((idx[:, None] // 64) <= (idx[None, :] // 64))
    for g, w in enumerate((2, 4, 8, 16)):
        p = idx - NPAD
        cnt = np.where(p >= 0, np.minimum(p + 1, w), w).astype(np.float32)
        cst[:, CS_IC + g * 128:CS_IC + (g + 1) * 128] = (1.0 / cnt)[None, :]
    pos = (np.arange(Lp) - NPAD).astype(np.float32)
    inv = (np.float32(500000.0) ** (-np.arange(0, 16, 2, dtype=np.float32) / np.float32(16))).astype(np.float32)
    ang = (pos[:, None] * inv[None, :]).astype(np.float32)
    cos = np.cos(ang).astype(np.float32).T
    sin = np.sin(ang).astype(np.float32).T
    rc = np.ones((128, Lp), np.float32)
    rs = np.zeros((128, Lp), np.float32)
    for p in range(128):
        d = p % 64
        if d < 8:
            rc[p] = cos[d]
            rs[p] = -sin[d]
        elif d < 16:
            rc[p] = cos[d - 8]
            rs[p] = sin[d - 8]
    return cst, rc, rs


def _pack_pp(l, norm_mix, norm_ffn, norm_final, pool_scale, rw_mu, rw_w0, rw_a0, rw_k_k, rw_k_a, rw_r_k,
             ffn_conv, da_subln, rw_lnx_w, rw_lnx_b, da_lambda):
    pp = np.zeros((128, NPP), np.float32)
    col = lambda v: np.asarray(v, np.float32).reshape(-1, 128).T
    pp[:, PP_GMIX:PP_GMIX + 8] = col(norm_mix[l])
    pp[:, PP_GFFN:PP_GFFN + 8] = col(norm_ffn[l])
    pp[:, PP_PSC:PP_PSC + 4] = col(pool_scale[l])
    pp[:, PP_MU:PP_MU + 14] = col(rw_mu[l])
    pp[:, PP_W0:PP_W0 + 4] = col(rw_w0[l])
    pp[:, PP_A0:PP_A0 + 4] = col(rw_a0[l])
    pp[:, PP_KK:PP_KK + 4] = col(rw_k_k[l])
    pp[:, PP_KA:PP_KA + 4] = col(rw_k_a[l])
    pp[:, PP_RK:PP_RK + 4] = col(rw_r_k[l].reshape(-1))
    for i in range(3):
        pp[:, PP_CONV + i * 44:PP_CONV + (i + 1) * 44] = col(ffn_conv[l, i])
    pp[:, PP_SUBLN:PP_SUBLN + 128] = np.asarray(da_subln[l], np.float32)[None, :]
    pp[:, PP_LNW:PP_LNW + 512] = np.asarray(rw_lnx_w[l], np.float32)[None, :]
    pp[:, PP_LNB:PP_LNB + 512] = np.asarray(rw_lnx_b[l], np.float32)[None, :]
    pp[:, PP_LAM:PP_LAM + 256] = np.asarray(da_lambda[l], np.float32).reshape(1, 256)
    pp[:, PP_GFIN:PP_GFIN + 8] = col(norm_final)
    lam_init = 0.8 - 0.6 * math.exp(-0.3 * l)
    pp[:, PP_OML] = 1.0 - lam_init
    pp[:, PP_NLI] = -lam_init
    return pp


_NC_CACHE = {}


def _run(xT_list, NT, layers, lam_ids, weights, cst, rc, rs):
    key = (NT, len(layers))
    if key not in _NC_CACHE:
        _NC_CACHE[key] = build(NT, len(layers))
    nc = _NC_CACHE[key]
    W = weights
    ls = list(layers)
    f = lambda a: np.ascontiguousarray(np.asarray(a, np.float32))
    shared = {
        "w_in": f(W["w_in"][ls]),
        "w_branch": f(W["w_branch"][ls].reshape(len(ls), 1536, D)),
        "w_out": f(W["w_out"][ls]),
        "ffn_up": f(W["ffn_up"][ls]),
        "ffn_down": f(W["ffn_down"][ls]),
        "pool_w": f(W["pool_w"][ls]),
        "rw_w2": f(W["rw_w2"][ls]),
        "rw_a2": f(W["rw_a2"][ls]),
        "rw_g2": f(W["rw_g2"][ls]),
        "pp": f(np.stack([_pack_pp(l, W["norm_mix"], W["norm_ffn"], W["norm_final"], W["pool_scale"], W["rw_mu"], W["rw_w0"], W["rw_a0"],
                                   W["rw_k_k"], W["rw_k_a"], W["rw_r_k"], W["ffn_conv"], W["da_subln"], W["rw_lnx_w"], W["rw_lnx_b"],
                                   W["da_lambda"]) for l in ls])),
        "cst": cst, "ropeC": rc, "ropeS": rs,
    }
    in_maps = []
    for xT in xT_list:
        m = dict(shared)
        m["xT"] = xT
        in_maps.append(m)
    res = run_bass_kernel_spmd(nc, in_maps, core_ids=list(range(len(in_maps))))
    return [(r["outT"], r["xTo"]) for r in res.results]


def kernel(x, meta_tokens, **W):
    x = np.asarray(x, np.float32)
    B, Lq, _ = x.shape
    L = NPAD + NMETA + Lq
    NT = (L + 127) // 128
    Lp = NT * 128
    cst, rc, rs = _host_consts(Lp)
    W = {k: np.asarray(v) for k, v in W.items()}
    NL = W["w_in"].shape[0]
    xTs = []
    for c in range(8):
        b = c % B
        xp = np.zeros((Lp, D), np.float32)
        xp[NPAD:NPAD + NMETA] = np.asarray(meta_tokens, np.float32)
        xp[NPAD + NMETA:NPAD + NMETA + Lq] = x[b]
        xTs.append(np.ascontiguousarray(xp.T))
    if FUSED:
        outs = _run(xTs, NT, list(range(NL)), list(range(NL)), W, cst, rc, rs)
    else:
        for l in range(NL):
            outs = _run(xTs, NT, [l], [l], W, cst, rc, rs)
            xTs = [np.ascontiguousarray(o[1]) for o in outs]
    out = np.stack([np.ascontiguousarray(outs[b][0].T[NPAD + NMETA:NPAD + NMETA + Lq]) for b in range(B)])
    return out.astype(np.float32)
```
